# Optimizing a Trainium2 kernel written in Bass

```python
import math
import jax, jax.numpy as jnp
from jax import lax
import numpy as np

D_MODEL = 1024
BATCH = 16
SEQ = 2048
DEPTH = 4

HEAD_DIM = 128
ROT_DIM = HEAD_DIM // 4
ROPE_THETA = 500000.0
DIL_PAIRS = ((128, 1), (512, 4), (2048, 16))
N_GROUPS = len(DIL_PAIRS)
A_HEADS = D_MODEL // HEAD_DIM
A_WIDTH = A_HEADS * HEAD_DIM
A_QKV = N_GROUPS * A_WIDTH
B_HEADS = D_MODEL // HEAD_DIM
B_KEY = 128
B_VAL = 128
B_WIDTH = B_HEADS * B_VAL
CHUNK = 64
MEM_LEN = 256
MEM_HEADS = 4
MEM_WIDTH = MEM_HEADS * HEAD_DIM
MIX_WIDTH = A_WIDTH + MEM_WIDTH
IN_A = 3 * A_QKV + MEM_WIDTH + MIX_WIDTH
IN_B = 3 * B_HEADS * B_KEY + MEM_WIDTH + MIX_WIDTH
N_A_LAYERS = (DEPTH + 1) // 2
N_B_LAYERS = DEPTH // 2
EPS = 1e-6
ATTN_SCALE = 1.0 / math.sqrt(HEAD_DIM)

kernel_name = "hybrid_dilated_attn_hgrn2_memxattn"


def rms_norm(x, g):
    xf = x.astype(jnp.float32)
    y = xf * lax.rsqrt(jnp.mean(xf * xf, axis=-1, keepdims=True) + EPS)
    return (y * g.astype(jnp.float32)).astype(x.dtype)


def rotary_tables(positions):
    inv_freq = ROPE_THETA ** (-jnp.arange(0, ROT_DIM, 2, dtype=jnp.float32) / ROT_DIM)
    ang = positions.astype(jnp.float32)[..., None] * inv_freq
    return jnp.cos(ang), jnp.sin(ang)


def apply_partial_rotary(x, cos, sin):
    xf = x.astype(jnp.float32)
    half = ROT_DIM // 2
    x1, x2, rest = xf[..., :half], xf[..., half:ROT_DIM], xf[..., ROT_DIM:]
    out = jnp.concatenate([x1 * cos - x2 * sin, x2 * cos + x1 * sin, rest], axis=-1)
    return out.astype(x.dtype)


def dilated_window_attention(q, k, v, dilation, n_back):
    B, S, H, Dh = q.shape
    span = dilation * n_back
    Sp = -(-S // span) * span
    nb = Sp // span

    def to_blocks(t):
        t = jnp.pad(t, ((0, 0), (0, Sp - S), (0, 0), (0, 0)))
        return t.reshape(B, nb, n_back, dilation, H, Dh).transpose(0, 3, 1, 2, 4, 5)

    qb, kb, vb = to_blocks(q), to_blocks(k), to_blocks(v)
    shift = lambda t: jnp.pad(t, ((0, 0), (0, 0), (1, 0), (0, 0), (0, 0), (0, 0)))[:, :, :-1]
    kc = jnp.concatenate([shift(kb), kb], axis=3)
    vc = jnp.concatenate([shift(vb), vb], axis=3)
    s = jnp.einsum('brcqhd,brckhd->brchqk', qb, kc).astype(jnp.float32) * ATTN_SCALE
    i = jnp.arange(n_back)[:, None]
    j = jnp.arange(2 * n_back)[None, :]
    dist = n_back + i - j
    band = (dist >= 0) & (dist <= n_back)
    c = jnp.arange(nb)[:, None, None]
    valid = band[None] & ((c * n_back + j[None] - n_back) >= 0)
    valid = valid[None, None, :, None]
    s = jnp.where(valid, s, -jnp.inf)
    m = jnp.max(s, axis=-1, keepdims=True)
    p = jnp.exp(s - m)
    den = jnp.sum(p, axis=-1, keepdims=True)
    o = jnp.einsum('brchqk,brckhd->brcqhd', p.astype(v.dtype), vc).astype(jnp.float32)
    o = o / jnp.moveaxis(den, 3, 4)
    lse = (m + jnp.log(den))[..., 0]
    o = o.transpose(0, 2, 3, 1, 4, 5).reshape(B, Sp, H, Dh)[:, :S]
    lse = lse.transpose(0, 2, 4, 1, 3).reshape(B, Sp, H)[:, :S]
    return o, lse


def dilated_mixer(cols, cos, sin, q_gain, k_gain):
    B, S, _ = cols.shape
    qkv = cols.reshape(B, S, 3, N_GROUPS, A_HEADS, HEAD_DIM)
    q = rms_norm(qkv[:, :, 0], q_gain[:, None, :])
    k = rms_norm(qkv[:, :, 1], k_gain[:, None, :])
    v = qkv[:, :, 2]
    cos5, sin5 = cos[:, :, None, None, :], sin[:, :, None, None, :]
    q = apply_partial_rotary(q, cos5, sin5)
    k = apply_partial_rotary(k, cos5, sin5)
    outs, lses = [], []
    for g, (window, dil) in enumerate(DIL_PAIRS):
        o, lse = dilated_window_attention(q[:, :, g], k[:, :, g], v[:, :, g], dil, window // dil)
        outs.append(o)
        lses.append(lse)
    alpha = jax.nn.softmax(jnp.stack(lses, axis=0), axis=0)
    o = jnp.sum(alpha[..., None] * jnp.stack(outs, axis=0), axis=0)
    return o.reshape(B, S, A_WIDTH).astype(cols.dtype)


def gla_chunk_scan(q, k, v, log_f):
    B, S, H, dk = q.shape
    dv = v.shape[-1]
    nc = S // CHUNK
    to_c = lambda t: t.reshape(B, nc, CHUNK, H, t.shape[-1]).transpose(1, 0, 3, 2, 4)
    qc, kc, vc = to_c(q), to_c(k), to_c(v)
    G = jnp.cumsum(to_c(log_f), axis=3)
    causal = jnp.tril(jnp.ones((CHUNK, CHUNK), dtype=bool))[..., None]

    def step(state, inp):
        qt, kt, vt, Gt = inp
        o_inter = jnp.einsum('bhtk,bhkv->bhtv', qt * jnp.exp(Gt), state)
        diff = Gt[:, :, :, None, :] - Gt[:, :, None, :, :]
        decay = jnp.where(causal, jnp.exp(jnp.where(causal, diff, 0.0)), 0.0)
        attn = jnp.einsum('bhtk,bhsk,bhtsk->bhts', qt, kt, decay)
        o_intra = jnp.einsum('bhts,bhsv->bhtv', attn, vt)
        g_last = Gt[:, :, -1:, :]
        k_dec = kt * jnp.exp(g_last - Gt)
        new_state = jnp.exp(g_last[:, :, 0, :])[..., None] * state + jnp.einsum('bhsk,bhsv->bhkv', k_dec, vt)
        return new_state, o_inter + o_intra

    init = jnp.zeros((B, H, dk, dv), jnp.float32)
    _, o = lax.scan(step, init, (qc, kc, vc, G))
    return o.transpose(1, 0, 3, 2, 4).reshape(B, S, H, dv)


def hgrn2_mixer(cols, lb, o_gain):
    B, S, _ = cols.shape
    w = B_HEADS * B_KEY
    q = cols[..., :w].astype(jnp.float32)
    f = cols[..., w:2 * w].astype(jnp.float32)
    iv = cols[..., 2 * w:].astype(jnp.float32)
    log_f = jnp.logaddexp(jnp.log(lb), jnp.log1p(-lb) + jax.nn.log_sigmoid(f))
    k = (1.0 - lb) * jax.nn.sigmoid(-f)
    sh = lambda t, d: t.reshape(B, S, B_HEADS, d)
    o = gla_chunk_scan(sh(q, B_KEY), sh(k, B_KEY), sh(iv, B_VAL), sh(log_f, B_KEY))
    o = rms_norm(o, o_gain)
    return o.reshape(B, S, B_WIDTH).astype(cols.dtype)


def memory_cross_attention(q_cols, mem, mem_gain, w_kv, q_gain, k_gain):
    B, S, _ = q_cols.shape
    M = mem.shape[1]
    kv = rms_norm(mem, mem_gain) @ w_kv
    km = rms_norm(kv[..., :MEM_WIDTH].reshape(B, M, MEM_HEADS, HEAD_DIM), k_gain)
    vm = kv[..., MEM_WIDTH:].reshape(B, M, MEM_HEADS, HEAD_DIM)
    qm = rms_norm(q_cols.reshape(B, S, MEM_HEADS, HEAD_DIM), q_gain)
    s = jnp.einsum('bshd,bmhd->bhsm', qm, km).astype(jnp.float32) * ATTN_SCALE
    p = jax.nn.softmax(s, axis=-1)
    o = jnp.einsum('bhsm,bmhd->bshd', p.astype(vm.dtype), vm)
    return o.reshape(B, S, MEM_WIDTH)


def setup_inputs(seed: int = 0) -> dict:
    key = jax.random.key(seed)
    ks = jax.random.split(key, 16)
    nrm = lambda k, shape, scale: jax.random.normal(k, shape, jnp.float32) * scale
    gain = lambda k, shape: 1.0 + 0.05 * jax.random.normal(k, shape, jnp.float32)
    x = nrm(ks[0], (BATCH, SEQ, D_MODEL), 1.0)
    mem = nrm(ks[1], (BATCH, MEM_LEN, D_MODEL), 1.0)
    offsets = jax.random.randint(ks[2], (BATCH, 1), 0, 4096, dtype=jnp.int32)
    positions = offsets + jnp.arange(SEQ, dtype=jnp.int32)[None, :]
    return {
        "x": x,
        "mem": mem,
        "positions": positions,
        "norm_gain": gain(ks[3], (DEPTH, D_MODEL)),
        "w_in_a": nrm(ks[4], (N_A_LAYERS, D_MODEL, IN_A), D_MODEL ** -0.5),
        "q_gain_a": gain(ks[5], (N_A_LAYERS, N_GROUPS, HEAD_DIM)),
        "k_gain_a": gain(ks[6], (N_A_LAYERS, N_GROUPS, HEAD_DIM)),
        "w_out_a": nrm(ks[7], (N_A_LAYERS, MIX_WIDTH, D_MODEL), MIX_WIDTH ** -0.5),
        "w_in_b": nrm(ks[8], (N_B_LAYERS, D_MODEL, IN_B), D_MODEL ** -0.5),
        "lb_logits": nrm(ks[9], (DEPTH, B_HEADS * B_KEY), 0.5),
        "o_gain_b": gain(ks[10], (N_B_LAYERS, B_VAL)),
        "w_out_b": nrm(ks[11], (N_B_LAYERS, MIX_WIDTH, D_MODEL), MIX_WIDTH ** -0.5),
        "mem_norm_gain": gain(ks[12], (DEPTH, D_MODEL)),
        "w_mem_kv": nrm(ks[13], (DEPTH, D_MODEL, 2 * MEM_WIDTH), D_MODEL ** -0.5),
        "mem_q_gain": gain(ks[14], (DEPTH, HEAD_DIM)),
        "mem_k_gain": gain(ks[15], (DEPTH, HEAD_DIM)),
    }


def reference(x, mem, positions, norm_gain, w_in_a, q_gain_a, k_gain_a, w_out_a,
              w_in_b, lb_logits, o_gain_b, w_out_b, mem_norm_gain, w_mem_kv,
              mem_q_gain, mem_k_gain):
    cos, sin = rotary_tables(positions)
    sm = jax.nn.softmax(lb_logits.astype(jnp.float32), axis=0)
    lower_bounds = jnp.cumsum(sm, axis=0) - sm[0:1]
    for l in range(DEPTH):
        j = l // 2
        h = rms_norm(x, norm_gain[l])
        if l % 2 == 0:
            cols = h @ w_in_a[j]
            mix = dilated_mixer(cols[..., :3 * A_QKV], cos, sin, q_gain_a[j], k_gain_a[j])
            q_mem = cols[..., 3 * A_QKV:3 * A_QKV + MEM_WIDTH]
            gate = cols[..., 3 * A_QKV + MEM_WIDTH:]
            w_out = w_out_a[j]
        else:
            cols = h @ w_in_b[j]
            wb = 3 * B_HEADS * B_KEY
            mix = hgrn2_mixer(cols[..., :wb], lower_bounds[l], o_gain_b[j])
            q_mem = cols[..., wb:wb + MEM_WIDTH]
            gate = cols[..., wb + MEM_WIDTH:]
            w_out = w_out_b[j]
        mem_out = memory_cross_attention(q_mem, mem, mem_norm_gain[l], w_mem_kv[l],
                                         mem_q_gain[l], mem_k_gain[l])
        y = jnp.concatenate([mix, mem_out], axis=-1) * jax.nn.silu(gate)
        x = x + y @ w_out
    return x
```

```python
import math
from contextlib import ExitStack
import numpy as np
import concourse.bass as bass
import concourse.mybir as mybir
from concourse.bass_utils import run_bass_kernel_spmd

F32 = mybir.dt.float32
BF16 = mybir.dt.bfloat16
I32 = mybir.dt.int32
AF = mybir.ActivationFunctionType
ALU = mybir.AluOpType

SEM_LIMIT = 30000
SAME_ENGINE_SYNC = True

D = 1024
S = 2048
NT = 16
MEM = 256
DEPTH = 4
EPS = 1e-6
ATT = 1.0 / math.sqrt(128.0)
IN_A = 11264
IN_B = 5120
NSPL = 1056
N_CORES = 8
SEQ_PER_CORE = 2
RING_COLS = 3072
PI_SAFE = 3.1415925


class Sem:
    def __init__(self, K, name, step):
        self.K, self.name, self.step = K, name, step
        self.handles = []
        self.count = 0
        self._new()

    def _new(self):
        self.handles.append(self.K.nc.alloc_semaphore(name="%s_e%d" % (self.name, len(self.handles))))
        self.count = 0

    def next(self):
        if self.count + self.step > SEM_LIMIT:
            self._new()
        self.count += self.step
        return (self, len(self.handles) - 1, self.count)


class T:
    def __init__(self, ap, name="", excl=False):
        self.ap = ap
        self.name = name
        self.excl = excl
        self.w = {}
        self.r = {}

    def __getitem__(self, k):
        return self.ap[k]


class Eng:
    def __init__(self, K, name, attr):
        self.K, self.name, self.attr = K, name, attr
        self.sem = Sem(K, "s_" + name, 1)
        self.waited = {}
        self.prog = []


class Kern:
    def __init__(self):
        self.nc = bass.Bass("TRN2", target_bir_lowering=False)
        self.E = {
            "pe": Eng(self, "pe", "tensor"),
            "act": Eng(self, "act", "scalar"),
            "dve": Eng(self, "dve", "vector"),
            "pool": Eng(self, "pool", "gpsimd"),
            "sp": Eng(self, "sp", "sync"),
        }
        self.n_inst = 0
        self.dma_toks = []

    def _wait(self, E, tok):
        sem, ep, val = tok
        key = id(sem)
        cur = E.waited.get(key, (-1, 0))
        if cur[0] > ep or (cur[0] == ep and cur[1] >= val):
            return
        E.waited[key] = (ep, val)
        h = sem.handles[ep]
        E.prog.append(lambda e, h=h, val=val: e.wait_ge(h, val))
        self.n_inst += 1

    def op(self, eng, fn, reads=(), writes=(), sem=None):
        E = self.E[eng]
        deps = []
        if any(t.excl for t in reads):
            writes = list(writes) + [t for t in reads if t.excl and t not in writes]
            reads = [t for t in reads if not t.excl]
        for t in reads:
            deps.extend((tok, True) for tok in t.w.values())
        for t in writes:
            deps.extend((tok, False) for tok in t.w.values())
            deps.extend((tok, False) for tok in t.r.values())
        for tok, raw in deps:
            if tok[0] is E.sem and (eng == "pe" or not SAME_ENGINE_SYNC):
                continue
            self._wait(E, tok)
        Sm = sem if sem is not None else E.sem
        tok = Sm.next()
        h = Sm.handles[tok[1]]
        step = Sm.step
        E.prog.append(lambda e, fn=fn, h=h, step=step: fn(e).then_inc(h, step))
        self.n_inst += 1
        for t in reads:
            t.r[id(Sm)] = tok
        for t in writes:
            t.w = {id(Sm): tok}
            t.r = {}
        if sem is not None:
            self.dma_toks.append(tok)
        return tok

    def wait_tok(self, eng, tok):
        self._wait(self.E[eng], tok)

    def barrier(self):
        toks = []
        for E in self.E.values():
            if E.sem.count > 0 or len(E.sem.handles) > 1:
                toks.append((E.sem, len(E.sem.handles) - 1, E.sem.count))
        last = {}
        for tok in self.dma_toks:
            last[id(tok[0])] = tok
        self.dma_toks = list(last.values())
        toks.extend(self.dma_toks)
        for E in self.E.values():
            for tok in toks:
                if tok[0] is E.sem:
                    continue
                if tok[2] == 0:
                    continue
                self._wait(E, tok)

    def finish(self):
        nc = self.nc
        with nc.Block() as block:
            for name, E in self.E.items():
                if not E.prog:
                    continue
                dec = getattr(block, E.attr)

                def body(e, prog=E.prog):
                    for f in prog:
                        f(e)
                dec(body)
        return nc


def _bundle(Wr, cols):
    return np.ascontiguousarray(Wr[:, :, cols].transpose(1, 0, 2)).reshape(128, -1)


def _r(a, b):
    return list(range(a, b))


def bundle_plan(l):
    plan = []
    is_a = (l % 2 == 0)
    for G in range(3):
        if G == 2:
            for i in range(4):
                plan.append(("kv%d" % i, 8, 256))
        for hh in range(4):
            c = G * 4 + hh
            if G < 2:
                if is_a:
                    for g in range(3):
                        plan.append(("qkv%d_%d" % (g, c), 8, 384))
                    plan.append(("gate_%d" % c, 8, 128))
                else:
                    plan.append(("gate_%d" % c, 8, 128))
                    plan.append(("qfv_%d" % c, 8, 384))
            else:
                if hh % 2 == 0:
                    plan.append(("qm_%d" % (hh // 2), 8, 256))
                plan.append(("gate_%d" % c, 8, 128))
        for nb in range(2):
            plan.append(("wo%d_%d" % (G, nb), 4, 512))
    return plan


def layer_weights(l, inputs):
    j = l // 2
    is_a = (l % 2 == 0)
    w_in = np.asarray(inputs["w_in_a"][j] if is_a else inputs["w_in_b"][j], dtype=np.float32)
    w_out = np.asarray(inputs["w_out_a"][j] if is_a else inputs["w_out_b"][j], dtype=np.float32)
    w_kv = np.asarray(inputs["w_mem_kv"][l], dtype=np.float32)
    Wi = w_in.reshape(8, 128, -1)
    Wkv = w_kv.reshape(8, 128, 1024)
    Wo = w_out.reshape(12, 128, 1024)
    qm0 = 9216 if is_a else 3072
    gt0 = 9728 if is_a else 3584
    parts = []
    for (name, kc, ncols) in bundle_plan(l):
        if name.startswith("kv"):
            i = int(name[2:])
            parts.append(_bundle(Wkv, _r(i * 256, (i + 1) * 256)))
        elif name.startswith("qkv"):
            g, c = name[3:].split("_")
            g, c = int(g), int(c)
            base = g * 1024 + c * 128
            cols = _r(base, base + 128) + _r(3072 + base, 3072 + base + 128) + _r(6144 + base, 6144 + base + 128)
            parts.append(_bundle(Wi, cols))
        elif name.startswith("qfv"):
            c = int(name[4:])
            cols = _r(c * 128, c * 128 + 128) + _r(1024 + c * 128, 1024 + c * 128 + 128) + _r(2048 + c * 128, 2048 + c * 128 + 128)
            parts.append(_bundle(Wi, cols))
        elif name.startswith("gate"):
            c = int(name[5:])
            parts.append(_bundle(Wi, _r(gt0 + c * 128, gt0 + c * 128 + 128)))
        elif name.startswith("qm"):
            i = int(name[3:])
            parts.append(_bundle(Wi, _r(qm0 + i * 256, qm0 + (i + 1) * 256)))
        elif name.startswith("wo"):
            G, nb = name[2:].split("_")
            G, nb = int(G), int(nb)
            parts.append(np.ascontiguousarray(Wo[G * 4:(G + 1) * 4, :, nb * 512:(nb + 1) * 512].transpose(1, 0, 2)).reshape(128, -1))
        else:
            raise ValueError(name)
    return np.ascontiguousarray(np.concatenate(parts, axis=1))


def layer_wtot(l):
    return sum(kc * n for (_, kc, n) in bundle_plan(l))


def small_params(l, inputs):
    j = l // 2
    sp = np.zeros((128, NSPL), np.float32)
    sp[:, 0:8] = np.asarray(inputs["norm_gain"][l]).reshape(8, 128).T
    sp[:, 8:16] = np.asarray(inputs["mem_norm_gain"][l]).reshape(8, 128).T
    sp[:, 16:144] = np.broadcast_to(np.asarray(inputs["mem_q_gain"][l])[None, :], (128, 128))
    sp[:, 144:272] = np.broadcast_to(np.asarray(inputs["mem_k_gain"][l])[None, :], (128, 128))
    if l % 2 == 0:
        qg = np.asarray(inputs["q_gain_a"][j])
        kg = np.asarray(inputs["k_gain_a"][j])
        for g in range(3):
            sp[:, 288 + g * 256:288 + g * 256 + 128] = np.broadcast_to(qg[g][None, :], (128, 128))
            sp[:, 288 + g * 256 + 128:288 + (g + 1) * 256] = np.broadcast_to(kg[g][None, :], (128, 128))
    else:
        sp[:, 272] = np.asarray(inputs["o_gain_b"][j])
    return sp


def const_arrays():
    jj = np.arange(128)[:, None]
    ii = np.arange(128)[None, :]
    cur = (jj <= ii).astype(np.float32)
    prev = (jj >= ii).astype(np.float32)
    bd = ((jj <= ii) & ((jj // 64) == (ii // 64))).astype(np.float32)
    NEG = np.float32(-30000.0)
    prevb = np.where(prev > 0, np.float32(0.0), NEG).astype(np.float32)
    curb = np.where(cur > 0, np.float32(0.0), NEG).astype(np.float32)
    cbf = np.concatenate([np.eye(128, dtype=np.float32), prevb, curb, prevb, curb, curb, curb, curb, curb, bd, bd, bd, bd], axis=1)
    rst = np.ones((128, 512), np.float32)
    rst[:, ::64] = 0.0
    invf = (500000.0 ** (-np.arange(0, 32, 2, dtype=np.float32) / 32.0)).astype(np.float32)
    cf = np.concatenate([rst, np.broadcast_to(invf[None, :], (128, 16))], axis=1).astype(np.float32)
    return np.ascontiguousarray(cbf), np.ascontiguousarray(cf)


def pos_layout(pos):
    out = np.zeros((128, 48), np.int32)
    p = np.arange(128)
    for blk in range(16):
        out[:, blk] = pos[blk * 128 + p]
        c, r = blk // 4, blk % 4
        out[:, 16 + blk] = pos[512 * c + 4 * p + r]
        out[:, 32 + blk] = pos[16 * p + blk]
    return out


class StopBuild(Exception):
    pass


def build(n_layers=DEPTH, n_seq=SEQ_PER_CORE, dbg_stop=None):
    def chk(n):
        if dbg_stop is not None and n == dbg_stop:
            raise StopBuild()
    K = Kern()
    nc = K.nc
    uid = [0]
    gs = ExitStack()

    def dram(name, shape, dt, kind):
        return nc.dram_tensor(name, list(shape), dt, kind=kind).ap()

    x_d = dram("x", [n_seq, S, D], F32, "ExternalInput")
    mem_d = dram("mem", [n_seq, MEM, D], F32, "ExternalInput")
    pos_d = dram("pos", [n_seq, 128, 48], I32, "ExternalInput")
    cbf_d = dram("cbf", [128, 1664], F32, "ExternalInput")
    cf_d = dram("cf", [128, 528], F32, "ExternalInput")
    lbl_d = dram("lbl", [128, 32], F32, "ExternalInput")
    ng_d = dram("ng", [128, 32], F32, "ExternalInput")
    spl_d = dram("spl", [DEPTH, 128, NSPL], F32, "ExternalInput")
    wl_d = [dram("wl%d" % l, [128, layer_wtot(l)], F32, "ExternalInput") for l in range(DEPTH)]
    out_d = dram("out", [n_seq, S, D], F32, "ExternalOutput")

    def sbt(es, name, shape, dt):
        uid[0] += 1
        h = es.enter_context(nc.sbuf_tensor("%s_%d" % (name, uid[0]), list(shape), dt))
        return T(h.ap() if hasattr(h, "ap") else h[:], name)

    x_t = [sbt(gs, "x%d" % i, [128, D], F32) for i in range(NT)]
    hT = sbt(gs, "hT", [128, 8, S], BF16)
    yT = [sbt(gs, "yT%d" % i, [128, S], BF16) for i in range(4)]
    ring = [sbt(gs, "ring%d" % i, [128, RING_COLS], BF16) for i in range(3)]
    ring_sem = [Sem(K, "ring%d" % i, 16) for i in range(3)]
    cbf = sbt(gs, "cbf", [128, 1664], BF16)
    cf = sbt(gs, "cf", [128, 528], F32)
    ones = sbt(gs, "ones", [128, 128], BF16)
    epsT = sbt(gs, "eps", [128, 1], F32)
    spl = sbt(gs, "spl", [128, NSPL], F32)
    lbl = sbt(gs, "lbl", [128, 4, 8], F32)
    ngall = sbt(gs, "ngall", [128, 32], F32)
    lbv = sbt(gs, "lbv", [128, 4, 8], F32)
    COS = sbt(gs, "COS", [128, 48, 16], F32)
    S2 = sbt(gs, "S2", [128, 48, 2, 16], F32)
    memkv = {}
    PB = []
    for i in range(8):
        uid[0] += 1
        h = gs.enter_context(nc.psum_tensor("pb%d" % i, [128, 512], F32))
        PB.append(T(h.ap() if hasattr(h, "ap") else h[:], "pb%d" % i, excl=True))
    ident = cbf.ap[:, 0:128]
    Mpc = cbf.ap[:, 128:640]
    Mcc = cbf.ap[:, 640:1152]
    Mbd = cbf.ap[:, 1152:1664]
    rst = cf.ap[:, 0:512]
    invf = cf.ap[:, 512:528]

    sem_cbf = Sem(K, "dcbf", 16)
    sem_cf = Sem(K, "dcf", 16)
    sem_lbl = Sem(K, "dlbl", 16)
    sem_ng = Sem(K, "dng", 16)
    sem_spl = Sem(K, "dspl", 16)
    sem_pos = Sem(K, "dpos", 16)
    sem_m32 = [Sem(K, "dm32_%d" % i, 16) for i in range(2)]
    sem_x = [Sem(K, "dx%d" % i, 16) for i in range(NT)]

    def bf16v(t):
        return t.ap.bitcast(BF16)

    def mm(out, lhsT, rhs, start, stop, reads, writes):
        K.op("pe", lambda e: e.matmul(out, lhsT=lhsT, rhs=rhs, start=start, stop=stop, skip_group_check=True), reads, writes)

    def tr(out, in_, reads, writes):
        K.op("pe", lambda e: e.transpose(out=out, in_=in_, identity=ident), list(reads) + [cbf], writes)

    def act(out, in_, func, reads, writes, scale=None, bias=None, accum=None):
        kw = {}
        if scale is not None:
            kw["scale"] = scale
        if bias is not None:
            kw["bias"] = bias
        if accum is not None:
            kw["accum_out"] = accum
        K.op("act", lambda e: e.activation(out=out, in_=in_, func=func, **kw), reads, writes)

    def tt(eng, out, in0, in1, op, reads, writes):
        K.op(eng, lambda e: e.tensor_tensor(out=out, in0=in0, in1=in1, op=op), reads, writes)

    def ts(eng, out, in0, s1, s2, op0, op1, reads, writes):
        if op1 is None:
            K.op(eng, lambda e: e.tensor_scalar(out=out, in0=in0, scalar1=s1, scalar2=None, op0=op0), reads, writes)
        else:
            K.op(eng, lambda e: e.tensor_scalar(out=out, in0=in0, scalar1=s1, scalar2=s2, op0=op0, op1=op1), reads, writes)

    def stt(out, in0, scalar, in1, op0, op1, reads, writes):
        K.op("dve", lambda e: e.scalar_tensor_tensor(out=out, in0=in0, scalar=scalar, in1=in1, op0=op0, op1=op1), reads, writes)

    def cp(eng, out, in_, reads, writes):
        if eng == "act":
            act(out, in_, AF.Copy, reads, writes)
        else:
            K.op(eng, lambda e: e.tensor_copy(out=out, in_=in_), reads, writes)

    def recip(out, in_, reads, writes):
        K.op("dve", lambda e: e.reciprocal(out=out, in_=in_), reads, writes)

    def memset(eng, out, val, writes):
        K.op(eng, lambda e: e.memset(out, val), (), writes)

    def bc(ap, shape):
        return ap.broadcast_to(list(shape))

    wplan = []
    for s_ in range(n_seq):
        for l in range(n_layers):
            off = 0
            for (name, kc, ncols) in bundle_plan(l):
                wplan.append((l, name, kc, ncols, off))
                off += kc * ncols
    wstate = {"issued": 0, "used": 0}

    def w_issue_upto(n):
        while wstate["issued"] < min(n, len(wplan)):
            i = wstate["issued"]
            l, name, kc, ncols, off = wplan[i]
            slot = ring[i % 3]
            src = wl_d[l][:, off:off + kc * ncols]
            dst = slot.ap[:, 0:kc * ncols]
            K.op("pool", lambda e, dst=dst, src=src: e.dma_start(out=dst, in_=src), (), [slot], sem=ring_sem[i % 3])
            wstate["issued"] += 1

    def w_get(name, ahead=2):
        i = wstate["used"]
        l, nm, kc, ncols, off = wplan[i]
        assert nm == name, (nm, name)
        w_issue_upto(i + ahead + 1)
        wstate["used"] += 1
        slot = ring[i % 3]
        return slot, slot.ap[:, 0:kc * ncols].rearrange("p (k n) -> p k n", k=kc)

    K.op("pool", lambda e: e.dma_start(out=cbf.ap, in_=cbf_d), (), [cbf], sem=sem_cbf)
    K.op("sp", lambda e: e.dma_start(out=cf.ap, in_=cf_d), (), [cf], sem=sem_cf)
    K.op("sp", lambda e: e.dma_start(out=lbl.ap.rearrange("p a b -> p (a b)"), in_=lbl_d), (), [lbl], sem=sem_lbl)
    K.op("sp", lambda e: e.dma_start(out=ngall.ap, in_=ng_d), (), [ngall], sem=sem_ng)
    memset("dve", ones.ap, 1.0, [ones])
    memset("dve", epsT.ap, EPS, [epsT])
    with ExitStack() as es:
        ex = sbt(es, "lb_ex", [128, 4, 8], F32)
        sm = sbt(es, "lb_sm", [128, 8], F32)
        act(ex.ap, lbl.ap, AF.Exp, [lbl], [ex])
        tt("dve", sm.ap, ex.ap[:, 0, :], ex.ap[:, 1, :], ALU.add, [ex], [sm])
        tt("dve", sm.ap, sm.ap, ex.ap[:, 2, :], ALU.add, [ex, sm], [sm])
        tt("dve", sm.ap, sm.ap, ex.ap[:, 3, :], ALU.add, [ex, sm], [sm])
        recip(sm.ap, sm.ap, [sm], [sm])
        tt("dve", lbv.ap[:, 0, :], ex.ap[:, 1, :], sm.ap, ALU.mult, [ex, sm], [lbv])
        tt("dve", ex.ap[:, 0, :], ex.ap[:, 1, :], ex.ap[:, 2, :], ALU.add, [ex], [ex])
        tt("dve", ex.ap[:, 0, :], ex.ap[:, 0, :], ex.ap[:, 3, :], ALU.add, [ex], [ex])
        tt("dve", lbv.ap[:, 1, :], ex.ap[:, 0, :], sm.ap, ALU.mult, [ex, sm], [lbv])
        ts("dve", lbv.ap[:, 2:4, :], lbv.ap[:, 0:2, :], -1.0, 1.0, ALU.mult, ALU.add, [lbv], [lbv])
        K.barrier()

    pjc = [0]
    pj_banks = [[0, 1]]

    def next_pj():
        pjc[0] += 1
        bl = pj_banks[0]
        return PB[bl[pjc[0] % len(bl)]]

    def rms_rstd(ss, rstd, n, reads_extra=()):
        act(rstd.ap, ss.ap, AF.Sqrt, [ss, epsT], [rstd], scale=1.0 / n, bias=epsT.ap[:, 0:1])
        recip(rstd.ap, rstd.ap, [rstd], [rstd])

    def tok_cols(g, ti):
        if g == 0:
            return slice(ti * 128, (ti + 1) * 128)
        if g == 1:
            c, r = ti // 4, ti % 4
            return slice(512 * c + r, 512 * c + 512, 4)
        return slice(ti, S, 16)

    def load_spl(l):
        K.op("sp", lambda e: e.dma_start(out=spl.ap, in_=spl_d[l]), (), [spl], sem=sem_spl)

    def run_pipeline(n, stages):
        ns = len(stages)
        for step in range(n + ns - 1):
            for k in range(ns - 1, -1, -1):
                i = step - k
                if 0 <= i < n:
                    stages[k](i)

    def hT_stages(es, l):
        junk = sbt(es, "junk", [128, D], BF16)
        NH = 4
        hn = [sbt(es, "hn%d" % i, [128, D], BF16) for i in range(NH)]
        ss = [sbt(es, "hss%d" % i, [128, 1], F32) for i in range(NH)]
        rs = [sbt(es, "hrs%d" % i, [128, 1], F32) for i in range(NH)]
        gain = ngall.ap[:, l * 8:(l + 1) * 8]

        def s0(ti):
            b = ti % NH
            act(junk.ap, x_t[ti].ap, AF.Square, [x_t[ti]], [ss[b], junk], accum=ss[b].ap)

        def s1(ti):
            b = ti % NH
            rms_rstd(ss[b], rs[b], float(D))

        def s2(ti):
            b = ti % NH
            ts("dve", hn[b].ap, x_t[ti].ap, rs[b].ap[:, 0:1], None, ALU.mult, None, [x_t[ti], rs[b]], [hn[b]])

        def s3(ti):
            b = ti % NH
            trb = PB[2 + ti % 2]
            tv = bf16v(trb).rearrange("p (k n) -> p k n", k=8)
            for kc in range(8):
                tr(tv[:, kc, :], hn[b].ap[:, kc * 128:(kc + 1) * 128], [hn[b]], [trb])

        def s4(ti):
            trb = PB[2 + ti % 2]
            tv = bf16v(trb).rearrange("p (k n) -> p k n", k=8)
            tt("dve", hT.ap[:, :, ti * 128:(ti + 1) * 128], tv, bc(gain.unsqueeze(2), [128, 8, 128]), ALU.mult, [trb, ngall], [hT])

        return [s0, s1, s2, s3, s4]

    def build_hT(es, l):
        run_pipeline(NT, hT_stages(es, l))

    def mem_prep(s_, l):
        kmT, vm = memkv["kmT"], memkv["vm"]
        with ExitStack() as es:
            m32 = [sbt(es, "m32_%d" % i, [128, D], F32) for i in range(2)]
            junk = sbt(es, "mjunk", [128, D], BF16)
            mh = [sbt(es, "mh%d" % i, [128, D], BF16) for i in range(2)]
            ss = [sbt(es, "mss%d" % i, [128, 4], F32) for i in range(2)]
            rs = [sbt(es, "mrs%d" % i, [128, 4], F32) for i in range(2)]
            mnT = sbt(es, "mnT", [128, 8, MEM], BF16)
            kn = [sbt(es, "kn%d" % i, [128, 512], F32) for i in range(2)]
            knb = [sbt(es, "knb%d" % i, [128, 512], BF16) for i in range(2)]
            for i in range(2):
                K.op("sp", lambda e, i=i: e.dma_start(out=m32[i].ap, in_=mem_d[s_, i * 128:(i + 1) * 128, :]), (), [m32[i]], sem=sem_m32[i])
            for i in range(2):
                act(junk.ap, m32[i].ap, AF.Square, [m32[i]], [ss[i], junk], accum=ss[i].ap[:, 0:1])
                act(rs[i].ap[:, 0:1], ss[i].ap[:, 0:1], AF.Sqrt, [ss[i], epsT], [rs[i]], scale=1.0 / D, bias=epsT.ap[:, 0:1])
                recip(rs[i].ap[:, 0:1], rs[i].ap[:, 0:1], [rs[i]], [rs[i]])
                ts("dve", mh[i].ap, m32[i].ap, rs[i].ap[:, 0:1], None, ALU.mult, None, [m32[i], rs[i]], [mh[i]])
                trb = PB[2 + i]
                tv = bf16v(trb).rearrange("p (k n) -> p k n", k=8)
                for kc in range(8):
                    tr(tv[:, kc, :], mh[i].ap[:, kc * 128:(kc + 1) * 128], [mh[i]], [trb])
                tt("dve", mnT.ap[:, :, i * 128:(i + 1) * 128], tv, bc(spl.ap[:, 8:16].unsqueeze(2), [128, 8, 128]), ALU.mult, [trb, spl], [mnT])
            pk = [PB[4], PB[5]]
            pv = [PB[6], PB[7]]
            for bi in range(4):
                slot, wv = w_get("kv%d" % bi)
                for i in range(2):
                    dst = (pk if bi < 2 else pv)[i]
                    c0 = (bi % 2) * 256
                    for kc in range(8):
                        mm(dst.ap[:, c0:c0 + 256], mnT.ap[:, kc, i * 128:(i + 1) * 128], wv[:, kc, :], kc == 0, kc == 7, [mnT, slot], [dst])
            for i in range(2):
                for hh in range(4):
                    act(junk.ap[:, 0:128], pk[i].ap[:, hh * 128:(hh + 1) * 128], AF.Square, [pk[i]], [ss[i], junk], accum=ss[i].ap[:, hh:hh + 1])
                rms_rstd(ss[i], rs[i], 128.0)
                tt("dve", kn[i].ap.rearrange("p (h d) -> p h d", h=4), pk[i].ap.rearrange("p (h d) -> p h d", h=4),
                   bc(rs[i].ap.unsqueeze(2), [128, 4, 128]), ALU.mult, [pk[i], rs[i]], [kn[i]])
                tt("pool", knb[i].ap.rearrange("p (h d) -> p h d", h=4), kn[i].ap.rearrange("p (h d) -> p h d", h=4),
                   bc(spl.ap[:, 144:272].unsqueeze(1), [128, 4, 128]), ALU.mult, [kn[i], spl], [knb[i]])
                trb = PB[2 + i]
                tv = bf16v(trb)[:, 0:512].rearrange("p (k n) -> p k n", k=4)
                for hh in range(4):
                    tr(tv[:, hh, :], knb[i].ap[:, hh * 128:(hh + 1) * 128], [knb[i]], [trb])
                cp("act", kmT.ap[:, :, i * 128:(i + 1) * 128], tv, [trb], [kmT])
                cp("act", vm.ap[:, i, :], pv[i].ap, [pv[i]], [vm])
            K.barrier()

    def final_norm(num, den, sgw, ydst, ytile, rden, t1):
        act(rden.ap, den.ap, AF.Ln, [den], [rden])
        act(rden.ap, rden.ap, AF.Exp, [rden], [rden], scale=-1.0)
        tt("dve", rden.ap, rden.ap, sgw, ALU.mult, [rden] + sgw_t[0], [rden])
        tt("dve", ydst, num.ap, rden.ap, ALU.mult, [num, rden], [ytile])

    sgw_t = [[]]

    def mem_heads(es, l, is_last_layer):
        kmT, vm = memkv["kmT"], memkv["vm"]
        NS = 4
        junk2 = [sbt(es, "qjunk%d" % i, [128, 128], BF16) for i in range(2)]
        qmT = [sbt(es, "qmT%d" % i, [128, S], BF16) for i in range(2)]
        sg = [sbt(es, "msg%d" % i, [128, S], BF16) for i in range(2)]
        ss = [sbt(es, "qss%d" % i, [128, 2], F32) for i in range(NS)]
        rs = [sbt(es, "qrs%d" % i, [128, 2], F32) for i in range(NS)]
        qn = [sbt(es, "qn%d" % i, [128, 256], F32) for i in range(NS)]
        qnb = [sbt(es, "qnb%d" % i, [128, 256], BF16) for i in range(NS)]
        PT = [sbt(es, "mPT%d" % i, [128, 512], BF16) for i in range(3)]
        rdens = [sbt(es, "mrden%d" % i, [128, 512], F32) for i in range(2)]
        ptc = [0]
        wcm = [0]
        stc = [0]
        for hp in range(2):
            pjs = {}
            wc_ = {}

            def s0(i):
                if i == 0:
                    wc_["w"] = w_get("qm_%d" % hp)
                slot, wv = wc_["w"]
                pj = next_pj()
                pjs[i] = pj
                for kc in range(8):
                    mm(pj.ap[:, 0:256], hT.ap[:, kc, i * 128:(i + 1) * 128], wv[:, kc, :], kc == 0, kc == 7, [hT, slot], [pj])

            def s1(i):
                b = i % NS
                pj = pjs.pop(i)
                for q in range(2):
                    act(junk2[q].ap, pj.ap[:, q * 128:(q + 1) * 128], AF.Square, [pj], [ss[b], junk2[q]], accum=ss[b].ap[:, q:q + 1])
                tt("dve", qn[b].ap.rearrange("p (h d) -> p h d", h=2), pj.ap[:, 0:256].rearrange("p (h d) -> p h d", h=2),
                   bc(spl.ap[:, 16:144].unsqueeze(1), [128, 2, 128]), ALU.mult, [pj, spl], [qn[b]])

            def s2(i):
                b = i % NS
                rms_rstd(ss[b], rs[b], 128.0)

            def s3(i):
                b = i % NS
                tt("pool", qnb[b].ap.rearrange("p (h d) -> p h d", h=2), qn[b].ap.rearrange("p (h d) -> p h d", h=2),
                   bc(rs[b].ap.unsqueeze(2), [128, 2, 128]), ALU.mult, [qn[b], rs[b]], [qnb[b]])

            def s4(i):
                b = i % NS
                trb = PB[2 + (i // 4) % 2]
                tv = bf16v(trb).rearrange("p (k n) -> p k n", k=8)
                for q in range(2):
                    tr(tv[:, (i % 4) * 2 + q, :], qnb[b].ap[:, q * 128:(q + 1) * 128], [qnb[b]], [trb])

            def s5(i):
                if i % 4 == 3:
                    trb = PB[2 + (i // 4) % 2]
                    t0 = i - 3
                    tv4 = bf16v(trb).rearrange("p (t q n) -> p t q n", t=4, q=2)
                    cp("act", qmT[0].ap[:, t0 * 128:(t0 + 4) * 128].rearrange("p (t n) -> p t n", t=4), tv4[:, :, 0, :], [trb], [qmT[0]])
                    cp("dve", qmT[1].ap[:, t0 * 128:(t0 + 4) * 128].rearrange("p (t n) -> p t n", t=4), tv4[:, :, 1, :], [trb], [qmT[1]])

            pj_banks[0] = [0, 1, 4, 5]
            run_pipeline(NT, [s0, s1, s2, s3, s4, s5])
            pj_banks[0] = [0, 1]
            for q in range(2):
                c = 8 + hp * 2 + q
                slot, wv = w_get("gate_%d" % c)
                for tb in range(4):
                    pj = next_pj()
                    for kc in range(8):
                        mm(pj.ap, wv[:, kc, :], hT.ap[:, kc, tb * 512:(tb + 1) * 512], kc == 0, kc == 7, [hT, slot], [pj])
                    act(sg[q].ap[:, tb * 512:(tb + 1) * 512], pj.ap, AF.Silu, [pj], [sg[q]])
            A = []
            Bq = []
            for q in range(2):
                mh = hp * 2 + q
                for w in range(4):
                    wcm[0] += 1
                    num, den = (PB[6], PB[7]) if wcm[0] % 2 == 0 else (PB[2], PB[3])
                    rden = rdens[wcm[0] % 2]
                    for mt in range(2):
                        box = {}

                        def fa(q=q, mh=mh, w=w, mt=mt, box=box):
                            st = PB[4 + stc[0] % 2]
                            stc[0] += 1
                            mm(st.ap, kmT.ap[:, mh, mt * 128:(mt + 1) * 128], qmT[q].ap[:, w * 512:(w + 1) * 512], True, True, [kmT, qmT[q]], [st])
                            pt = PT[ptc[0] % 3]
                            ptc[0] += 1
                            act(pt.ap, st.ap, AF.Exp, [st], [pt], scale=ATT)
                            box["pt"] = pt

                        def fb(q=q, mh=mh, w=w, mt=mt, box=box, num=num, den=den, rden=rden):
                            pt = box["pt"]
                            mm(num.ap, vm.ap[:, mt, mh * 128:(mh + 1) * 128], pt.ap, mt == 0, mt == 1, [vm, pt], [num])
                            mm(den.ap, ones.ap, pt.ap, mt == 0, mt == 1, [ones, pt], [den])
                            if mt == 1:
                                sgw_t[0] = [sg[q]]
                                final_norm(num, den, sg[q].ap[:, w * 512:(w + 1) * 512], yT[mh].ap[:, w * 512:(w + 1) * 512], yT[mh], rden, None)
                        A.append(fa)
                        Bq.append(fb)
            LAG = 2
            nb = len(A)
            for step in range(nb + LAG):
                if step < nb:
                    A[step]()
                if step - LAG >= 0:
                    Bq[step - LAG]()

    def out_proj(G, s_, is_last, next_hT=None):
        for nb in range(2):
            slot, wv = w_get("wo%d_%d" % (G, nb))
            fuse = next_hT is not None and nb == 1
            ns = len(next_hT) if fuse else 0
            for ti in range(NT + ns if fuse else NT):
                if ti < NT:
                    pj = next_pj()
                    for ci in range(4):
                        mm(pj.ap, yT[ci].ap[:, ti * 128:(ti + 1) * 128], wv[:, ci, :], ci == 0, ci == 3, [yT[ci], slot], [pj])
                    xs = x_t[ti].ap[:, nb * 512:(nb + 1) * 512]
                    tt("dve", xs, pj.ap, xs, ALU.add, [pj, x_t[ti]], [x_t[ti]])
                    if is_last and G == 2 and nb == 1:
                        K.op("sp", lambda e, ti=ti: e.dma_start(out=out_d[s_, ti * 128:(ti + 1) * 128, :], in_=x_t[ti].ap),
                             [x_t[ti]], (), sem=sem_x[ti])
                if fuse:
                    for k in range(ns - 1, -1, -1):
                        i = ti - 1 - k
                        if 0 <= i < NT:
                            next_hT[k](i)

    def layer_A(s_, l, is_last):
        if l == 0:
            with ExitStack() as es:
                build_hT(es, l)
                K.barrier()
        chk(3)
        with ExitStack() as es:
            qT = [sbt(es, "qT%d" % g, [128, S], BF16) for g in range(3)]
            kT = [sbt(es, "kT%d" % g, [128, S], BF16) for g in range(3)]
            V = [sbt(es, "V%d" % g, [128, NT, 128], BF16) for g in range(3)]
            sg = sbt(es, "sg", [128, S], BF16)
            PT2 = sbt(es, "PT2", [128, NT, 128], BF16)
            PT = [sbt(es, "PT%d" % i, [128, 512], BF16) for i in range(3)]
            junk2 = [sbt(es, "ajunk%d" % i, [128, 128], BF16) for i in range(2)]
            NSET = 4
            ss = [sbt(es, "ass%d" % i, [128, 2], F32) for i in range(NSET)]
            rs = [sbt(es, "ars%d" % i, [128, 2], F32) for i in range(NSET)]
            gq = [sbt(es, "gq%d" % i, [128, 256], F32) for i in range(NSET)]
            rB = [sbt(es, "rB%d" % i, [128, 64], F32) for i in range(NSET)]
            qkb = [sbt(es, "qkb%d" % i, [128, 256], BF16) for i in range(NSET)]
            rdens = [sbt(es, "rden%d" % i, [128, 512], F32) for i in range(2)]
            wc = [0]
            ptc = [0]
            stc = [0]
            tic = [0]

            def run_pipeline(n, stages):
                ns = len(stages)
                for step in range(n + ns - 1):
                    for k in range(ns - 1, -1, -1):
                        i = step - k
                        if 0 <= i < n:
                            stages[k](i)

            wcur = {}

            def proj_head(c):
                def geti(i):
                    return i // 16, i % 16, i % NSET

                def s0(i):
                    g, ti, b = geti(i)
                    if ti == 0:
                        wcur[g] = w_get("qkv%d_%d" % (g, c))
                    slot, wv = wcur[g]
                    pj = next_pj()
                    pjs[i] = pj
                    cols = tok_cols(g, ti)
                    for kc in range(8):
                        mm(pj.ap[:, 0:384], hT.ap[:, kc, cols], wv[:, kc, :], kc == 0, kc == 7, [hT, slot], [pj])

                def s1(i):
                    g, ti, b = geti(i)
                    pj = pjs.pop(i)
                    for q in range(2):
                        act(junk2[q].ap, pj.ap[:, q * 128:(q + 1) * 128], AF.Square, [pj], [ss[b], junk2[q]], accum=ss[b].ap[:, q:q + 1])
                    tt("dve", gq[b].ap, pj.ap[:, 0:256], spl.ap[:, 288 + g * 256:288 + (g + 1) * 256], ALU.mult, [pj, spl], [gq[b]])
                    cp("act", V[g].ap[:, ti, :], pj.ap[:, 256:384], [pj], [V[g]])

                def s2(i):
                    g, ti, b = geti(i)
                    rms_rstd(ss[b], rs[b], 128.0)
                    R = gq[b].ap.rearrange("p (a d) -> p a d", a=2)[:, :, 0:32].rearrange("p a (h f) -> p a h f", h=2)
                    Rsw = R[:, :, ::-1, :]
                    cosb = bc(COS.ap[:, g * 16 + ti, :].unsqueeze(1).unsqueeze(1), [128, 2, 2, 16])
                    s2b = bc(S2.ap[:, g * 16 + ti, :, :].unsqueeze(1), [128, 2, 2, 16])
                    B4 = rB[b].ap.rearrange("p (a h f) -> p a h f", a=2, h=2)
                    tt("pool", B4, Rsw, s2b, ALU.mult, [gq[b], S2], [rB[b]])
                    tt("pool", R, R, cosb, ALU.mult, [gq[b], COS], [gq[b]])
                    tt("pool", R, R, B4, ALU.add, [gq[b], rB[b]], [gq[b]])

                def s3(i):
                    g, ti, b = geti(i)
                    tt("dve", qkb[b].ap.rearrange("p (a d) -> p a d", a=2), gq[b].ap.rearrange("p (a d) -> p a d", a=2),
                       bc(rs[b].ap.unsqueeze(2), [128, 2, 128]), ALU.mult, [gq[b], rs[b]], [qkb[b]])

                def s4(i):
                    g, ti, b = geti(i)
                    trb = PB[2 + (i // 4) % 2]
                    tv = bf16v(trb).rearrange("p (k n) -> p k n", k=8)
                    for q in range(2):
                        tr(tv[:, (i % 4) * 2 + q, :], qkb[b].ap[:, q * 128:(q + 1) * 128], [qkb[b]], [trb])

                def s5(i):
                    g, ti, b = geti(i)
                    if i % 4 == 3:
                        trb = PB[2 + (i // 4) % 2]
                        t0 = ti - 3
                        tv4 = bf16v(trb).rearrange("p (t q n) -> p t q n", t=4, q=2)
                        cp("act", qT[g].ap[:, t0 * 128:(t0 + 4) * 128].rearrange("p (t n) -> p t n", t=4), tv4[:, :, 0, :], [trb], [qT[g]])
                        cp("dve", kT[g].ap[:, t0 * 128:(t0 + 4) * 128].rearrange("p (t n) -> p t n", t=4), tv4[:, :, 1, :], [trb], [kT[g]])

                pjs = {}
                run_pipeline(48, [s0, s1, s2, s3, s4, s5])

            def gate(c):
                slot, wv = w_get("gate_%d" % c)
                for tb in range(4):
                    pj = next_pj()
                    for kc in range(8):
                        mm(pj.ap, wv[:, kc, :], hT.ap[:, kc, tb * 512:(tb + 1) * 512], kc == 0, kc == 7, [hT, slot], [pj])
                    act(sg.ap[:, tb * 512:(tb + 1) * 512], pj.ap, AF.Silu, [pj], [sg])

            ST_BANKS = [4, 5, 2, 3]

            def attn(c):
                LAG = 2
                A = []
                Bq = []

                def add_st(jobs, mask, pv_list_fn):
                    box = {}

                    def fa():
                        st = PB[ST_BANKS[stc[0] % 4]]
                        stc[0] += 1
                        lo = min(j[3] for j in jobs) * 128
                        hi = (max(j[3] for j in jobs) + 1) * 128
                        for n_, (g, kb, qb, sl) in enumerate(jobs):
                            mm(st.ap[:, sl * 128:(sl + 1) * 128], kT[g].ap[:, kb * 128:(kb + 1) * 128], qT[g].ap[:, qb * 128:(qb + 1) * 128],
                               n_ == 0, False, [kT[g], qT[g]], [st])
                        mm(st.ap[:, lo:hi], ident, mask[:, lo:hi], False, True, [cbf], [st])
                        pt = PT[ptc[0] % 3]
                        ptc[0] += 1
                        act(pt.ap[:, lo:hi], st.ap[:, lo:hi], AF.Exp, [st], [pt], scale=ATT)
                        box["pt"] = pt

                    def fb():
                        pv_list_fn(box["pt"])
                    A.append(fa)
                    Bq.append(fb)

                for bb in range(4):
                    def fa2(bb=bb):
                        st = PB[ST_BANKS[stc[0] % 4]]
                        stc[0] += 1
                        for j in range(4):
                            r = bb * 4 + j
                            mm(st.ap[:, j * 128:(j + 1) * 128], kT[2].ap[:, r * 128:(r + 1) * 128], qT[2].ap[:, r * 128:(r + 1) * 128],
                               j == 0, False, [kT[2], qT[2]], [st])
                        mm(st.ap, ident, Mcc, False, True, [cbf], [st])
                        dst = PT2.ap[:, bb * 4:(bb + 1) * 4, :].rearrange("p a b -> p (a b)")
                        act(dst, st.ap, AF.Exp, [st], [PT2], scale=ATT)
                    A.append(fa2)
                    Bq.append(None)
                for w in range(4):
                    wc[0] += 1
                    num, den = (PB[6], PB[7]) if wc[0] % 2 == 0 else (PB[0], PB[1])
                    rden = rdens[wc[0] % 2]
                    first = [True]

                    def pv(vt, vap, pt_t, ptap, ocols, num=num, den=den, first=first):
                        mm(num.ap[:, ocols], vap, ptap, first[0], False, [vt, pt_t], [num])
                        mm(den.ap[:, ocols], ones.ap, ptap, first[0], False, [ones, pt_t], [den])
                        first[0] = False
                    for half in range(2):
                        jobs = []
                        for k in range(2):
                            qb = 4 * w + 2 * half + k
                            if qb > 0:
                                jobs.append((0, qb - 1, qb, 2 * k))
                            jobs.append((0, qb, qb, 2 * k + 1))

                        def pvs0(pt, jobs=jobs, pv=pv):
                            for (g, kb, qb, sl) in jobs:
                                pv(V[0], V[0].ap[:, kb, :], pt, pt.ap[:, sl * 128:(sl + 1) * 128], slice((qb % 4) * 128, (qb % 4 + 1) * 128))
                        add_st(jobs, Mpc, pvs0)
                    if w == 0:
                        jl = [([(1, r, r, r) for r in range(4)], Mcc)]
                    else:
                        jl = []
                        for half in range(2):
                            jobs = []
                            for k in range(2):
                                r = 2 * half + k
                                qb = 4 * w + r
                                jobs.append((1, qb - 4, qb, 2 * k))
                                jobs.append((1, qb, qb, 2 * k + 1))
                            jl.append((jobs, Mpc))
                    for (jobs, mask) in jl:
                        def pvs1(pt, jobs=jobs, pv=pv):
                            for (g, kb, qb, sl) in jobs:
                                pv(V[1], V[1].ap[:, kb, :], pt, pt.ap[:, sl * 128:(sl + 1) * 128], slice(qb % 4, 512, 4))
                        add_st(jobs, mask, pvs1)

                    def fin(w=w, pv=pv, num=num, den=den, rden=rden):
                        for r in range(16):
                            pv(V[2], V[2].ap[:, r, :], PT2, PT2.ap[:, r, 32 * w:32 * w + 32], slice(r, 512, 16))
                        sgw_t[0] = [sg]
                        final_norm(num, den, sg.ap[:, w * 512:(w + 1) * 512], yT[c % 4].ap[:, w * 512:(w + 1) * 512], yT[c % 4], rden, None)
                    A.append(None)
                    Bq.append(fin)
                nb = len(A)
                for step in range(nb + LAG):
                    if step < nb and A[step] is not None:
                        A[step]()
                    if step - LAG >= 0 and Bq[step - LAG] is not None:
                        Bq[step - LAG]()

            for G in range(2):
                for hh in range(4):
                    c = G * 4 + hh
                    pj_banks[0] = [0, 1, 4, 5]
                    proj_head(c)
                    chk(6)
                    gate(c)
                    pj_banks[0] = [0, 1]
                    chk(7)
                    attn(c)
                    chk(8)
                out_proj(G, s_, False)
                chk(9)
            K.barrier()
        with ExitStack() as es:
            memkv["kmT"] = sbt(es, "kmT", [128, 4, MEM], BF16)
            memkv["vm"] = sbt(es, "vm", [128, 2, 512], BF16)
            mem_prep(s_, l)
            mem_heads(es, l, is_last)
            out_proj(2, s_, is_last, next_hT=None if is_last else hT_stages(es, l + 1))
            K.barrier()

    def layer_B(s_, l, is_last):
        jb = l // 2
        if l == 0:
            with ExitStack() as es:
                build_hT(es, l)
                K.barrier()
        with ExitStack() as es:
            def mk(name, shape, dt, n=2):
                return [sbt(es, "%s%d" % (name, i), shape, dt) for i in range(n)]
            qfs = mk("qf", [128, 512], F32)
            Fts = mk("Ft", [128, 512], F32)
            Gt = sbt(es, "Gt", [128, 512], F32)
            kk = sbt(es, "kk", [128, 512], F32)
            Dts = mk("Dt", [128, 512], F32)
            Ets = mk("Et", [128, 512], BF16)
            dec = [0]
            Rt = mk("Rt", [128, 8, 4], F32)
            eg = mk("eg", [128, 8], F32, 3)
            qh = mk("qh", [128, 512], BF16)
            qi = mk("qi", [128, 512], BF16, 3)
            kh = [mk("kh%d_" % I, [128, 512], BF16) for I in range(4)]
            kd = mk("kd", [128, 512], BF16)
            kdt = mk("kdt", [128, 4, 128], BF16)
            Vb = mk("Vb", [128, 4, 128], BF16, 3)
            sgu = mk("sgu", [128, 512], BF16, 3)
            ATb = mk("ATb", [128, 512], BF16)
            Sall = mk("Sall", [128, 8, 128], BF16, 3)
            Sf = mk("Sf", [128, 128], F32)
            sq = sbt(es, "osq", [128, 512], BF16)
            Uf = mk("Uf", [128, 512], F32)
            rsn = sbt(es, "orsn", [128, 512], F32)
            sfc = [0]
            wcur = {}
            for I in range(3):
                for j_ in range(2):
                    memset("pool", kh[I][j_].ap, 0.0, [kh[I][j_]])

            def mA(u):
                c, tb = divmod(u, 4)
                u2 = u % 2
                qf, Ft = qfs[u2], Fts[u2]
                if tb == 0:
                    wcur[("g", c)] = w_get("gate_%d" % c)
                    wcur[("q", c)] = w_get("qfv_%d" % c, ahead=1)
                slot_q, wq = wcur[("q", c)]
                cols = slice(tb * 512, (tb + 1) * 512)
                pq = next_pj()
                for kc in range(8):
                    mm(pq.ap, wq[:, kc, 0:128], hT.ap[:, kc, cols], kc == 0, kc == 7, [hT, slot_q], [pq])
                cp("act", qf.ap, pq.ap, [pq], [qf])
                pf = next_pj()
                for kc in range(8):
                    mm(pf.ap, wq[:, kc, 128:256], hT.ap[:, kc, cols], kc == 0, kc == 7, [hT, slot_q], [pf])
                act(Ft.ap, pf.ap, AF.Sigmoid, [pf], [Ft])

            def m0(u):
                c, tb = divmod(u, 4)
                u2, u3 = u % 2, u % 3
                qf, Ft = qfs[u2], Fts[u2]
                slot_g, wg = wcur[("g", c)]
                slot_q, wq = wcur[("q", c)]
                lb_ap = lbv.ap[:, jb, c:c + 1]
                oml_ap = lbv.ap[:, 2 + jb, c:c + 1]
                cols = slice(tb * 512, (tb + 1) * 512)
                ts("dve", Ft.ap, Ft.ap, oml_ap, lb_ap, ALU.mult, ALU.add, [Ft, lbv], [Ft])
                ts("pool", kk.ap, Ft.ap, -1.0, 1.0, ALU.mult, ALU.add, [Ft], [kk])
                act(Ft.ap, Ft.ap, AF.Ln, [Ft], [Ft])
                K.op("dve", lambda e: e.tensor_tensor_scan(out=Gt.ap, data0=rst, data1=Ft.ap, initial=0.0, op0=ALU.mult, op1=ALU.add),
                     [Ft, cf], [Gt])
                G4 = Gt.ap.rearrange("p (c i t) -> p c i t", c=8, i=4)
                G3 = Gt.ap.rearrange("p (c t) -> p c t", c=8)
                R_ = Rt[u2]
                memset("pool", R_.ap[:, :, 0:1], 0.0, [R_])
                cp("pool", R_.ap[:, :, 1:4], G4[:, :, 0:3, 15], [Gt], [R_])
                yield
                trip = []

                def add_trip(sub_fn, exp_src, mult_fn):
                    trip.append((sub_fn, exp_src, mult_fn))

                add_trip(None, None, lambda Et: tt("dve", qi[u3].ap, qf.ap, Et.ap, ALU.mult, [qf, Et], [qi[u3]]))
                add_trip(lambda Dt: tt("pool", Dt.ap.rearrange("p (c i t) -> p c i t", c=8, i=4), G4, bc(R_.ap.unsqueeze(3), [128, 8, 4, 16]), ALU.subtract, [Gt, R_], [Dt]),
                         True, lambda Et: tt("dve", qh[u2].ap, qf.ap, Et.ap, ALU.mult, [qf, Et], [qh[u2]]))
                kk3 = kk.ap.rearrange("p (c t) -> p c t", c=8)
                for I in range(4):
                    se = "dve" if I % 2 == 0 else "pool"
                    wI = 16 * (I + 1)
                    add_trip(lambda Dt, I=I, se=se, wI=wI: tt(se, Dt.ap.rearrange("p (c t) -> p c t", c=8)[:, :, 0:wI], bc(R_.ap[:, :, I:I + 1], [128, 8, wI]), G3[:, :, 0:wI], ALU.subtract, [Gt, R_], [Dt]),
                             wI, lambda Et, I=I, wI=wI: stt(kh[I][u2].ap.rearrange("p (c t) -> p c t", c=8)[:, :, 0:wI], Et.ap.rearrange("p (c t) -> p c t", c=8)[:, :, 0:wI], 1e30, kk3[:, :, 0:wI], ALU.min, ALU.mult, [Et, kk], [kh[I][u2]]))
                add_trip(lambda Dt: tt("dve", Dt.ap.rearrange("p (c t) -> p c t", c=8), bc(G3[:, :, 63:64], [128, 8, 64]), G3, ALU.subtract, [Gt], [Dt]),
                         True, lambda Et: tt("dve", kd[u2].ap, Et.ap, kk.ap, ALU.mult, [Et, kk], [kd[u2]]))
                nt_ = len(trip)
                base = dec[0]
                dec[0] += nt_
                for k in range(nt_ + 2):
                    if k < nt_ and trip[k][0] is not None:
                        trip[k][0](Dts[(base + k) % 2])
                    if 0 <= k - 1 < nt_:
                        j = k - 1
                        src = Dts[(base + j) % 2] if trip[j][1] else Gt
                        ed = Ets[(base + j) % 2]
                        if trip[j][1] is True or trip[j][1] is None or trip[j][1] == 64:
                            act(ed.ap, src.ap, AF.Exp, [src], [ed])
                        else:
                            wj = trip[j][1]
                            act(ed.ap.rearrange("p (c t) -> p c t", c=8)[:, :, 0:wj], src.ap.rearrange("p (c t) -> p c t", c=8)[:, :, 0:wj], AF.Exp, [src], [ed])
                    if 0 <= k - 2 < nt_:
                        j = k - 2
                        trip[j][2](Ets[(base + j) % 2])
                    yield
                act(eg[u3].ap, G3[:, :, 63], AF.Exp, [Gt], [eg[u3]])
                yield
                pvv = next_pj()
                for t4 in range(4):
                    ti = tb * 4 + t4
                    for kc in range(8):
                        mm(pvv.ap[:, t4 * 128:(t4 + 1) * 128], hT.ap[:, kc, ti * 128:(ti + 1) * 128], wq[:, kc, 256:384], kc == 0, kc == 7, [hT, slot_q], [pvv])
                cp("act", Vb[u3].ap.rearrange("p a b -> p (a b)"), pvv.ap, [pvv], [Vb[u3]])
                pg = next_pj()
                for kc in range(8):
                    mm(pg.ap, wg[:, kc, :], hT.ap[:, kc, cols], kc == 0, kc == 7, [hT, slot_g], [pg])
                act(sgu[u3].ap, pg.ap, AF.Sigmoid, [pg], [sgu[u3]])
                tt("dve", sgu[u3].ap, pg.ap, sgu[u3].ap, ALU.mult, [pg, sgu[u3]], [sgu[u3]])

            def m1(u):
                c, tb = divmod(u, 4)
                u2, u3 = u % 2, u % 3
                trb = PB[2]
                tv = bf16v(trb)[:, 0:512].rearrange("p (k n) -> p k n", k=4)
                for t4 in range(4):
                    tr(tv[:, t4, :], kd[u2].ap[:, t4 * 128:(t4 + 1) * 128], [kd[u2]], [trb])
                cp("act", kdt[u2].ap, tv, [trb], [kdt[u2]])
                yield
                at = PB[3]
                fst = True
                for t4 in range(4):
                    for a in range(2):
                        for I in range(4):
                            c0 = t4 * 128 + a * 64 + 16 * I
                            mm(at.ap[:, c0:c0 + 16], kh[I][u2].ap[:, t4 * 128:(t4 + 1) * 128], qh[u2].ap[:, c0:c0 + 16], fst, False, [kh[I][u2], qh[u2]], [at])
                            fst = False
                tt("dve", ATb[u2].ap, at.ap, Mbd, ALU.mult, [at, cbf], [ATb[u2]])
                yield
                for ch in range(8):
                    t4, a = ch // 2, ch % 2
                    ub = PB[4 + a]
                    mm(ub.ap[:, t4 * 128:(t4 + 1) * 128], kdt[u2].ap[64 * a:64 * a + 64, t4, :], Vb[u3].ap[64 * a:64 * a + 64, t4, :],
                       t4 == 0, t4 == 3, [kdt[u2], Vb[u3]], [ub])
                if tb == 0:
                    memset("pool", Sall[u3].ap[:, 0, :], 0.0, [Sall[u3]])
                    memset("dve", Sf[sfc[0] % 2].ap, 0.0, [Sf[sfc[0] % 2]])
                cp("act", Uf[0].ap, PB[4].ap, [PB[4]], [Uf[0]])
                cp("act", Uf[1].ap, PB[5].ap, [PB[5]], [Uf[1]])
                yield
                for ch in range(8):
                    so = Sf[sfc[0] % 2]
                    sn = Sf[(sfc[0] + 1) % 2]
                    sfc[0] += 1
                    stt(sn.ap, so.ap, eg[u3].ap[:, ch:ch + 1], Uf[ch % 2].ap[:, (ch // 2) * 128:(ch // 2 + 1) * 128], ALU.mult, ALU.add, [so, eg[u3], Uf[ch % 2]], [sn])
                    if ch < 7:
                        cp("act", Sall[u3].ap[:, ch + 1, :], sn.ap, [sn], [Sall[u3]])
                    elif tb < 3:
                        cp("act", Sall[(u + 1) % 3].ap[:, 0, :], sn.ap, [sn], [Sall[(u + 1) % 3]])
                    yield

            def m2(u):
                c, tb = divmod(u, 4)
                u2, u3 = u % 2, u % 3
                cols = slice(tb * 512, (tb + 1) * 512)
                ob = PB[6]
                for ch in range(8):
                    mm(ob.ap[:, ch * 64:(ch + 1) * 64], Sall[u3].ap[:, ch, :], qi[u3].ap[:, ch * 64:(ch + 1) * 64], ch == 0, False, [Sall[u3], qi[u3]], [ob])
                for t4 in range(4):
                    mm(ob.ap[:, t4 * 128:(t4 + 1) * 128], Vb[u3].ap[:, t4, :], ATb[u2].ap[:, t4 * 128:(t4 + 1) * 128], False, t4 == 3, [Vb[u3], ATb[u2]], [ob])
                act(sq.ap, ob.ap, AF.Square, [ob], [sq])
                yield
                ssb = PB[7]
                mm(ssb.ap, ones.ap, sq.ap, True, True, [ones, sq], [ssb])
                act(rsn.ap, ssb.ap, AF.Ln, [ssb, epsT], [rsn], scale=1.0 / 128, bias=epsT.ap[:, 0:1])
                act(rsn.ap, rsn.ap, AF.Exp, [rsn], [rsn], scale=-0.5)
                yield
                tt("dve", rsn.ap, rsn.ap, sgu[u3].ap, ALU.mult, [rsn, sgu[u3]], [rsn])
                stt(yT[c % 4].ap[:, cols], ob.ap, spl.ap[:, 272:273], rsn.ap, ALU.mult, ALU.mult, [ob, spl, rsn], [yT[c % 4]])

            for G in range(2):
                n = 16
                pj_banks[0] = [0, 1]
                for step in range(n + 3):
                    def gen(fn, i):
                        return fn(16 * G + i) if 0 <= i < n else None

                    def merged():
                        gens = [g for g in (gen(m2, step - 3), gen(m1, step - 2), gen(m0, step - 1)) if g is not None]
                        while gens:
                            for g in list(gens):
                                try:
                                    next(g)
                                except StopIteration:
                                    gens.remove(g)
                    if step % 4 == 0:
                        merged()
                        if step < n:
                            mA(16 * G + step)
                    else:
                        if step < n:
                            mA(16 * G + step)
                        merged()
                out_proj(G, s_, False)
            K.barrier()
        with ExitStack() as es:
            memkv["kmT"] = sbt(es, "kmT", [128, 4, MEM], BF16)
            memkv["vm"] = sbt(es, "vm", [128, 2, 512], BF16)
            mem_prep(s_, l)
            mem_heads(es, l, is_last)
            out_proj(2, s_, is_last, next_hT=None if is_last else hT_stages(es, l + 1))
            K.barrier()

    def rotary_tables(s_):
        with ExitStack() as es:
            pi_ = sbt(es, "pos_i", [128, 48], I32)
            pf = sbt(es, "pos_f", [128, 48], F32)
            ang = sbt(es, "ang", [128, 48, 16], F32)
            u = sbt(es, "ru", [128, 48 * 16], F32)
            ni = sbt(es, "rni", [128, 48 * 16], I32)
            nf = sbt(es, "rnf", [128, 48 * 16], F32)
            m = sbt(es, "rm", [128, 48 * 16], F32)
            K.op("sp", lambda e: e.dma_start(out=pi_.ap, in_=pos_d[s_]), (), [pi_], sem=sem_pos)
            cp("dve", pf.ap, pi_.ap, [pi_], [pf])
            tt("dve", ang.ap, bc(pf.ap.unsqueeze(2), [128, 48, 16]), bc(invf.unsqueeze(1), [128, 48, 16]), ALU.mult, [pf, cf], [ang])
            angf = ang.ap.rearrange("p a b -> p (a b)")
            for which, dstT in ((0, S2), (1, COS)):
                dst = COS.ap if which == 1 else S2.ap[:, :, 1, :]
                if which == 1:
                    ts("dve", u.ap, angf, math.pi / 2, None, ALU.add, None, [ang], [u])
                    src_t, src = u, u.ap
                else:
                    src_t, src = ang, angf
                ts("dve", nf.ap, src, 1.0 / (2 * math.pi), None, ALU.mult, None, [src_t], [nf])
                cp("dve", ni.ap, nf.ap, [nf], [ni])
                cp("dve", nf.ap, ni.ap, [ni], [nf])
                stt(m.ap, nf.ap, -2 * math.pi, src, ALU.mult, ALU.add, [nf, src_t], [m])
                ts("dve", nf.ap, m.ap, math.pi, None, ALU.is_gt, None, [m], [nf])
                stt(m.ap, nf.ap, -2 * math.pi, m.ap, ALU.mult, ALU.add, [nf, m], [m])
                ts("dve", nf.ap, m.ap, -math.pi, None, ALU.is_lt, None, [m], [nf])
                stt(m.ap, nf.ap, 2 * math.pi, m.ap, ALU.mult, ALU.add, [nf, m], [m])
                ts("dve", m.ap, m.ap, PI_SAFE, -PI_SAFE, ALU.min, ALU.max, [m], [m])
                act(dst, m.ap.rearrange("p (a b) -> p a b", a=48), AF.Sin, [m], [dstT])
            ts("dve", S2.ap[:, :, 0, :], S2.ap[:, :, 1, :], -1.0, None, ALU.mult, None, [S2], [S2])
            K.barrier()

    try:
        for s_ in range(n_seq):
            rotary_tables(s_)
            for ti in range(NT):
                K.op("sp", lambda e, ti=ti, s_=s_: e.dma_start(out=x_t[ti].ap, in_=x_d[s_, ti * 128:(ti + 1) * 128, :]), (), [x_t[ti]], sem=sem_x[ti])
            chk(1)
            for l in range(n_layers):
                is_last = (l == n_layers - 1)
                load_spl(l)
                chk(2)
                if l % 2 == 0:
                    layer_A(s_, l, is_last)
                else:
                    layer_B(s_, l, is_last)
    except StopBuild:
        K.barrier()
        for ti in range(NT):
            K.op("sp", lambda e, ti=ti: e.dma_start(out=out_d[0, ti * 128:(ti + 1) * 128, :], in_=x_t[ti].ap), [x_t[ti]], (), sem=sem_x[ti])
    K.barrier()
    nc = K.finish()
    return nc, K


_CACHE = {}


def make_in_maps(inputs, n_layers=DEPTH, n_seq=SEQ_PER_CORE, n_cores=N_CORES):
    cbf, cf = const_arrays()
    lbl = np.ascontiguousarray(np.asarray(inputs["lb_logits"], dtype=np.float32).reshape(4, 8, 128).transpose(2, 0, 1)).reshape(128, 32)
    spl = np.stack([small_params(l, inputs) for l in range(DEPTH)], axis=0)
    ng = np.ascontiguousarray(np.asarray(inputs["norm_gain"], dtype=np.float32).reshape(4, 8, 128).transpose(2, 0, 1)).reshape(128, 32)
    wl = [layer_weights(l, inputs) for l in range(DEPTH)]
    x = np.asarray(inputs["x"], dtype=np.float32)
    mem = np.asarray(inputs["mem"], dtype=np.float32)
    pos = np.asarray(inputs["positions"]).astype(np.int32)
    maps = []
    for c in range(n_cores):
        sl = slice(c * n_seq, (c + 1) * n_seq)
        m = {
            "x": np.ascontiguousarray(x[sl]),
            "mem": np.ascontiguousarray(mem[sl]),
            "pos": np.stack([pos_layout(pos[b]) for b in range(c * n_seq, (c + 1) * n_seq)], axis=0),
            "cbf": cbf, "cf": cf, "lbl": lbl, "spl": spl, "ng": ng,
        }
        for l in range(DEPTH):
            m["wl%d" % l] = wl[l]
        maps.append(m)
    return maps


def kernel(**inputs):
    if "nc" not in _CACHE:
        _CACHE["nc"] = build()[0]
    nc = _CACHE["nc"]
    maps = make_in_maps(inputs)
    res = run_bass_kernel_spmd(nc, maps, core_ids=list(range(N_CORES)))
    out = np.concatenate([np.asarray(r["out"]) for r in res.results], axis=0)
    return out.astype(np.float32)
```

```python
import math
from contextlib import ExitStack
import numpy as np
import concourse.bass as bass
import concourse.mybir as mybir
from concourse.bass_utils import run_bass_kernel_spmd

F32 = mybir.dt.float32
BF16 = mybir.dt.bfloat16
I32 = mybir.dt.int32
AF = mybir.ActivationFunctionType
ALU = mybir.AluOpType

SEM_LIMIT = 30000
SAME_ENGINE_SYNC = True

D = 1024
S = 2048
NT = 16
MEM = 256
DEPTH = 4
EPS = 1e-6
ATT = 1.0 / math.sqrt(128.0)
IN_A = 11264
IN_B = 5120
NSPL = 1056
N_CORES = 8
SEQ_PER_CORE = 2
RING_COLS = 3072
PI_SAFE = 3.1415925


class Sem:
    def __init__(self, K, name, step):
        self.K, self.name, self.step = K, name, step
        self.handles = []
        self.count = 0
        self._new()

    def _new(self):
        self.handles.append(self.K.nc.alloc_semaphore(name="%s_e%d" % (self.name, len(self.handles))))
        self.count = 0

    def next(self):
        if self.count + self.step > SEM_LIMIT:
            self._new()
        self.count += self.step
        return (self, len(self.handles) - 1, self.count)


class T:
    def __init__(self, ap, name="", excl=False):
        self.ap = ap
        self.name = name
        self.excl = excl
        self.w = {}
        self.r = {}

    def __getitem__(self, k):
        return self.ap[k]


class Eng:
    def __init__(self, K, name, attr):
        self.K, self.name, self.attr = K, name, attr
        self.sem = Sem(K, "s_" + name, 1)
        self.waited = {}
        self.prog = []


class Kern:
    def __init__(self):
        self.nc = bass.Bass("TRN2", target_bir_lowering=False)
        self.E = {
            "pe": Eng(self, "pe", "tensor"),
            "act": Eng(self, "act", "scalar"),
            "dve": Eng(self, "dve", "vector"),
            "pool": Eng(self, "pool", "gpsimd"),
            "sp": Eng(self, "sp", "sync"),
        }
        self.n_inst = 0
        self.dma_toks = []

    def _wait(self, E, tok):
        sem, ep, val = tok
        key = id(sem)
        cur = E.waited.get(key, (-1, 0))
        if cur[0] > ep or (cur[0] == ep and cur[1] >= val):
            return
        E.waited[key] = (ep, val)
        h = sem.handles[ep]
        E.prog.append(lambda e, h=h, val=val: e.wait_ge(h, val))
        self.n_inst += 1

    def op(self, eng, fn, reads=(), writes=(), sem=None):
        E = self.E[eng]
        deps = []
        if any(t.excl for t in reads):
            writes = list(writes) + [t for t in reads if t.excl and t not in writes]
            reads = [t for t in reads if not t.excl]
        for t in reads:
            deps.extend((tok, True) for tok in t.w.values())
        for t in writes:
            deps.extend((tok, False) for tok in t.w.values())
            deps.extend((tok, False) for tok in t.r.values())
        for tok, raw in deps:
            if tok[0] is E.sem and (eng == "pe" or not SAME_ENGINE_SYNC):
                continue
            self._wait(E, tok)
        Sm = sem if sem is not None else E.sem
        tok = Sm.next()
        h = Sm.handles[tok[1]]
        step = Sm.step
        E.prog.append(lambda e, fn=fn, h=h, step=step: fn(e).then_inc(h, step))
        self.n_inst += 1
        for t in reads:
            t.r[id(Sm)] = tok
        for t in writes:
            t.w = {id(Sm): tok}
            t.r = {}
        if sem is not None:
            self.dma_toks.append(tok)
        return tok

    def wait_tok(self, eng, tok):
        self._wait(self.E[eng], tok)

    def barrier(self):
        toks = []
        for E in self.E.values():
            if E.sem.count > 0 or len(E.sem.handles) > 1:
                toks.append((E.sem, len(E.sem.handles) - 1, E.sem.count))
        last = {}
        for tok in self.dma_toks:
            last[id(tok[0])] = tok
        self.dma_toks = list(last.values())
        toks.extend(self.dma_toks)
        for E in self.E.values():
            for tok in toks:
                if tok[0] is E.sem:
                    continue
                if tok[2] == 0:
                    continue
                self._wait(E, tok)

    def finish(self):
        nc = self.nc
        with nc.Block() as block:
            for name, E in self.E.items():
                if not E.prog:
                    continue
                dec = getattr(block, E.attr)

                def body(e, prog=E.prog):
                    for f in prog:
                        f(e)
                dec(body)
        return nc


def _bundle(Wr, cols):
    return np.ascontiguousarray(Wr[:, :, cols].transpose(1, 0, 2)).reshape(128, -1)


def _r(a, b):
    return list(range(a, b))


def bundle_plan(l):
    plan = []
    is_a = (l % 2 == 0)
    for G in range(3):
        if G == 2:
            for i in range(4):
                plan.append(("kv%d" % i, 8, 256))
        for hh in range(4):
            c = G * 4 + hh
            if G < 2:
                if is_a:
                    for g in range(3):
                        plan.append(("qkv%d_%d" % (g, c), 8, 384))
                    plan.append(("gate_%d" % c, 8, 128))
                else:
                    plan.append(("gate_%d" % c, 8, 128))
                    plan.append(("qfv_%d" % c, 8, 384))
            else:
                if hh % 2 == 0:
                    plan.append(("qm_%d" % (hh // 2), 8, 256))
                plan.append(("gate_%d" % c, 8, 128))
        for nb in range(2):
            plan.append(("wo%d_%d" % (G, nb), 4, 512))
    return plan


def layer_weights(l, inputs):
    j = l // 2
    is_a = (l % 2 == 0)
    w_in = np.asarray(inputs["w_in_a"][j] if is_a else inputs["w_in_b"][j], dtype=np.float32)
    w_out = np.asarray(inputs["w_out_a"][j] if is_a else inputs["w_out_b"][j], dtype=np.float32)
    w_kv = np.asarray(inputs["w_mem_kv"][l], dtype=np.float32)
    Wi = w_in.reshape(8, 128, -1)
    Wkv = w_kv.reshape(8, 128, 1024)
    Wo = w_out.reshape(12, 128, 1024)
    qm0 = 9216 if is_a else 3072
    gt0 = 9728 if is_a else 3584
    parts = []
    for (name, kc, ncols) in bundle_plan(l):
        if name.startswith("kv"):
            i = int(name[2:])
            parts.append(_bundle(Wkv, _r(i * 256, (i + 1) * 256)))
        elif name.startswith("qkv"):
            g, c = name[3:].split("_")
            g, c = int(g), int(c)
            base = g * 1024 + c * 128
            cols = _r(base, base + 128) + _r(3072 + base, 3072 + base + 128) + _r(6144 + base, 6144 + base + 128)
            parts.append(_bundle(Wi, cols))
        elif name.startswith("qfv"):
            c = int(name[4:])
            cols = _r(c * 128, c * 128 + 128) + _r(1024 + c * 128, 1024 + c * 128 + 128) + _r(2048 + c * 128, 2048 + c * 128 + 128)
            parts.append(_bundle(Wi, cols))
        elif name.startswith("gate"):
            c = int(name[5:])
            parts.append(_bundle(Wi, _r(gt0 + c * 128, gt0 + c * 128 + 128)))
        elif name.startswith("qm"):
            i = int(name[3:])
            parts.append(_bundle(Wi, _r(qm0 + i * 256, qm0 + (i + 1) * 256)))
        elif name.startswith("wo"):
            G, nb = name[2:].split("_")
            G, nb = int(G), int(nb)
            parts.append(np.ascontiguousarray(Wo[G * 4:(G + 1) * 4, :, nb * 512:(nb + 1) * 512].transpose(1, 0, 2)).reshape(128, -1))
        else:
            raise ValueError(name)
    return np.ascontiguousarray(np.concatenate(parts, axis=1))


def layer_wtot(l):
    return sum(kc * n for (_, kc, n) in bundle_plan(l))


def small_params(l, inputs):
    j = l // 2
    sp = np.zeros((128, NSPL), np.float32)
    sp[:, 0:8] = np.asarray(inputs["norm_gain"][l]).reshape(8, 128).T
    sp[:, 8:16] = np.asarray(inputs["mem_norm_gain"][l]).reshape(8, 128).T
    sp[:, 16:144] = np.broadcast_to(np.asarray(inputs["mem_q_gain"][l])[None, :], (128, 128))
    sp[:, 144:272] = np.broadcast_to(np.asarray(inputs["mem_k_gain"][l])[None, :], (128, 128))
    if l % 2 == 0:
        qg = np.asarray(inputs["q_gain_a"][j])
        kg = np.asarray(inputs["k_gain_a"][j])
        for g in range(3):
            sp[:, 288 + g * 256:288 + g * 256 + 128] = np.broadcast_to(qg[g][None, :], (128, 128))
            sp[:, 288 + g * 256 + 128:288 + (g + 1) * 256] = np.broadcast_to(kg[g][None, :], (128, 128))
    else:
        sp[:, 272] = np.asarray(inputs["o_gain_b"][j])
    return sp


def const_arrays():
    jj = np.arange(128)[:, None]
    ii = np.arange(128)[None, :]
    cur = (jj <= ii).astype(np.float32)
    prev = (jj >= ii).astype(np.float32)
    bd = ((jj <= ii) & ((jj // 64) == (ii // 64))).astype(np.float32)
    NEG = np.float32(-30000.0)
    prevb = np.where(prev > 0, np.float32(0.0), NEG).astype(np.float32)
    curb = np.where(cur > 0, np.float32(0.0), NEG).astype(np.float32)
    cbf = np.concatenate([np.eye(128, dtype=np.float32), prevb, curb, prevb, curb, curb, curb, curb, curb, bd, bd, bd, bd], axis=1)
    rst = np.ones((128, 512), np.float32)
    rst[:, ::64] = 0.0
    invf = (500000.0 ** (-np.arange(0, 32, 2, dtype=np.float32) / 32.0)).astype(np.float32)
    cf = np.concatenate([rst, np.broadcast_to(invf[None, :], (128, 16))], axis=1).astype(np.float32)
    return np.ascontiguousarray(cbf), np.ascontiguousarray(cf)


def pos_layout(pos):
    out = np.zeros((128, 48), np.int32)
    p = np.arange(128)
    for blk in range(16):
        out[:, blk] = pos[blk * 128 + p]
        c, r = blk // 4, blk % 4
        out[:, 16 + blk] = pos[512 * c + 4 * p + r]
        out[:, 32 + blk] = pos[16 * p + blk]
    return out


class StopBuild(Exception):
    pass


def build(n_layers=DEPTH, n_seq=SEQ_PER_CORE, dbg_stop=None):
    def chk(n):
        if dbg_stop is not None and n == dbg_stop:
            raise StopBuild()
    K = Kern()
    nc = K.nc
    uid = [0]
    gs = ExitStack()

    def dram(name, shape, dt, kind):
        return nc.dram_tensor(name, list(shape), dt, kind=kind).ap()

    x_d = dram("x", [n_seq, S, D], F32, "ExternalInput")
    mem_d = dram("mem", [n_seq, MEM, D], F32, "ExternalInput")
    pos_d = dram("pos", [n_seq, 128, 48], I32, "ExternalInput")
    cbf_d = dram("cbf", [128, 1664], F32, "ExternalInput")
    cf_d = dram("cf", [128, 528], F32, "ExternalInput")
    lbl_d = dram("lbl", [128, 32], F32, "ExternalInput")
    ng_d = dram("ng", [128, 32], F32, "ExternalInput")
    spl_d = dram("spl", [DEPTH, 128, NSPL], F32, "ExternalInput")
    wl_d = [dram("wl%d" % l, [128, layer_wtot(l)], F32, "ExternalInput") for l in range(DEPTH)]
    out_d = dram("out", [n_seq, S, D], F32, "ExternalOutput")

    def sbt(es, name, shape, dt):
        uid[0] += 1
        h = es.enter_context(nc.sbuf_tensor("%s_%d" % (name, uid[0]), list(shape), dt))
        return T(h.ap() if hasattr(h, "ap") else h[:], name)

    x_t = [sbt(gs, "x%d" % i, [128, D], F32) for i in range(NT)]
    hT = sbt(gs, "hT", [128, 8, S], BF16)
    yT = [sbt(gs, "yT%d" % i, [128, S], BF16) for i in range(4)]
    ring = [sbt(gs, "ring%d" % i, [128, RING_COLS], BF16) for i in range(3)]
    ring_sem = [Sem(K, "ring%d" % i, 16) for i in range(3)]
    cbf = sbt(gs, "cbf", [128, 1664], BF16)
    cf = sbt(gs, "cf", [128, 528], F32)
    ones = sbt(gs, "ones", [128, 128], BF16)
    epsT = sbt(gs, "eps", [128, 1], F32)
    spl = sbt(gs, "spl", [128, NSPL], F32)
    lbl = sbt(gs, "lbl", [128, 4, 8], F32)
    ngall = sbt(gs, "ngall", [128, 32], F32)
    lbv = sbt(gs, "lbv", [128, 4, 8], F32)
    COS = sbt(gs, "COS", [128, 48, 16], F32)
    S2 = sbt(gs, "S2", [128, 48, 2, 16], F32)
    memkv = {}
    PB = []
    for i in range(8):
        uid[0] += 1
        h = gs.enter_context(nc.psum_tensor("pb%d" % i, [128, 512], F32))
        PB.append(T(h.ap() if hasattr(h, "ap") else h[:], "pb%d" % i, excl=True))
    ident = cbf.ap[:, 0:128]
    Mpc = cbf.ap[:, 128:640]
    Mcc = cbf.ap[:, 640:1152]
    Mbd = cbf.ap[:, 1152:1664]
    rst = cf.ap[:, 0:512]
    invf = cf.ap[:, 512:528]

    sem_cbf = Sem(K, "dcbf", 16)
    sem_cf = Sem(K, "dcf", 16)
    sem_lbl = Sem(K, "dlbl", 16)
    sem_ng = Sem(K, "dng", 16)
    sem_spl = Sem(K, "dspl", 16)
    sem_pos = Sem(K, "dpos", 16)
    sem_m32 = [Sem(K, "dm32_%d" % i, 16) for i in range(2)]
    sem_x = [Sem(K, "dx%d" % i, 16) for i in range(NT)]

    def bf16v(t):
        return t.ap.bitcast(BF16)

    def mm(out, lhsT, rhs, start, stop, reads, writes):
        K.op("pe", lambda e: e.matmul(out, lhsT=lhsT, rhs=rhs, start=start, stop=stop, skip_group_check=True), reads, writes)

    def tr(out, in_, reads, writes):
        K.op("pe", lambda e: e.transpose(out=out, in_=in_, identity=ident), list(reads) + [cbf], writes)

    def act(out, in_, func, reads, writes, scale=None, bias=None, accum=None):
        kw = {}
        if scale is not None:
            kw["scale"] = scale
        if bias is not None:
            kw["bias"] = bias
        if accum is not None:
            kw["accum_out"] = accum
        K.op("act", lambda e: e.activation(out=out, in_=in_, func=func, **kw), reads, writes)

    def tt(eng, out, in0, in1, op, reads, writes):
        K.op(eng, lambda e: e.tensor_tensor(out=out, in0=in0, in1=in1, op=op), reads, writes)

    def ts(eng, out, in0, s1, s2, op0, op1, reads, writes):
        if op1 is None:
            K.op(eng, lambda e: e.tensor_scalar(out=out, in0=in0, scalar1=s1, scalar2=None, op0=op0), reads, writes)
        else:
            K.op(eng, lambda e: e.tensor_scalar(out=out, in0=in0, scalar1=s1, scalar2=s2, op0=op0, op1=op1), reads, writes)

    def stt(out, in0, scalar, in1, op0, op1, reads, writes):
        K.op("dve", lambda e: e.scalar_tensor_tensor(out=out, in0=in0, scalar=scalar, in1=in1, op0=op0, op1=op1), reads, writes)

    def cp(eng, out, in_, reads, writes):
        if eng == "act":
            act(out, in_, AF.Copy, reads, writes)
        else:
            K.op(eng, lambda e: e.tensor_copy(out=out, in_=in_), reads, writes)

    def recip(out, in_, reads, writes):
        K.op("dve", lambda e: e.reciprocal(out=out, in_=in_), reads, writes)

    def memset(eng, out, val, writes):
        K.op(eng, lambda e: e.memset(out, val), (), writes)

    def bc(ap, shape):
        return ap.broadcast_to(list(shape))

    wplan = []
    for s_ in range(n_seq):
        for l in range(n_layers):
            off = 0
            for (name, kc, ncols) in bundle_plan(l):
                wplan.append((l, name, kc, ncols, off))
                off += kc * ncols
    wstate = {"issued": 0, "used": 0}

    def w_issue_upto(n):
        while wstate["issued"] < min(n, len(wplan)):
            i = wstate["issued"]
            l, name, kc, ncols, off = wplan[i]
            slot = ring[i % 3]
            src = wl_d[l][:, off:off + kc * ncols]
            dst = slot.ap[:, 0:kc * ncols]
            K.op("pool", lambda e, dst=dst, src=src: e.dma_start(out=dst, in_=src), (), [slot], sem=ring_sem[i % 3])
            wstate["issued"] += 1

    def w_get(name, ahead=2):
        i = wstate["used"]
        l, nm, kc, ncols, off = wplan[i]
        assert nm == name, (nm, name)
        w_issue_upto(i + ahead + 1)
        wstate["used"] += 1
        slot = ring[i % 3]
        return slot, slot.ap[:, 0:kc * ncols].rearrange("p (k n) -> p k n", k=kc)

    K.op("pool", lambda e: e.dma_start(out=cbf.ap, in_=cbf_d), (), [cbf], sem=sem_cbf)
    K.op("sp", lambda e: e.dma_start(out=cf.ap, in_=cf_d), (), [cf], sem=sem_cf)
    K.op("sp", lambda e: e.dma_start(out=lbl.ap.rearrange("p a b -> p (a b)"), in_=lbl_d), (), [lbl], sem=sem_lbl)
    K.op("sp", lambda e: e.dma_start(out=ngall.ap, in_=ng_d), (), [ngall], sem=sem_ng)
    memset("dve", ones.ap, 1.0, [ones])
    memset("dve", epsT.ap, EPS, [epsT])
    with ExitStack() as es:
        ex = sbt(es, "lb_ex", [128, 4, 8], F32)
        sm = sbt(es, "lb_sm", [128, 8], F32)
        act(ex.ap, lbl.ap, AF.Exp, [lbl], [ex])
        tt("dve", sm.ap, ex.ap[:, 0, :], ex.ap[:, 1, :], ALU.add, [ex], [sm])
        tt("dve", sm.ap, sm.ap, ex.ap[:, 2, :], ALU.add, [ex, sm], [sm])
        tt("dve", sm.ap, sm.ap, ex.ap[:, 3, :], ALU.add, [ex, sm], [sm])
        recip(sm.ap, sm.ap, [sm], [sm])
        tt("dve", lbv.ap[:, 0, :], ex.ap[:, 1, :], sm.ap, ALU.mult, [ex, sm], [lbv])
        tt("dve", ex.ap[:, 0, :], ex.ap[:, 1, :], ex.ap[:, 2, :], ALU.add, [ex], [ex])
        tt("dve", ex.ap[:, 0, :], ex.ap[:, 0, :], ex.ap[:, 3, :], ALU.add, [ex], [ex])
        tt("dve", lbv.ap[:, 1, :], ex.ap[:, 0, :], sm.ap, ALU.mult, [ex, sm], [lbv])
        ts("dve", lbv.ap[:, 2:4, :], lbv.ap[:, 0:2, :], -1.0, 1.0, ALU.mult, ALU.add, [lbv], [lbv])
        K.barrier()

    pjc = [0]
    pj_banks = [[0, 1]]

    def next_pj():
        pjc[0] += 1
        bl = pj_banks[0]
        return PB[bl[pjc[0] % len(bl)]]

    def rms_rstd(ss, rstd, n, reads_extra=()):
        act(rstd.ap, ss.ap, AF.Sqrt, [ss, epsT], [rstd], scale=1.0 / n, bias=epsT.ap[:, 0:1])
        recip(rstd.ap, rstd.ap, [rstd], [rstd])

    def tok_cols(g, ti):
        if g == 0:
            return slice(ti * 128, (ti + 1) * 128)
        if g == 1:
            c, r = ti // 4, ti % 4
            return slice(512 * c + r, 512 * c + 512, 4)
        return slice(ti, S, 16)

    def load_spl(l):
        K.op("sp", lambda e: e.dma_start(out=spl.ap, in_=spl_d[l]), (), [spl], sem=sem_spl)

    def run_pipeline(n, stages):
        ns = len(stages)
        for step in range(n + ns - 1):
            for k in range(ns - 1, -1, -1):
                i = step - k
                if 0 <= i < n:
                    stages[k](i)

    def hT_stages(es, l):
        junk = sbt(es, "junk", [128, D], BF16)
        NH = 4
        hn = [sbt(es, "hn%d" % i, [128, D], BF16) for i in range(NH)]
        ss = [sbt(es, "hss%d" % i, [128, 1], F32) for i in range(NH)]
        rs = [sbt(es, "hrs%d" % i, [128, 1], F32) for i in range(NH)]
        gain = ngall.ap[:, l * 8:(l + 1) * 8]

        def s0(ti):
            b = ti % NH
            act(junk.ap, x_t[ti].ap, AF.Square, [x_t[ti]], [ss[b], junk], accum=ss[b].ap)

        def s1(ti):
            b = ti % NH
            rms_rstd(ss[b], rs[b], float(D))

        def s2(ti):
            b = ti % NH
            ts("dve", hn[b].ap, x_t[ti].ap, rs[b].ap[:, 0:1], None, ALU.mult, None, [x_t[ti], rs[b]], [hn[b]])

        def s3(ti):
            b = ti % NH
            trb = PB[2 + ti % 2]
            tv = bf16v(trb).rearrange("p (k n) -> p k n", k=8)
            for kc in range(8):
                tr(tv[:, kc, :], hn[b].ap[:, kc * 128:(kc + 1) * 128], [hn[b]], [trb])

        def s4(ti):
            trb = PB[2 + ti % 2]
            tv = bf16v(trb).rearrange("p (k n) -> p k n", k=8)
            tt("dve", hT.ap[:, :, ti * 128:(ti + 1) * 128], tv, bc(gain.unsqueeze(2), [128, 8, 128]), ALU.mult, [trb, ngall], [hT])

        return [s0, s1, s2, s3, s4]

    def build_hT(es, l):
        run_pipeline(NT, hT_stages(es, l))

    def mem_prep(s_, l):
        kmT, vm = memkv["kmT"], memkv["vm"]
        with ExitStack() as es:
            m32 = [sbt(es, "m32_%d" % i, [128, D], F32) for i in range(2)]
            junk = sbt(es, "mjunk", [128, D], BF16)
            mh = [sbt(es, "mh%d" % i, [128, D], BF16) for i in range(2)]
            ss = [sbt(es, "mss%d" % i, [128, 4], F32) for i in range(2)]
            rs = [sbt(es, "mrs%d" % i, [128, 4], F32) for i in range(2)]
            mnT = sbt(es, "mnT", [128, 8, MEM], BF16)
            kn = [sbt(es, "kn%d" % i, [128, 512], F32) for i in range(2)]
            knb = [sbt(es, "knb%d" % i, [128, 512], BF16) for i in range(2)]
            for i in range(2):
                K.op("sp", lambda e, i=i: e.dma_start(out=m32[i].ap, in_=mem_d[s_, i * 128:(i + 1) * 128, :]), (), [m32[i]], sem=sem_m32[i])
            for i in range(2):
                act(junk.ap, m32[i].ap, AF.Square, [m32[i]], [ss[i], junk], accum=ss[i].ap[:, 0:1])
                act(rs[i].ap[:, 0:1], ss[i].ap[:, 0:1], AF.Sqrt, [ss[i], epsT], [rs[i]], scale=1.0 / D, bias=epsT.ap[:, 0:1])
                recip(rs[i].ap[:, 0:1], rs[i].ap[:, 0:1], [rs[i]], [rs[i]])
                ts("dve", mh[i].ap, m32[i].ap, rs[i].ap[:, 0:1], None, ALU.mult, None, [m32[i], rs[i]], [mh[i]])
                trb = PB[2 + i]
                tv = bf16v(trb).rearrange("p (k n) -> p k n", k=8)
                for kc in range(8):
                    tr(tv[:, kc, :], mh[i].ap[:, kc * 128:(kc + 1) * 128], [mh[i]], [trb])
                tt("dve", mnT.ap[:, :, i * 128:(i + 1) * 128], tv, bc(spl.ap[:, 8:16].unsqueeze(2), [128, 8, 128]), ALU.mult, [trb, spl], [mnT])
            pk = [PB[4], PB[5]]
            pv = [PB[6], PB[7]]
            for bi in range(4):
                slot, wv = w_get("kv%d" % bi)
                for i in range(2):
                    dst = (pk if bi < 2 else pv)[i]
                    c0 = (bi % 2) * 256
                    for kc in range(8):
                        mm(dst.ap[:, c0:c0 + 256], mnT.ap[:, kc, i * 128:(i + 1) * 128], wv[:, kc, :], kc == 0, kc == 7, [mnT, slot], [dst])
            for i in range(2):
                for hh in range(4):
                    act(junk.ap[:, 0:128], pk[i].ap[:, hh * 128:(hh + 1) * 128], AF.Square, [pk[i]], [ss[i], junk], accum=ss[i].ap[:, hh:hh + 1])
                rms_rstd(ss[i], rs[i], 128.0)
                tt("dve", kn[i].ap.rearrange("p (h d) -> p h d", h=4), pk[i].ap.rearrange("p (h d) -> p h d", h=4),
                   bc(rs[i].ap.unsqueeze(2), [128, 4, 128]), ALU.mult, [pk[i], rs[i]], [kn[i]])
                tt("pool", knb[i].ap.rearrange("p (h d) -> p h d", h=4), kn[i].ap.rearrange("p (h d) -> p h d", h=4),
                   bc(spl.ap[:, 144:272].unsqueeze(1), [128, 4, 128]), ALU.mult, [kn[i], spl], [knb[i]])
                trb = PB[2 + i]
                tv = bf16v(trb)[:, 0:512].rearrange("p (k n) -> p k n", k=4)
                for hh in range(4):
                    tr(tv[:, hh, :], knb[i].ap[:, hh * 128:(hh + 1) * 128], [knb[i]], [trb])
                cp("act", kmT.ap[:, :, i * 128:(i + 1) * 128], tv, [trb], [kmT])
                cp("act", vm.ap[:, i, :], pv[i].ap, [pv[i]], [vm])
            K.barrier()

    def final_norm(num, den, sgw, ydst, ytile, rden, t1):
        act(rden.ap, den.ap, AF.Ln, [den], [rden])
        act(rden.ap, rden.ap, AF.Exp, [rden], [rden], scale=-1.0)
        tt("dve", rden.ap, rden.ap, sgw, ALU.mult, [rden] + sgw_t[0], [rden])
        tt("dve", ydst, num.ap, rden.ap, ALU.mult, [num, rden], [ytile])

    sgw_t = [[]]

    def mem_heads(es, l, is_last_layer):
        kmT, vm = memkv["kmT"], memkv["vm"]
        NS = 4
        junk2 = [sbt(es, "qjunk%d" % i, [128, 128], BF16) for i in range(2)]
        qmT = [sbt(es, "qmT%d" % i, [128, S], BF16) for i in range(2)]
        sg = [sbt(es, "msg%d" % i, [128, S], BF16) for i in range(2)]
        ss = [sbt(es, "qss%d" % i, [128, 2], F32) for i in range(NS)]
        rs = [sbt(es, "qrs%d" % i, [128, 2], F32) for i in range(NS)]
        qn = [sbt(es, "qn%d" % i, [128, 256], F32) for i in range(NS)]
        qnb = [sbt(es, "qnb%d" % i, [128, 256], BF16) for i in range(NS)]
        PT = [sbt(es, "mPT%d" % i, [128, 512], BF16) for i in range(3)]
        rdens = [sbt(es, "mrden%d" % i, [128, 512], F32) for i in range(2)]
        ptc = [0]
        wcm = [0]
        stc = [0]
        for hp in range(2):
            pjs = {}
            wc_ = {}

            def s0(i):
                if i == 0:
                    wc_["w"] = w_get("qm_%d" % hp)
                slot, wv = wc_["w"]
                pj = next_pj()
                pjs[i] = pj
                for kc in range(8):
                    mm(pj.ap[:, 0:256], hT.ap[:, kc, i * 128:(i + 1) * 128], wv[:, kc, :], kc == 0, kc == 7, [hT, slot], [pj])

            def s1(i):
                b = i % NS
                pj = pjs.pop(i)
                for q in range(2):
                    act(junk2[q].ap, pj.ap[:, q * 128:(q + 1) * 128], AF.Square, [pj], [ss[b], junk2[q]], accum=ss[b].ap[:, q:q + 1])
                tt("dve", qn[b].ap.rearrange("p (h d) -> p h d", h=2), pj.ap[:, 0:256].rearrange("p (h d) -> p h d", h=2),
                   bc(spl.ap[:, 16:144].unsqueeze(1), [128, 2, 128]), ALU.mult, [pj, spl], [qn[b]])

            def s2(i):
                b = i % NS
                rms_rstd(ss[b], rs[b], 128.0)

            def s3(i):
                b = i % NS
                tt("pool", qnb[b].ap.rearrange("p (h d) -> p h d", h=2), qn[b].ap.rearrange("p (h d) -> p h d", h=2),
                   bc(rs[b].ap.unsqueeze(2), [128, 2, 128]), ALU.mult, [qn[b], rs[b]], [qnb[b]])

            def s4(i):
                b = i % NS
                trb = PB[2 + (i // 4) % 2]
                tv = bf16v(trb).rearrange("p (k n) -> p k n", k=8)
                for q in range(2):
                    tr(tv[:, (i % 4) * 2 + q, :], qnb[b].ap[:, q * 128:(q + 1) * 128], [qnb[b]], [trb])

            def s5(i):
                if i % 4 == 3:
                    trb = PB[2 + (i // 4) % 2]
                    t0 = i - 3
                    tv4 = bf16v(trb).rearrange("p (t q n) -> p t q n", t=4, q=2)
                    cp("act", qmT[0].ap[:, t0 * 128:(t0 + 4) * 128].rearrange("p (t n) -> p t n", t=4), tv4[:, :, 0, :], [trb], [qmT[0]])
                    cp("dve", qmT[1].ap[:, t0 * 128:(t0 + 4) * 128].rearrange("p (t n) -> p t n", t=4), tv4[:, :, 1, :], [trb], [qmT[1]])

            pj_banks[0] = [0, 1, 4, 5]
            run_pipeline(NT, [s0, s1, s2, s3, s4, s5])
            pj_banks[0] = [0, 1]
            for q in range(2):
                c = 8 + hp * 2 + q
                slot, wv = w_get("gate_%d" % c)
                for tb in range(4):
                    pj = next_pj()
                    for kc in range(8):
                        mm(pj.ap, wv[:, kc, :], hT.ap[:, kc, tb * 512:(tb + 1) * 512], kc == 0, kc == 7, [hT, slot], [pj])
                    act(sg[q].ap[:, tb * 512:(tb + 1) * 512], pj.ap, AF.Silu, [pj], [sg[q]])
            A = []
            Bq = []
            for q in range(2):
                mh = hp * 2 + q
                for w in range(4):
                    wcm[0] += 1
                    num, den = (PB[6], PB[7]) if wcm[0] % 2 == 0 else (PB[2], PB[3])
                    rden = rdens[wcm[0] % 2]
                    for mt in range(2):
                        box = {}

                        def fa(q=q, mh=mh, w=w, mt=mt, box=box):
                            st = PB[4 + stc[0] % 2]
                            stc[0] += 1
                            mm(st.ap, kmT.ap[:, mh, mt * 128:(mt + 1) * 128], qmT[q].ap[:, w * 512:(w + 1) * 512], True, True, [kmT, qmT[q]], [st])
                            pt = PT[ptc[0] % 3]
                            ptc[0] += 1
                            act(pt.ap, st.ap, AF.Exp, [st], [pt], scale=ATT)
                            box["pt"] = pt

                        def fb(q=q, mh=mh, w=w, mt=mt, box=box, num=num, den=den, rden=rden):
                            pt = box["pt"]
                            mm(num.ap, vm.ap[:, mt, mh * 128:(mh + 1) * 128], pt.ap, mt == 0, mt == 1, [vm, pt], [num])
                            mm(den.ap, ones.ap, pt.ap, mt == 0, mt == 1, [ones, pt], [den])
                            if mt == 1:
                                sgw_t[0] = [sg[q]]
                                final_norm(num, den, sg[q].ap[:, w * 512:(w + 1) * 512], yT[mh].ap[:, w * 512:(w + 1) * 512], yT[mh], rden, None)
                        A.append(fa)
                        Bq.append(fb)
            LAG = 2
            nb = len(A)
            for step in range(nb + LAG):
                if step < nb:
                    A[step]()
                if step - LAG >= 0:
                    Bq[step - LAG]()

    def out_proj(G, s_, is_last, next_hT=None):
        for nb in range(2):
            slot, wv = w_get("wo%d_%d" % (G, nb))
            fuse = next_hT is not None and nb == 1
            ns = len(next_hT) if fuse else 0
            for ti in range(NT + ns if fuse else NT):
                if ti < NT:
                    pj = next_pj()
                    for ci in range(4):
                        mm(pj.ap, yT[ci].ap[:, ti * 128:(ti + 1) * 128], wv[:, ci, :], ci == 0, ci == 3, [yT[ci], slot], [pj])
                    xs = x_t[ti].ap[:, nb * 512:(nb + 1) * 512]
                    tt("dve", xs, pj.ap, xs, ALU.add, [pj, x_t[ti]], [x_t[ti]])
                    if is_last and G == 2 and nb == 1:
                        K.op("sp", lambda e, ti=ti: e.dma_start(out=out_d[s_, ti * 128:(ti + 1) * 128, :], in_=x_t[ti].ap),
                             [x_t[ti]], (), sem=sem_x[ti])
                if fuse:
                    for k in range(ns - 1, -1, -1):
                        i = ti - 1 - k
                        if 0 <= i < NT:
                            next_hT[k](i)

    def layer_A(s_, l, is_last):
        if l == 0:
            with ExitStack() as es:
                build_hT(es, l)
                K.barrier()
        chk(3)
        with ExitStack() as es:
            qT = [sbt(es, "qT%d" % g, [128, S], BF16) for g in range(3)]
            kT = [sbt(es, "kT%d" % g, [128, S], BF16) for g in range(3)]
            V = [sbt(es, "V%d" % g, [128, NT, 128], BF16) for g in range(3)]
            sg = sbt(es, "sg", [128, S], BF16)
            PT2 = sbt(es, "PT2", [128, NT, 128], BF16)
            PT = [sbt(es, "PT%d" % i, [128, 512], BF16) for i in range(3)]
            junk2 = [sbt(es, "ajunk%d" % i, [128, 128], BF16) for i in range(2)]
            NSET = 4
            ss = [sbt(es, "ass%d" % i, [128, 2], F32) for i in range(NSET)]
            rs = [sbt(es, "ars%d" % i, [128, 2], F32) for i in range(NSET)]
            gq = [sbt(es, "gq%d" % i, [128, 256], F32) for i in range(NSET)]
            rB = [sbt(es, "rB%d" % i, [128, 64], F32) for i in range(NSET)]
            qkb = [sbt(es, "qkb%d" % i, [128, 256], BF16) for i in range(NSET)]
            rdens = [sbt(es, "rden%d" % i, [128, 512], F32) for i in range(2)]
            wc = [0]
            ptc = [0]
            stc = [0]
            tic = [0]

            def run_pipeline(n, stages):
                ns = len(stages)
                for step in range(n + ns - 1):
                    for k in range(ns - 1, -1, -1):
                        i = step - k
                        if 0 <= i < n:
                            stages[k](i)

            wcur = {}

            def proj_head(c):
                def geti(i):
                    return i // 16, i % 16, i % NSET

                def s0(i):
                    g, ti, b = geti(i)
                    if ti == 0:
                        wcur[g] = w_get("qkv%d_%d" % (g, c))
                    slot, wv = wcur[g]
                    pj = next_pj()
                    pjs[i] = pj
                    cols = tok_cols(g, ti)
                    for kc in range(8):
                        mm(pj.ap[:, 0:384], hT.ap[:, kc, cols], wv[:, kc, :], kc == 0, kc == 7, [hT, slot], [pj])

                def s1(i):
                    g, ti, b = geti(i)
                    pj = pjs.pop(i)
                    for q in range(2):
                        act(junk2[q].ap, pj.ap[:, q * 128:(q + 1) * 128], AF.Square, [pj], [ss[b], junk2[q]], accum=ss[b].ap[:, q:q + 1])
                    tt("dve", gq[b].ap, pj.ap[:, 0:256], spl.ap[:, 288 + g * 256:288 + (g + 1) * 256], ALU.mult, [pj, spl], [gq[b]])
                    cp("act", V[g].ap[:, ti, :], pj.ap[:, 256:384], [pj], [V[g]])

                def s2(i):
                    g, ti, b = geti(i)
                    rms_rstd(ss[b], rs[b], 128.0)
                    R = gq[b].ap.rearrange("p (a d) -> p a d", a=2)[:, :, 0:32].rearrange("p a (h f) -> p a h f", h=2)
                    Rsw = R[:, :, ::-1, :]
                    cosb = bc(COS.ap[:, g * 16 + ti, :].unsqueeze(1).unsqueeze(1), [128, 2, 2, 16])
                    s2b = bc(S2.ap[:, g * 16 + ti, :, :].unsqueeze(1), [128, 2, 2, 16])
                    B4 = rB[b].ap.rearrange("p (a h f) -> p a h f", a=2, h=2)
                    tt("pool", B4, Rsw, s2b, ALU.mult, [gq[b], S2], [rB[b]])
                    tt("pool", R, R, cosb, ALU.mult, [gq[b], COS], [gq[b]])
                    tt("pool", R, R, B4, ALU.add, [gq[b], rB[b]], [gq[b]])

                def s3(i):
                    g, ti, b = geti(i)
                    tt("dve", qkb[b].ap.rearrange("p (a d) -> p a d", a=2), gq[b].ap.rearrange("p (a d) -> p a d", a=2),
                       bc(rs[b].ap.unsqueeze(2), [128, 2, 128]), ALU.mult, [gq[b], rs[b]], [qkb[b]])

                def s4(i):
                    g, ti, b = geti(i)
                    trb = PB[2 + (i // 4) % 2]
                    tv = bf16v(trb).rearrange("p (k n) -> p k n", k=8)
                    for q in range(2):
                        tr(tv[:, (i % 4) * 2 + q, :], qkb[b].ap[:, q * 128:(q + 1) * 128], [qkb[b]], [trb])

                def s5(i):
                    g, ti, b = geti(i)
                    if i % 4 == 3:
                        trb = PB[2 + (i // 4) % 2]
                        t0 = ti - 3
                        tv4 = bf16v(trb).rearrange("p (t q n) -> p t q n", t=4, q=2)
                        cp("act", qT[g].ap[:, t0 * 128:(t0 + 4) * 128].rearrange("p (t n) -> p t n", t=4), tv4[:, :, 0, :], [trb], [qT[g]])
                        cp("dve", kT[g].ap[:, t0 * 128:(t0 + 4) * 128].rearrange("p (t n) -> p t n", t=4), tv4[:, :, 1, :], [trb], [kT[g]])

                pjs = {}
                run_pipeline(48, [s0, s1, s2, s3, s4, s5])

            def gate(c):
                slot, wv = w_get("gate_%d" % c)
                for tb in range(4):
                    pj = next_pj()
                    for kc in range(8):
                        mm(pj.ap, wv[:, kc, :], hT.ap[:, kc, tb * 512:(tb + 1) * 512], kc == 0, kc == 7, [hT, slot], [pj])
                    act(sg.ap[:, tb * 512:(tb + 1) * 512], pj.ap, AF.Silu, [pj], [sg])

            ST_BANKS = [4, 5, 2, 3]

            def attn(c):
                LAG = 2
                A = []
                Bq = []

                def add_st(jobs, mask, pv_list_fn):
                    box = {}

                    def fa():
                        st = PB[ST_BANKS[stc[0] % 4]]
                        stc[0] += 1
                        lo = min(j[3] for j in jobs) * 128
                        hi = (max(j[3] for j in jobs) + 1) * 128
                        for n_, (g, kb, qb, sl) in enumerate(jobs):
                            mm(st.ap[:, sl * 128:(sl + 1) * 128], kT[g].ap[:, kb * 128:(kb + 1) * 128], qT[g].ap[:, qb * 128:(qb + 1) * 128],
                               n_ == 0, False, [kT[g], qT[g]], [st])
                        mm(st.ap[:, lo:hi], ident, mask[:, lo:hi], False, True, [cbf], [st])
                        pt = PT[ptc[0] % 3]
                        ptc[0] += 1
                        act(pt.ap[:, lo:hi], st.ap[:, lo:hi], AF.Exp, [st], [pt], scale=ATT)
                        box["pt"] = pt

                    def fb():
                        pv_list_fn(box["pt"])
                    A.append(fa)
                    Bq.append(fb)

                for bb in range(4):
                    def fa2(bb=bb):
                        st = PB[ST_BANKS[stc[0] % 4]]
                        stc[0] += 1
                        for j in range(4):
                            r = bb * 4 + j
                            mm(st.ap[:, j * 128:(j + 1) * 128], kT[2].ap[:, r * 128:(r + 1) * 128], qT[2].ap[:, r * 128:(r + 1) * 128],
                               j == 0, False, [kT[2], qT[2]], [st])
                        mm(st.ap, ident, Mcc, False, True, [cbf], [st])
                        dst = PT2.ap[:, bb * 4:(bb + 1) * 4, :].rearrange("p a b -> p (a b)")
                        act(dst, st.ap, AF.Exp, [st], [PT2], scale=ATT)
                    A.append(fa2)
                    Bq.append(None)
                for w in range(4):
                    wc[0] += 1
                    num, den = (PB[6], PB[7]) if wc[0] % 2 == 0 else (PB[0], PB[1])
                    rden = rdens[wc[0] % 2]
                    first = [True]

                    def pv(vt, vap, pt_t, ptap, ocols, num=num, den=den, first=first):
                        mm(num.ap[:, ocols], vap, ptap, first[0], False, [vt, pt_t], [num])
                        mm(den.ap[:, ocols], ones.ap, ptap, first[0], False, [ones, pt_t], [den])
                        first[0] = False
                    for half in range(2):
                        jobs = []
                        for k in range(2):
                            qb = 4 * w + 2 * half + k
                            if qb > 0:
                                jobs.append((0, qb - 1, qb, 2 * k))
                            jobs.append((0, qb, qb, 2 * k + 1))

                        def pvs0(pt, jobs=jobs, pv=pv):
                            for (g, kb, qb, sl) in jobs:
                                pv(V[0], V[0].ap[:, kb, :], pt, pt.ap[:, sl * 128:(sl + 1) * 128], slice((qb % 4) * 128, (qb % 4 + 1) * 128))
                        add_st(jobs, Mpc, pvs0)
                    if w == 0:
                        jl = [([(1, r, r, r) for r in range(4)], Mcc)]
                    else:
                        jl = []
                        for half in range(2):
                            jobs = []
                            for k in range(2):
                                r = 2 * half + k
                                qb = 4 * w + r
                                jobs.append((1, qb - 4, qb, 2 * k))
                                jobs.append((1, qb, qb, 2 * k + 1))
                            jl.append((jobs, Mpc))
                    for (jobs, mask) in jl:
                        def pvs1(pt, jobs=jobs, pv=pv):
                            for (g, kb, qb, sl) in jobs:
                                pv(V[1], V[1].ap[:, kb, :], pt, pt.ap[:, sl * 128:(sl + 1) * 128], slice(qb % 4, 512, 4))
                        add_st(jobs, mask, pvs1)

                    def fin(w=w, pv=pv, num=num, den=den, rden=rden):
                        for r in range(16):
                            pv(V[2], V[2].ap[:, r, :], PT2, PT2.ap[:, r, 32 * w:32 * w + 32], slice(r, 512, 16))
                        sgw_t[0] = [sg]
                        final_norm(num, den, sg.ap[:, w * 512:(w + 1) * 512], yT[c % 4].ap[:, w * 512:(w + 1) * 512], yT[c % 4], rden, None)
                    A.append(None)
                    Bq.append(fin)
                nb = len(A)
                for step in range(nb + LAG):
                    if step < nb and A[step] is not None:
                        A[step]()
                    if step - LAG >= 0 and Bq[step - LAG] is not None:
                        Bq[step - LAG]()

            for G in range(2):
                for hh in range(4):
                    c = G * 4 + hh
                    pj_banks[0] = [0, 1, 4, 5]
                    proj_head(c)
                    chk(6)
                    gate(c)
                    pj_banks[0] = [0, 1]
                    chk(7)
                    attn(c)
                    chk(8)
                out_proj(G, s_, False)
                chk(9)
            K.barrier()
        with ExitStack() as es:
            memkv["kmT"] = sbt(es, "kmT", [128, 4, MEM], BF16)
            memkv["vm"] = sbt(es, "vm", [128, 2, 512], BF16)
            mem_prep(s_, l)
            mem_heads(es, l, is_last)
            out_proj(2, s_, is_last, next_hT=None if is_last else hT_stages(es, l + 1))
            K.barrier()

    def layer_B(s_, l, is_last):
        jb = l // 2
        if l == 0:
            with ExitStack() as es:
                build_hT(es, l)
                K.barrier()
        with ExitStack() as es:
            def mk(name, shape, dt, n=2):
                return [sbt(es, "%s%d" % (name, i), shape, dt) for i in range(n)]
            qfs = mk("qf", [128, 512], F32)
            Fts = mk("Ft", [128, 512], F32)
            Gt = sbt(es, "Gt", [128, 512], F32)
            kk = sbt(es, "kk", [128, 512], F32)
            Dts = mk("Dt", [128, 512], F32)
            Ets = mk("Et", [128, 512], BF16)
            dec = [0]
            Rt = mk("Rt", [128, 8, 4], F32)
            eg = mk("eg", [128, 8], F32, 3)
            qh = mk("qh", [128, 512], BF16)
            qi = mk("qi", [128, 512], BF16, 3)
            kh = [mk("kh%d_" % I, [128, 512], BF16) for I in range(4)]
            kd = mk("kd", [128, 512], BF16)
            kdt = mk("kdt", [128, 4, 128], BF16)
            Vb = mk("Vb", [128, 4, 128], BF16, 3)
            sgu = mk("sgu", [128, 512], BF16, 3)
            ATb = mk("ATb", [128, 512], BF16)
            Sall = mk("Sall", [128, 8, 128], BF16, 3)
            Sf = mk("Sf", [128, 128], F32)
            sq = sbt(es, "osq", [128, 512], BF16)
            Uf = mk("Uf", [128, 512], F32)
            rsn = sbt(es, "orsn", [128, 512], F32)
            sfc = [0]
            wcur = {}
            for I in range(3):
                for j_ in range(2):
                    memset("pool", kh[I][j_].ap, 0.0, [kh[I][j_]])

            def mA(u):
                c, tb = divmod(u, 4)
                u2 = u % 2
                qf, Ft = qfs[u2], Fts[u2]
                if tb == 0:
                    wcur[("g", c)] = w_get("gate_%d" % c)
                    wcur[("q", c)] = w_get("qfv_%d" % c, ahead=1)
                slot_q, wq = wcur[("q", c)]
                cols = slice(tb * 512, (tb + 1) * 512)
                pq = next_pj()
                for kc in range(8):
                    mm(pq.ap, wq[:, kc, 0:128], hT.ap[:, kc, cols], kc == 0, kc == 7, [hT, slot_q], [pq])
                cp("act", qf.ap, pq.ap, [pq], [qf])
                pf = next_pj()
                for kc in range(8):
                    mm(pf.ap, wq[:, kc, 128:256], hT.ap[:, kc, cols], kc == 0, kc == 7, [hT, slot_q], [pf])
                act(Ft.ap, pf.ap, AF.Sigmoid, [pf], [Ft])

            def m0(u):
                c, tb = divmod(u, 4)
                u2, u3 = u % 2, u % 3
                qf, Ft = qfs[u2], Fts[u2]
                slot_g, wg = wcur[("g", c)]
                slot_q, wq = wcur[("q", c)]
                lb_ap = lbv.ap[:, jb, c:c + 1]
                oml_ap = lbv.ap[:, 2 + jb, c:c + 1]
                cols = slice(tb * 512, (tb + 1) * 512)
                ts("dve", Ft.ap, Ft.ap, oml_ap, lb_ap, ALU.mult, ALU.add, [Ft, lbv], [Ft])
                ts("pool", kk.ap, Ft.ap, -1.0, 1.0, ALU.mult, ALU.add, [Ft], [kk])
                act(Ft.ap, Ft.ap, AF.Ln, [Ft], [Ft])
                K.op("dve", lambda e: e.tensor_tensor_scan(out=Gt.ap, data0=rst, data1=Ft.ap, initial=0.0, op0=ALU.mult, op1=ALU.add),
                     [Ft, cf], [Gt])
                G4 = Gt.ap.rearrange("p (c i t) -> p c i t", c=8, i=4)
                G3 = Gt.ap.rearrange("p (c t) -> p c t", c=8)
                R_ = Rt[u2]
                memset("pool", R_.ap[:, :, 0:1], 0.0, [R_])
                cp("pool", R_.ap[:, :, 1:4], G4[:, :, 0:3, 15], [Gt], [R_])
                yield
                trip = []

                def add_trip(sub_fn, exp_src, mult_fn):
                    trip.append((sub_fn, exp_src, mult_fn))

                add_trip(None, None, lambda Et: tt("dve", qi[u3].ap, qf.ap, Et.ap, ALU.mult, [qf, Et], [qi[u3]]))
                add_trip(lambda Dt: tt("pool", Dt.ap.rearrange("p (c i t) -> p c i t", c=8, i=4), G4, bc(R_.ap.unsqueeze(3), [128, 8, 4, 16]), ALU.subtract, [Gt, R_], [Dt]),
                         True, lambda Et: tt("dve", qh[u2].ap, qf.ap, Et.ap, ALU.mult, [qf, Et], [qh[u2]]))
                kk3 = kk.ap.rearrange("p (c t) -> p c t", c=8)
                for I in range(4):
                    se = "dve" if I % 2 == 0 else "pool"
                    wI = 16 * (I + 1)
                    add_trip(lambda Dt, I=I, se=se, wI=wI: tt(se, Dt.ap.rearrange("p (c t) -> p c t", c=8)[:, :, 0:wI], bc(R_.ap[:, :, I:I + 1], [128, 8, wI]), G3[:, :, 0:wI], ALU.subtract, [Gt, R_], [Dt]),
                             wI, lambda Et, I=I, wI=wI: stt(kh[I][u2].ap.rearrange("p (c t) -> p c t", c=8)[:, :, 0:wI], Et.ap.rearrange("p (c t) -> p c t", c=8)[:, :, 0:wI], 1e30, kk3[:, :, 0:wI], ALU.min, ALU.mult, [Et, kk], [kh[I][u2]]))
                add_trip(lambda Dt: tt("dve", Dt.ap.rearrange("p (c t) -> p c t", c=8), bc(G3[:, :, 63:64], [128, 8, 64]), G3, ALU.subtract, [Gt], [Dt]),
                         True, lambda Et: tt("dve", kd[u2].ap, Et.ap, kk.ap, ALU.mult, [Et, kk], [kd[u2]]))
                nt_ = len(trip)
                base = dec[0]
                dec[0] += nt_
                for k in range(nt_ + 2):
                    if k < nt_ and trip[k][0] is not None:
                        trip[k][0](Dts[(base + k) % 2])
                    if 0 <= k - 1 < nt_:
                        j = k - 1
                        src = Dts[(base + j) % 2] if trip[j][1] else Gt
                        ed = Ets[(base + j) % 2]
                        if trip[j][1] is True or trip[j][1] is None or trip[j][1] == 64:
                            act(ed.ap, src.ap, AF.Exp, [src], [ed])
                        else:
                            wj = trip[j][1]
                            act(ed.ap.rearrange("p (c t) -> p c t", c=8)[:, :, 0:wj], src.ap.rearrange("p (c t) -> p c t", c=8)[:, :, 0:wj], AF.Exp, [src], [ed])
                    if 0 <= k - 2 < nt_:
                        j = k - 2
                        trip[j][2](Ets[(base + j) % 2])
                    yield
                act(eg[u3].ap, G3[:, :, 63], AF.Exp, [Gt], [eg[u3]])
                yield
                pvv = next_pj()
                for t4 in range(4):
                    ti = tb * 4 + t4
                    for kc in range(8):
                        mm(pvv.ap[:, t4 * 128:(t4 + 1) * 128], hT.ap[:, kc, ti * 128:(ti + 1) * 128], wq[:, kc, 256:384], kc == 0, kc == 7, [hT, slot_q], [pvv])
                cp("act", Vb[u3].ap.rearrange("p a b -> p (a b)"), pvv.ap, [pvv], [Vb[u3]])
                pg = next_pj()
                for kc in range(8):
                    mm(pg.ap, wg[:, kc, :], hT.ap[:, kc, cols], kc == 0, kc == 7, [hT, slot_g], [pg])
                act(sgu[u3].ap, pg.ap, AF.Sigmoid, [pg], [sgu[u3]])
                tt("dve", sgu[u3].ap, pg.ap, sgu[u3].ap, ALU.mult, [pg, sgu[u3]], [sgu[u3]])

            def m1(u):
                c, tb = divmod(u, 4)
                u2, u3 = u % 2, u % 3
                trb = PB[2]
                tv = bf16v(trb)[:, 0:512].rearrange("p (k n) -> p k n", k=4)
                for t4 in range(4):
                    tr(tv[:, t4, :], kd[u2].ap[:, t4 * 128:(t4 + 1) * 128], [kd[u2]], [trb])
                cp("act", kdt[u2].ap, tv, [trb], [kdt[u2]])
                yield
                at = PB[3]
                fst = True
                for t4 in range(4):
                    for a in range(2):
                        for I in range(4):
                            c0 = t4 * 128 + a * 64 + 16 * I
                            mm(at.ap[:, c0:c0 + 16], kh[I][u2].ap[:, t4 * 128:(t4 + 1) * 128], qh[u2].ap[:, c0:c0 + 16], fst, False, [kh[I][u2], qh[u2]], [at])
                            fst = False
                tt("dve", ATb[u2].ap, at.ap, Mbd, ALU.mult, [at, cbf], [ATb[u2]])
                yield
                for ch in range(8):
                    t4, a = ch // 2, ch % 2
                    ub = PB[4 + a]
                    mm(ub.ap[:, t4 * 128:(t4 + 1) * 128], kdt[u2].ap[64 * a:64 * a + 64, t4, :], Vb[u3].ap[64 * a:64 * a + 64, t4, :],
                       t4 == 0, t4 == 3, [kdt[u2], Vb[u3]], [ub])
                if tb == 0:
                    memset("pool", Sall[u3].ap[:, 0, :], 0.0, [Sall[u3]])
                    memset("dve", Sf[sfc[0] % 2].ap, 0.0, [Sf[sfc[0] % 2]])
                cp("act", Uf[0].ap, PB[4].ap, [PB[4]], [Uf[0]])
                cp("act", Uf[1].ap, PB[5].ap, [PB[5]], [Uf[1]])
                yield
                for ch in range(8):
                    so = Sf[sfc[0] % 2]
                    sn = Sf[(sfc[0] + 1) % 2]
                    sfc[0] += 1
                    stt(sn.ap, so.ap, eg[u3].ap[:, ch:ch + 1], Uf[ch % 2].ap[:, (ch // 2) * 128:(ch // 2 + 1) * 128], ALU.mult, ALU.add, [so, eg[u3], Uf[ch % 2]], [sn])
                    if ch < 7:
                        cp("act", Sall[u3].ap[:, ch + 1, :], sn.ap, [sn], [Sall[u3]])
                    elif tb < 3:
                        cp("act", Sall[(u + 1) % 3].ap[:, 0, :], sn.ap, [sn], [Sall[(u + 1) % 3]])
                    yield

            def m2(u):
                c, tb = divmod(u, 4)
                u2, u3 = u % 2, u % 3
                cols = slice(tb * 512, (tb + 1) * 512)
                ob = PB[6]
                for ch in range(8):
                    mm(ob.ap[:, ch * 64:(ch + 1) * 64], Sall[u3].ap[:, ch, :], qi[u3].ap[:, ch * 64:(ch + 1) * 64], ch == 0, False, [Sall[u3], qi[u3]], [ob])
                for t4 in range(4):
                    mm(ob.ap[:, t4 * 128:(t4 + 1) * 128], Vb[u3].ap[:, t4, :], ATb[u2].ap[:, t4 * 128:(t4 + 1) * 128], False, t4 == 3, [Vb[u3], ATb[u2]], [ob])
                act(sq.ap, ob.ap, AF.Square, [ob], [sq])
                yield
                ssb = PB[7]
                mm(ssb.ap, ones.ap, sq.ap, True, True, [ones, sq], [ssb])
                act(rsn.ap, ssb.ap, AF.Ln, [ssb, epsT], [rsn], scale=1.0 / 128, bias=epsT.ap[:, 0:1])
                act(rsn.ap, rsn.ap, AF.Exp, [rsn], [rsn], scale=-0.5)
                yield
                tt("pool", rsn.ap, rsn.ap, sgu[u3].ap, ALU.mult, [rsn, sgu[u3]], [rsn])
                stt(yT[c % 4].ap[:, cols], ob.ap, spl.ap[:, 272:273], rsn.ap, ALU.mult, ALU.mult, [ob, spl, rsn], [yT[c % 4]])

            for G in range(2):
                n = 16
                pj_banks[0] = [0, 1]
                for step in range(n + 3):
                    def gen(fn, i):
                        return fn(16 * G + i) if 0 <= i < n else None

                    def merged():
                        gens = [g for g in (gen(m2, step - 3), gen(m1, step - 2), gen(m0, step - 1)) if g is not None]
                        while gens:
                            for g in list(gens):
                                try:
                                    next(g)
                                except StopIteration:
                                    gens.remove(g)
                    if step % 4 == 0:
                        merged()
                        if step < n:
                            mA(16 * G + step)
                    else:
                        if step < n:
                            mA(16 * G + step)
                        merged()
                out_proj(G, s_, False)
            K.barrier()
        with ExitStack() as es:
            memkv["kmT"] = sbt(es, "kmT", [128, 4, MEM], BF16)
            memkv["vm"] = sbt(es, "vm", [128, 2, 512], BF16)
            mem_prep(s_, l)
            mem_heads(es, l, is_last)
            out_proj(2, s_, is_last, next_hT=None if is_last else hT_stages(es, l + 1))
            K.barrier()

    def rotary_tables(s_):
        with ExitStack() as es:
            pi_ = sbt(es, "pos_i", [128, 48], I32)
            pf = sbt(es, "pos_f", [128, 48], F32)
            ang = sbt(es, "ang", [128, 48, 16], F32)
            u = sbt(es, "ru", [128, 48 * 16], F32)
            ni = sbt(es, "rni", [128, 48 * 16], I32)
            nf = sbt(es, "rnf", [128, 48 * 16], F32)
            m = sbt(es, "rm", [128, 48 * 16], F32)
            K.op("sp", lambda e: e.dma_start(out=pi_.ap, in_=pos_d[s_]), (), [pi_], sem=sem_pos)
            cp("dve", pf.ap, pi_.ap, [pi_], [pf])
            tt("dve", ang.ap, bc(pf.ap.unsqueeze(2), [128, 48, 16]), bc(invf.unsqueeze(1), [128, 48, 16]), ALU.mult, [pf, cf], [ang])
            angf = ang.ap.rearrange("p a b -> p (a b)")
            for which, dstT in ((0, S2), (1, COS)):
                dst = COS.ap if which == 1 else S2.ap[:, :, 1, :]
                if which == 1:
                    ts("dve", u.ap, angf, math.pi / 2, None, ALU.add, None, [ang], [u])
                    src_t, src = u, u.ap
                else:
                    src_t, src = ang, angf
                ts("dve", nf.ap, src, 1.0 / (2 * math.pi), None, ALU.mult, None, [src_t], [nf])
                cp("dve", ni.ap, nf.ap, [nf], [ni])
                cp("dve", nf.ap, ni.ap, [ni], [nf])
                stt(m.ap, nf.ap, -2 * math.pi, src, ALU.mult, ALU.add, [nf, src_t], [m])
                ts("dve", nf.ap, m.ap, math.pi, None, ALU.is_gt, None, [m], [nf])
                stt(m.ap, nf.ap, -2 * math.pi, m.ap, ALU.mult, ALU.add, [nf, m], [m])
                ts("dve", nf.ap, m.ap, -math.pi, None, ALU.is_lt, None, [m], [nf])
                stt(m.ap, nf.ap, 2 * math.pi, m.ap, ALU.mult, ALU.add, [nf, m], [m])
                ts("dve", m.ap, m.ap, PI_SAFE, -PI_SAFE, ALU.min, ALU.max, [m], [m])
                act(dst, m.ap.rearrange("p (a b) -> p a b", a=48), AF.Sin, [m], [dstT])
            ts("dve", S2.ap[:, :, 0, :], S2.ap[:, :, 1, :], -1.0, None, ALU.mult, None, [S2], [S2])
            K.barrier()

    try:
        for s_ in range(n_seq):
            rotary_tables(s_)
            for ti in range(NT):
                K.op("sp", lambda e, ti=ti, s_=s_: e.dma_start(out=x_t[ti].ap, in_=x_d[s_, ti * 128:(ti + 1) * 128, :]), (), [x_t[ti]], sem=sem_x[ti])
            chk(1)
            for l in range(n_layers):
                is_last = (l == n_layers - 1)
                load_spl(l)
                chk(2)
                if l % 2 == 0:
                    layer_A(s_, l, is_last)
                else:
                    layer_B(s_, l, is_last)
    except StopBuild:
        K.barrier()
        for ti in range(NT):
            K.op("sp", lambda e, ti=ti: e.dma_start(out=out_d[0, ti * 128:(ti + 1) * 128, :], in_=x_t[ti].ap), [x_t[ti]], (), sem=sem_x[ti])
    K.barrier()
    nc = K.finish()
    return nc, K


_CACHE = {}


def make_in_maps(inputs, n_layers=DEPTH, n_seq=SEQ_PER_CORE, n_cores=N_CORES):
    cbf, cf = const_arrays()
    lbl = np.ascontiguousarray(np.asarray(inputs["lb_logits"], dtype=np.float32).reshape(4, 8, 128).transpose(2, 0, 1)).reshape(128, 32)
    spl = np.stack([small_params(l, inputs) for l in range(DEPTH)], axis=0)
    ng = np.ascontiguousarray(np.asarray(inputs["norm_gain"], dtype=np.float32).reshape(4, 8, 128).transpose(2, 0, 1)).reshape(128, 32)
    wl = [layer_weights(l, inputs) for l in range(DEPTH)]
    x = np.asarray(inputs["x"], dtype=np.float32)
    mem = np.asarray(inputs["mem"], dtype=np.float32)
    pos = np.asarray(inputs["positions"]).astype(np.int32)
    maps = []
    for c in range(n_cores):
        sl = slice(c * n_seq, (c + 1) * n_seq)
        m = {
            "x": np.ascontiguousarray(x[sl]),
            "mem": np.ascontiguousarray(mem[sl]),
            "pos": np.stack([pos_layout(pos[b]) for b in range(c * n_seq, (c + 1) * n_seq)], axis=0),
            "cbf": cbf, "cf": cf, "lbl": lbl, "spl": spl, "ng": ng,
        }
        for l in range(DEPTH):
            m["wl%d" % l] = wl[l]
        maps.append(m)
    return maps


def kernel(**inputs):
    if "nc" not in _CACHE:
        _CACHE["nc"] = build()[0]
    nc = _CACHE["nc"]
    maps = make_in_maps(inputs)
    res = run_bass_kernel_spmd(nc, maps, core_ids=list(range(N_CORES)))
    out = np.concatenate([np.asarray(r["out"]) for r in res.results], axis=0)
    return out.astype(np.float32)
```

```python
import math
from contextlib import ExitStack
import numpy as np
import concourse.bass as bass
import concourse.mybir as mybir
from concourse.bass_utils import run_bass_kernel_spmd

F32 = mybir.dt.float32
BF16 = mybir.dt.bfloat16
I32 = mybir.dt.int32
AF = mybir.ActivationFunctionType
ALU = mybir.AluOpType

SEM_LIMIT = 30000
SAME_ENGINE_SYNC = True

D = 1024
S = 2048
NT = 16
MEM = 256
DEPTH = 4
EPS = 1e-6
ATT = 1.0 / math.sqrt(128.0)
IN_A = 11264
IN_B = 5120
NSPL = 1056
N_CORES = 8
SEQ_PER_CORE = 2
RING_COLS = 3072
PI_SAFE = 3.1415925


class Sem:
    def __init__(self, K, name, step):
        self.K, self.name, self.step = K, name, step
        self.handles = []
        self.count = 0
        self._new()

    def _new(self):
        self.handles.append(self.K.nc.alloc_semaphore(name="%s_e%d" % (self.name, len(self.handles))))
        self.count = 0

    def next(self):
        if self.count + self.step > SEM_LIMIT:
            self._new()
        self.count += self.step
        return (self, len(self.handles) - 1, self.count)


class T:
    def __init__(self, ap, name="", excl=False):
        self.ap = ap
        self.name = name
        self.excl = excl
        self.w = {}
        self.r = {}

    def __getitem__(self, k):
        return self.ap[k]


class Eng:
    def __init__(self, K, name, attr):
        self.K, self.name, self.attr = K, name, attr
        self.sem = Sem(K, "s_" + name, 1)
        self.waited = {}
        self.prog = []


class Kern:
    def __init__(self):
        self.nc = bass.Bass("TRN2", target_bir_lowering=False)
        self.E = {
            "pe": Eng(self, "pe", "tensor"),
            "act": Eng(self, "act", "scalar"),
            "dve": Eng(self, "dve", "vector"),
            "pool": Eng(self, "pool", "gpsimd"),
            "sp": Eng(self, "sp", "sync"),
        }
        self.n_inst = 0
        self.dma_toks = []

    def _wait(self, E, tok):
        sem, ep, val = tok
        key = id(sem)
        cur = E.waited.get(key, (-1, 0))
        if cur[0] > ep or (cur[0] == ep and cur[1] >= val):
            return
        E.waited[key] = (ep, val)
        h = sem.handles[ep]
        E.prog.append(lambda e, h=h, val=val: e.wait_ge(h, val))
        self.n_inst += 1

    def op(self, eng, fn, reads=(), writes=(), sem=None):
        E = self.E[eng]
        deps = []
        if any(t.excl for t in reads):
            writes = list(writes) + [t for t in reads if t.excl and t not in writes]
            reads = [t for t in reads if not t.excl]
        for t in reads:
            deps.extend((tok, True) for tok in t.w.values())
        for t in writes:
            deps.extend((tok, False) for tok in t.w.values())
            deps.extend((tok, False) for tok in t.r.values())
        for tok, raw in deps:
            if tok[0] is E.sem and (eng == "pe" or not SAME_ENGINE_SYNC):
                continue
            self._wait(E, tok)
        Sm = sem if sem is not None else E.sem
        tok = Sm.next()
        h = Sm.handles[tok[1]]
        step = Sm.step
        E.prog.append(lambda e, fn=fn, h=h, step=step: fn(e).then_inc(h, step))
        self.n_inst += 1
        for t in reads:
            t.r[id(Sm)] = tok
        for t in writes:
            t.w = {id(Sm): tok}
            t.r = {}
        if sem is not None:
            self.dma_toks.append(tok)
        return tok

    def wait_tok(self, eng, tok):
        self._wait(self.E[eng], tok)

    def barrier(self):
        toks = []
        for E in self.E.values():
            if E.sem.count > 0 or len(E.sem.handles) > 1:
                toks.append((E.sem, len(E.sem.handles) - 1, E.sem.count))
        last = {}
        for tok in self.dma_toks:
            last[id(tok[0])] = tok
        self.dma_toks = list(last.values())
        toks.extend(self.dma_toks)
        for E in self.E.values():
            for tok in toks:
                if tok[0] is E.sem:
                    continue
                if tok[2] == 0:
                    continue
                self._wait(E, tok)

    def finish(self):
        nc = self.nc
        with nc.Block() as block:
            for name, E in self.E.items():
                if not E.prog:
                    continue
                dec = getattr(block, E.attr)

                def body(e, prog=E.prog):
                    for f in prog:
                        f(e)
                dec(body)
        return nc


def _bundle(Wr, cols):
    return np.ascontiguousarray(Wr[:, :, cols].transpose(1, 0, 2)).reshape(128, -1)


def _r(a, b):
    return list(range(a, b))


def bundle_plan(l):
    plan = []
    is_a = (l % 2 == 0)
    for G in range(3):
        if G == 2:
            for i in range(4):
                plan.append(("kv%d" % i, 8, 256))
        for hh in range(4):
            c = G * 4 + hh
            if G < 2:
                if is_a:
                    for g in range(3):
                        plan.append(("qkv%d_%d" % (g, c), 8, 384))
                    plan.append(("gate_%d" % c, 8, 128))
                else:
                    plan.append(("gate_%d" % c, 8, 128))
                    plan.append(("qfv_%d" % c, 8, 384))
            else:
                if hh % 2 == 0:
                    plan.append(("qm_%d" % (hh // 2), 8, 256))
                plan.append(("gate_%d" % c, 8, 128))
        for nb in range(2):
            plan.append(("wo%d_%d" % (G, nb), 4, 512))
    return plan


def layer_weights(l, inputs):
    j = l // 2
    is_a = (l % 2 == 0)
    w_in = np.asarray(inputs["w_in_a"][j] if is_a else inputs["w_in_b"][j], dtype=np.float32)
    w_out = np.asarray(inputs["w_out_a"][j] if is_a else inputs["w_out_b"][j], dtype=np.float32)
    w_kv = np.asarray(inputs["w_mem_kv"][l], dtype=np.float32)
    Wi = w_in.reshape(8, 128, -1)
    Wkv = w_kv.reshape(8, 128, 1024)
    Wo = w_out.reshape(12, 128, 1024)
    qm0 = 9216 if is_a else 3072
    gt0 = 9728 if is_a else 3584
    parts = []
    for (name, kc, ncols) in bundle_plan(l):
        if name.startswith("kv"):
            i = int(name[2:])
            parts.append(_bundle(Wkv, _r(i * 256, (i + 1) * 256)))
        elif name.startswith("qkv"):
            g, c = name[3:].split("_")
            g, c = int(g), int(c)
            base = g * 1024 + c * 128
            cols = _r(base, base + 128) + _r(3072 + base, 3072 + base + 128) + _r(6144 + base, 6144 + base + 128)
            parts.append(_bundle(Wi, cols))
        elif name.startswith("qfv"):
            c = int(name[4:])
            cols = _r(c * 128, c * 128 + 128) + _r(1024 + c * 128, 1024 + c * 128 + 128) + _r(2048 + c * 128, 2048 + c * 128 + 128)
            parts.append(_bundle(Wi, cols))
        elif name.startswith("gate"):
            c = int(name[5:])
            parts.append(_bundle(Wi, _r(gt0 + c * 128, gt0 + c * 128 + 128)))
        elif name.startswith("qm"):
            i = int(name[3:])
            parts.append(_bundle(Wi, _r(qm0 + i * 256, qm0 + (i + 1) * 256)))
        elif name.startswith("wo"):
            G, nb = name[2:].split("_")
            G, nb = int(G), int(nb)
            parts.append(np.ascontiguousarray(Wo[G * 4:(G + 1) * 4, :, nb * 512:(nb + 1) * 512].transpose(1, 0, 2)).reshape(128, -1))
        else:
            raise ValueError(name)
    return np.ascontiguousarray(np.concatenate(parts, axis=1))


def layer_wtot(l):
    return sum(kc * n for (_, kc, n) in bundle_plan(l))


def small_params(l, inputs):
    j = l // 2
    sp = np.zeros((128, NSPL), np.float32)
    sp[:, 0:8] = np.asarray(inputs["norm_gain"][l]).reshape(8, 128).T
    sp[:, 8:16] = np.asarray(inputs["mem_norm_gain"][l]).reshape(8, 128).T
    sp[:, 16:144] = np.broadcast_to(np.asarray(inputs["mem_q_gain"][l])[None, :], (128, 128))
    sp[:, 144:272] = np.broadcast_to(np.asarray(inputs["mem_k_gain"][l])[None, :], (128, 128))
    if l % 2 == 0:
        qg = np.asarray(inputs["q_gain_a"][j])
        kg = np.asarray(inputs["k_gain_a"][j])
        for g in range(3):
            sp[:, 288 + g * 256:288 + g * 256 + 128] = np.broadcast_to(qg[g][None, :], (128, 128))
            sp[:, 288 + g * 256 + 128:288 + (g + 1) * 256] = np.broadcast_to(kg[g][None, :], (128, 128))
    else:
        sp[:, 272] = np.asarray(inputs["o_gain_b"][j])
    return sp


def const_arrays():
    jj = np.arange(128)[:, None]
    ii = np.arange(128)[None, :]
    cur = (jj <= ii).astype(np.float32)
    prev = (jj >= ii).astype(np.float32)
    bd = ((jj <= ii) & ((jj // 64) == (ii // 64))).astype(np.float32)
    NEG = np.float32(-30000.0)
    prevb = np.where(prev > 0, np.float32(0.0), NEG).astype(np.float32)
    curb = np.where(cur > 0, np.float32(0.0), NEG).astype(np.float32)
    cbf = np.concatenate([np.eye(128, dtype=np.float32), prevb, curb, prevb, curb, curb, curb, curb, curb, bd, bd, bd, bd], axis=1)
    rst = np.ones((128, 512), np.float32)
    rst[:, ::64] = 0.0
    invf = (500000.0 ** (-np.arange(0, 32, 2, dtype=np.float32) / 32.0)).astype(np.float32)
    cf = np.concatenate([rst, np.broadcast_to(invf[None, :], (128, 16))], axis=1).astype(np.float32)
    return np.ascontiguousarray(cbf), np.ascontiguousarray(cf)


def pos_layout(pos):
    out = np.zeros((128, 48), np.int32)
    p = np.arange(128)
    for blk in range(16):
        out[:, blk] = pos[blk * 128 + p]
        c, r = blk // 4, blk % 4
        out[:, 16 + blk] = pos[512 * c + 4 * p + r]
        out[:, 32 + blk] = pos[16 * p + blk]
    return out


class StopBuild(Exception):
    pass


def build(n_layers=DEPTH, n_seq=SEQ_PER_CORE, dbg_stop=None):
    def chk(n):
        if dbg_stop is not None and n == dbg_stop:
            raise StopBuild()
    K = Kern()
    nc = K.nc
    uid = [0]
    gs = ExitStack()

    def dram(name, shape, dt, kind):
        return nc.dram_tensor(name, list(shape), dt, kind=kind).ap()

    x_d = dram("x", [n_seq, S, D], F32, "ExternalInput")
    mem_d = dram("mem", [n_seq, MEM, D], F32, "ExternalInput")
    pos_d = dram("pos", [n_seq, 128, 48], I32, "ExternalInput")
    cbf_d = dram("cbf", [128, 1664], F32, "ExternalInput")
    cf_d = dram("cf", [128, 528], F32, "ExternalInput")
    lbl_d = dram("lbl", [128, 32], F32, "ExternalInput")
    spl_d = dram("spl", [DEPTH, 128, NSPL], F32, "ExternalInput")
    wl_d = [dram("wl%d" % l, [128, layer_wtot(l)], F32, "ExternalInput") for l in range(DEPTH)]
    out_d = dram("out", [n_seq, S, D], F32, "ExternalOutput")

    def sbt(es, name, shape, dt):
        uid[0] += 1
        h = es.enter_context(nc.sbuf_tensor("%s_%d" % (name, uid[0]), list(shape), dt))
        return T(h.ap() if hasattr(h, "ap") else h[:], name)

    x_t = [sbt(gs, "x%d" % i, [128, D], F32) for i in range(NT)]
    hT = sbt(gs, "hT", [128, 8, S], BF16)
    yT = [sbt(gs, "yT%d" % i, [128, S], BF16) for i in range(4)]
    ring = [sbt(gs, "ring%d" % i, [128, RING_COLS], BF16) for i in range(3)]
    ring_sem = [Sem(K, "ring%d" % i, 16) for i in range(3)]
    cbf = sbt(gs, "cbf", [128, 1664], BF16)
    cf = sbt(gs, "cf", [128, 528], F32)
    ones = sbt(gs, "ones", [128, 128], BF16)
    epsT = sbt(gs, "eps", [128, 1], F32)
    spl = sbt(gs, "spl", [128, NSPL], F32)
    lbl = sbt(gs, "lbl", [128, 4, 8], F32)
    lbv = sbt(gs, "lbv", [128, 4, 8], F32)
    COS = sbt(gs, "COS", [128, 48, 16], F32)
    S2 = sbt(gs, "S2", [128, 48, 2, 16], F32)
    memkv = {}
    PB = []
    for i in range(8):
        uid[0] += 1
        h = gs.enter_context(nc.psum_tensor("pb%d" % i, [128, 512], F32))
        PB.append(T(h.ap() if hasattr(h, "ap") else h[:], "pb%d" % i, excl=True))
    ident = cbf.ap[:, 0:128]
    Mpc = cbf.ap[:, 128:640]
    Mcc = cbf.ap[:, 640:1152]
    Mbd = cbf.ap[:, 1152:1664]
    rst = cf.ap[:, 0:512]
    invf = cf.ap[:, 512:528]

    sem_cbf = Sem(K, "dcbf", 16)
    sem_cf = Sem(K, "dcf", 16)
    sem_lbl = Sem(K, "dlbl", 16)
    sem_spl = Sem(K, "dspl", 16)
    sem_pos = Sem(K, "dpos", 16)
    sem_m32 = [Sem(K, "dm32_%d" % i, 16) for i in range(2)]
    sem_x = [Sem(K, "dx%d" % i, 16) for i in range(NT)]

    def bf16v(t):
        return t.ap.bitcast(BF16)

    def mm(out, lhsT, rhs, start, stop, reads, writes):
        K.op("pe", lambda e: e.matmul(out, lhsT=lhsT, rhs=rhs, start=start, stop=stop, skip_group_check=True), reads, writes)

    def tr(out, in_, reads, writes):
        K.op("pe", lambda e: e.transpose(out=out, in_=in_, identity=ident), list(reads) + [cbf], writes)

    def act(out, in_, func, reads, writes, scale=None, bias=None, accum=None):
        kw = {}
        if scale is not None:
            kw["scale"] = scale
        if bias is not None:
            kw["bias"] = bias
        if accum is not None:
            kw["accum_out"] = accum
        K.op("act", lambda e: e.activation(out=out, in_=in_, func=func, **kw), reads, writes)

    def tt(eng, out, in0, in1, op, reads, writes):
        K.op(eng, lambda e: e.tensor_tensor(out=out, in0=in0, in1=in1, op=op), reads, writes)

    def ts(eng, out, in0, s1, s2, op0, op1, reads, writes):
        if op1 is None:
            K.op(eng, lambda e: e.tensor_scalar(out=out, in0=in0, scalar1=s1, scalar2=None, op0=op0), reads, writes)
        else:
            K.op(eng, lambda e: e.tensor_scalar(out=out, in0=in0, scalar1=s1, scalar2=s2, op0=op0, op1=op1), reads, writes)

    def stt(out, in0, scalar, in1, op0, op1, reads, writes):
        K.op("dve", lambda e: e.scalar_tensor_tensor(out=out, in0=in0, scalar=scalar, in1=in1, op0=op0, op1=op1), reads, writes)

    def cp(eng, out, in_, reads, writes):
        if eng == "act":
            act(out, in_, AF.Copy, reads, writes)
        else:
            K.op(eng, lambda e: e.tensor_copy(out=out, in_=in_), reads, writes)

    def recip(out, in_, reads, writes):
        K.op("dve", lambda e: e.reciprocal(out=out, in_=in_), reads, writes)

    def memset(eng, out, val, writes):
        K.op(eng, lambda e: e.memset(out, val), (), writes)

    def bc(ap, shape):
        return ap.broadcast_to(list(shape))

    wplan = []
    for s_ in range(n_seq):
        for l in range(n_layers):
            off = 0
            for (name, kc, ncols) in bundle_plan(l):
                wplan.append((l, name, kc, ncols, off))
                off += kc * ncols
    wstate = {"issued": 0, "used": 0}

    def w_issue_upto(n):
        while wstate["issued"] < min(n, len(wplan)):
            i = wstate["issued"]
            l, name, kc, ncols, off = wplan[i]
            slot = ring[i % 3]
            src = wl_d[l][:, off:off + kc * ncols]
            dst = slot.ap[:, 0:kc * ncols]
            K.op("pool", lambda e, dst=dst, src=src: e.dma_start(out=dst, in_=src), (), [slot], sem=ring_sem[i % 3])
            wstate["issued"] += 1

    def w_get(name, ahead=2):
        i = wstate["used"]
        l, nm, kc, ncols, off = wplan[i]
        assert nm == name, (nm, name)
        w_issue_upto(i + ahead + 1)
        wstate["used"] += 1
        slot = ring[i % 3]
        return slot, slot.ap[:, 0:kc * ncols].rearrange("p (k n) -> p k n", k=kc)

    K.op("pool", lambda e: e.dma_start(out=cbf.ap, in_=cbf_d), (), [cbf], sem=sem_cbf)
    K.op("sp", lambda e: e.dma_start(out=cf.ap, in_=cf_d), (), [cf], sem=sem_cf)
    K.op("sp", lambda e: e.dma_start(out=lbl.ap.rearrange("p a b -> p (a b)"), in_=lbl_d), (), [lbl], sem=sem_lbl)
    memset("dve", ones.ap, 1.0, [ones])
    memset("dve", epsT.ap, EPS, [epsT])
    with ExitStack() as es:
        ex = sbt(es, "lb_ex", [128, 4, 8], F32)
        sm = sbt(es, "lb_sm", [128, 8], F32)
        act(ex.ap, lbl.ap, AF.Exp, [lbl], [ex])
        tt("dve", sm.ap, ex.ap[:, 0, :], ex.ap[:, 1, :], ALU.add, [ex], [sm])
        tt("dve", sm.ap, sm.ap, ex.ap[:, 2, :], ALU.add, [ex, sm], [sm])
        tt("dve", sm.ap, sm.ap, ex.ap[:, 3, :], ALU.add, [ex, sm], [sm])
        recip(sm.ap, sm.ap, [sm], [sm])
        tt("dve", lbv.ap[:, 0, :], ex.ap[:, 1, :], sm.ap, ALU.mult, [ex, sm], [lbv])
        tt("dve", ex.ap[:, 0, :], ex.ap[:, 1, :], ex.ap[:, 2, :], ALU.add, [ex], [ex])
        tt("dve", ex.ap[:, 0, :], ex.ap[:, 0, :], ex.ap[:, 3, :], ALU.add, [ex], [ex])
        tt("dve", lbv.ap[:, 1, :], ex.ap[:, 0, :], sm.ap, ALU.mult, [ex, sm], [lbv])
        ts("dve", lbv.ap[:, 2:4, :], lbv.ap[:, 0:2, :], -1.0, 1.0, ALU.mult, ALU.add, [lbv], [lbv])
        K.barrier()

    pjc = [0]
    pj_banks = [[0, 1]]

    def next_pj():
        pjc[0] += 1
        bl = pj_banks[0]
        return PB[bl[pjc[0] % len(bl)]]

    def rms_rstd(ss, rstd, n, reads_extra=()):
        act(rstd.ap, ss.ap, AF.Sqrt, [ss, epsT], [rstd], scale=1.0 / n, bias=epsT.ap[:, 0:1])
        recip(rstd.ap, rstd.ap, [rstd], [rstd])

    def tok_cols(g, ti):
        if g == 0:
            return slice(ti * 128, (ti + 1) * 128)
        if g == 1:
            c, r = ti // 4, ti % 4
            return slice(512 * c + r, 512 * c + 512, 4)
        return slice(ti, S, 16)

    def load_spl(l):
        K.op("sp", lambda e: e.dma_start(out=spl.ap, in_=spl_d[l]), (), [spl], sem=sem_spl)

    def run_pipeline(n, stages):
        ns = len(stages)
        for step in range(n + ns - 1):
            for k in range(ns - 1, -1, -1):
                i = step - k
                if 0 <= i < n:
                    stages[k](i)

    def build_hT(es):
        junk = sbt(es, "junk", [128, D], BF16)
        NH = 4
        hn = [sbt(es, "hn%d" % i, [128, D], BF16) for i in range(NH)]
        ss = [sbt(es, "hss%d" % i, [128, 1], F32) for i in range(NH)]
        rs = [sbt(es, "hrs%d" % i, [128, 1], F32) for i in range(NH)]

        def s0(ti):
            b = ti % NH
            act(junk.ap, x_t[ti].ap, AF.Square, [x_t[ti]], [ss[b], junk], accum=ss[b].ap)

        def s1(ti):
            b = ti % NH
            rms_rstd(ss[b], rs[b], float(D))

        def s2(ti):
            b = ti % NH
            ts("dve", hn[b].ap, x_t[ti].ap, rs[b].ap[:, 0:1], None, ALU.mult, None, [x_t[ti], rs[b]], [hn[b]])

        def s3(ti):
            b = ti % NH
            trb = PB[2 + ti % 2]
            tv = bf16v(trb).rearrange("p (k n) -> p k n", k=8)
            for kc in range(8):
                tr(tv[:, kc, :], hn[b].ap[:, kc * 128:(kc + 1) * 128], [hn[b]], [trb])

        def s4(ti):
            trb = PB[2 + ti % 2]
            tv = bf16v(trb).rearrange("p (k n) -> p k n", k=8)
            tt("dve", hT.ap[:, :, ti * 128:(ti + 1) * 128], tv, bc(spl.ap[:, 0:8].unsqueeze(2), [128, 8, 128]), ALU.mult, [trb, spl], [hT])

        run_pipeline(NT, [s0, s1, s2, s3, s4])

    def mem_prep(s_, l):
        kmT, vm = memkv["kmT"], memkv["vm"]
        with ExitStack() as es:
            m32 = [sbt(es, "m32_%d" % i, [128, D], F32) for i in range(2)]
            junk = sbt(es, "mjunk", [128, D], BF16)
            mh = [sbt(es, "mh%d" % i, [128, D], BF16) for i in range(2)]
            ss = [sbt(es, "mss%d" % i, [128, 4], F32) for i in range(2)]
            rs = [sbt(es, "mrs%d" % i, [128, 4], F32) for i in range(2)]
            mnT = sbt(es, "mnT", [128, 8, MEM], BF16)
            kn = [sbt(es, "kn%d" % i, [128, 512], F32) for i in range(2)]
            knb = [sbt(es, "knb%d" % i, [128, 512], BF16) for i in range(2)]
            for i in range(2):
                K.op("sp", lambda e, i=i: e.dma_start(out=m32[i].ap, in_=mem_d[s_, i * 128:(i + 1) * 128, :]), (), [m32[i]], sem=sem_m32[i])
            for i in range(2):
                act(junk.ap, m32[i].ap, AF.Square, [m32[i]], [ss[i], junk], accum=ss[i].ap[:, 0:1])
                act(rs[i].ap[:, 0:1], ss[i].ap[:, 0:1], AF.Sqrt, [ss[i], epsT], [rs[i]], scale=1.0 / D, bias=epsT.ap[:, 0:1])
                recip(rs[i].ap[:, 0:1], rs[i].ap[:, 0:1], [rs[i]], [rs[i]])
                ts("dve", mh[i].ap, m32[i].ap, rs[i].ap[:, 0:1], None, ALU.mult, None, [m32[i], rs[i]], [mh[i]])
                trb = PB[2 + i]
                tv = bf16v(trb).rearrange("p (k n) -> p k n", k=8)
                for kc in range(8):
                    tr(tv[:, kc, :], mh[i].ap[:, kc * 128:(kc + 1) * 128], [mh[i]], [trb])
                tt("dve", mnT.ap[:, :, i * 128:(i + 1) * 128], tv, bc(spl.ap[:, 8:16].unsqueeze(2), [128, 8, 128]), ALU.mult, [trb, spl], [mnT])
            pk = [PB[4], PB[5]]
            pv = [PB[6], PB[7]]
            for bi in range(4):
                slot, wv = w_get("kv%d" % bi)
                for i in range(2):
                    dst = (pk if bi < 2 else pv)[i]
                    c0 = (bi % 2) * 256
                    for kc in range(8):
                        mm(dst.ap[:, c0:c0 + 256], mnT.ap[:, kc, i * 128:(i + 1) * 128], wv[:, kc, :], kc == 0, kc == 7, [mnT, slot], [dst])
            for i in range(2):
                for hh in range(4):
                    act(junk.ap[:, 0:128], pk[i].ap[:, hh * 128:(hh + 1) * 128], AF.Square, [pk[i]], [ss[i], junk], accum=ss[i].ap[:, hh:hh + 1])
                rms_rstd(ss[i], rs[i], 128.0)
                tt("dve", kn[i].ap.rearrange("p (h d) -> p h d", h=4), pk[i].ap.rearrange("p (h d) -> p h d", h=4),
                   bc(rs[i].ap.unsqueeze(2), [128, 4, 128]), ALU.mult, [pk[i], rs[i]], [kn[i]])
                tt("pool", knb[i].ap.rearrange("p (h d) -> p h d", h=4), kn[i].ap.rearrange("p (h d) -> p h d", h=4),
                   bc(spl.ap[:, 144:272].unsqueeze(1), [128, 4, 128]), ALU.mult, [kn[i], spl], [knb[i]])
                trb = PB[2 + i]
                tv = bf16v(trb)[:, 0:512].rearrange("p (k n) -> p k n", k=4)
                for hh in range(4):
                    tr(tv[:, hh, :], knb[i].ap[:, hh * 128:(hh + 1) * 128], [knb[i]], [trb])
                cp("act", kmT.ap[:, :, i * 128:(i + 1) * 128], tv, [trb], [kmT])
                cp("act", vm.ap[:, i, :], pv[i].ap, [pv[i]], [vm])
            K.barrier()

    def final_norm(num, den, sgw, ydst, ytile, rden, t1):
        act(rden.ap, den.ap, AF.Ln, [den], [rden])
        act(rden.ap, rden.ap, AF.Exp, [rden], [rden], scale=-1.0)
        tt("dve", rden.ap, rden.ap, sgw, ALU.mult, [rden] + sgw_t[0], [rden])
        tt("dve", ydst, num.ap, rden.ap, ALU.mult, [num, rden], [ytile])

    sgw_t = [[]]

    def mem_heads(es, l, is_last_layer):
        kmT, vm = memkv["kmT"], memkv["vm"]
        NS = 4
        junk2 = [sbt(es, "qjunk%d" % i, [128, 128], BF16) for i in range(2)]
        qmT = [sbt(es, "qmT%d" % i, [128, S], BF16) for i in range(2)]
        sg = [sbt(es, "msg%d" % i, [128, S], BF16) for i in range(2)]
        ss = [sbt(es, "qss%d" % i, [128, 2], F32) for i in range(NS)]
        rs = [sbt(es, "qrs%d" % i, [128, 2], F32) for i in range(NS)]
        qn = [sbt(es, "qn%d" % i, [128, 256], F32) for i in range(NS)]
        qnb = [sbt(es, "qnb%d" % i, [128, 256], BF16) for i in range(NS)]
        PT = [sbt(es, "mPT%d" % i, [128, 512], BF16) for i in range(3)]
        rdens = [sbt(es, "mrden%d" % i, [128, 512], F32) for i in range(2)]
        ptc = [0]
        wcm = [0]
        stc = [0]
        for hp in range(2):
            pjs = {}
            wc_ = {}

            def s0(i):
                if i == 0:
                    wc_["w"] = w_get("qm_%d" % hp)
                slot, wv = wc_["w"]
                pj = next_pj()
                pjs[i] = pj
                for kc in range(8):
                    mm(pj.ap[:, 0:256], hT.ap[:, kc, i * 128:(i + 1) * 128], wv[:, kc, :], kc == 0, kc == 7, [hT, slot], [pj])

            def s1(i):
                b = i % NS
                pj = pjs.pop(i)
                for q in range(2):
                    act(junk2[q].ap, pj.ap[:, q * 128:(q + 1) * 128], AF.Square, [pj], [ss[b], junk2[q]], accum=ss[b].ap[:, q:q + 1])
                tt("dve", qn[b].ap.rearrange("p (h d) -> p h d", h=2), pj.ap[:, 0:256].rearrange("p (h d) -> p h d", h=2),
                   bc(spl.ap[:, 16:144].unsqueeze(1), [128, 2, 128]), ALU.mult, [pj, spl], [qn[b]])

            def s2(i):
                b = i % NS
                rms_rstd(ss[b], rs[b], 128.0)

            def s3(i):
                b = i % NS
                tt("pool", qnb[b].ap.rearrange("p (h d) -> p h d", h=2), qn[b].ap.rearrange("p (h d) -> p h d", h=2),
                   bc(rs[b].ap.unsqueeze(2), [128, 2, 128]), ALU.mult, [qn[b], rs[b]], [qnb[b]])

            def s4(i):
                b = i % NS
                trb = PB[2 + (i // 4) % 2]
                tv = bf16v(trb).rearrange("p (k n) -> p k n", k=8)
                for q in range(2):
                    tr(tv[:, (i % 4) * 2 + q, :], qnb[b].ap[:, q * 128:(q + 1) * 128], [qnb[b]], [trb])

            def s5(i):
                if i % 4 == 3:
                    trb = PB[2 + (i // 4) % 2]
                    t0 = i - 3
                    tv4 = bf16v(trb).rearrange("p (t q n) -> p t q n", t=4, q=2)
                    cp("act", qmT[0].ap[:, t0 * 128:(t0 + 4) * 128].rearrange("p (t n) -> p t n", t=4), tv4[:, :, 0, :], [trb], [qmT[0]])
                    cp("dve", qmT[1].ap[:, t0 * 128:(t0 + 4) * 128].rearrange("p (t n) -> p t n", t=4), tv4[:, :, 1, :], [trb], [qmT[1]])

            pj_banks[0] = [0, 1, 4, 5]
            run_pipeline(NT, [s0, s1, s2, s3, s4, s5])
            pj_banks[0] = [0, 1]
            for q in range(2):
                c = 8 + hp * 2 + q
                slot, wv = w_get("gate_%d" % c)
                for tb in range(4):
                    pj = next_pj()
                    for kc in range(8):
                        mm(pj.ap, wv[:, kc, :], hT.ap[:, kc, tb * 512:(tb + 1) * 512], kc == 0, kc == 7, [hT, slot], [pj])
                    act(sg[q].ap[:, tb * 512:(tb + 1) * 512], pj.ap, AF.Silu, [pj], [sg[q]])
            A = []
            Bq = []
            for q in range(2):
                mh = hp * 2 + q
                for w in range(4):
                    wcm[0] += 1
                    num, den = (PB[6], PB[7]) if wcm[0] % 2 == 0 else (PB[2], PB[3])
                    rden = rdens[wcm[0] % 2]
                    for mt in range(2):
                        box = {}

                        def fa(q=q, mh=mh, w=w, mt=mt, box=box):
                            st = PB[4 + stc[0] % 2]
                            stc[0] += 1
                            mm(st.ap, kmT.ap[:, mh, mt * 128:(mt + 1) * 128], qmT[q].ap[:, w * 512:(w + 1) * 512], True, True, [kmT, qmT[q]], [st])
                            pt = PT[ptc[0] % 3]
                            ptc[0] += 1
                            act(pt.ap, st.ap, AF.Exp, [st], [pt], scale=ATT)
                            box["pt"] = pt

                        def fb(q=q, mh=mh, w=w, mt=mt, box=box, num=num, den=den, rden=rden):
                            pt = box["pt"]
                            mm(num.ap, vm.ap[:, mt, mh * 128:(mh + 1) * 128], pt.ap, mt == 0, mt == 1, [vm, pt], [num])
                            mm(den.ap, ones.ap, pt.ap, mt == 0, mt == 1, [ones, pt], [den])
                            if mt == 1:
                                sgw_t[0] = [sg[q]]
                                final_norm(num, den, sg[q].ap[:, w * 512:(w + 1) * 512], yT[mh].ap[:, w * 512:(w + 1) * 512], yT[mh], rden, None)
                        A.append(fa)
                        Bq.append(fb)
            LAG = 2
            nb = len(A)
            for step in range(nb + LAG):
                if step < nb:
                    A[step]()
                if step - LAG >= 0:
                    Bq[step - LAG]()

    def out_proj(G, s_, is_last):
        for nb in range(2):
            slot, wv = w_get("wo%d_%d" % (G, nb))
            for ti in range(NT):
                pj = next_pj()
                for ci in range(4):
                    mm(pj.ap, yT[ci].ap[:, ti * 128:(ti + 1) * 128], wv[:, ci, :], ci == 0, ci == 3, [yT[ci], slot], [pj])
                xs = x_t[ti].ap[:, nb * 512:(nb + 1) * 512]
                tt("dve", xs, pj.ap, xs, ALU.add, [pj, x_t[ti]], [x_t[ti]])
                if is_last and G == 2 and nb == 1:
                    K.op("sp", lambda e, ti=ti: e.dma_start(out=out_d[s_, ti * 128:(ti + 1) * 128, :], in_=x_t[ti].ap),
                         [x_t[ti]], (), sem=sem_x[ti])

    def layer_A(s_, l, is_last):
        with ExitStack() as es:
            build_hT(es)
            K.barrier()
        chk(3)
        with ExitStack() as es:
            qT = [sbt(es, "qT%d" % g, [128, S], BF16) for g in range(3)]
            kT = [sbt(es, "kT%d" % g, [128, S], BF16) for g in range(3)]
            V = [sbt(es, "V%d" % g, [128, NT, 128], BF16) for g in range(3)]
            sg = sbt(es, "sg", [128, S], BF16)
            PT2 = sbt(es, "PT2", [128, NT, 128], BF16)
            PT = [sbt(es, "PT%d" % i, [128, 512], BF16) for i in range(3)]
            junk2 = [sbt(es, "ajunk%d" % i, [128, 128], BF16) for i in range(2)]
            NSET = 4
            ss = [sbt(es, "ass%d" % i, [128, 2], F32) for i in range(NSET)]
            rs = [sbt(es, "ars%d" % i, [128, 2], F32) for i in range(NSET)]
            gq = [sbt(es, "gq%d" % i, [128, 256], F32) for i in range(NSET)]
            rB = [sbt(es, "rB%d" % i, [128, 64], F32) for i in range(NSET)]
            qkb = [sbt(es, "qkb%d" % i, [128, 256], BF16) for i in range(NSET)]
            rdens = [sbt(es, "rden%d" % i, [128, 512], F32) for i in range(2)]
            wc = [0]
            ptc = [0]
            stc = [0]
            tic = [0]

            def run_pipeline(n, stages):
                ns = len(stages)
                for step in range(n + ns - 1):
                    for k in range(ns - 1, -1, -1):
                        i = step - k
                        if 0 <= i < n:
                            stages[k](i)

            wcur = {}

            def proj_head(c):
                def geti(i):
                    return i // 16, i % 16, i % NSET

                def s0(i):
                    g, ti, b = geti(i)
                    if ti == 0:
                        wcur[g] = w_get("qkv%d_%d" % (g, c))
                    slot, wv = wcur[g]
                    pj = next_pj()
                    pjs[i] = pj
                    cols = tok_cols(g, ti)
                    for kc in range(8):
                        mm(pj.ap[:, 0:384], hT.ap[:, kc, cols], wv[:, kc, :], kc == 0, kc == 7, [hT, slot], [pj])

                def s1(i):
                    g, ti, b = geti(i)
                    pj = pjs.pop(i)
                    for q in range(2):
                        act(junk2[q].ap, pj.ap[:, q * 128:(q + 1) * 128], AF.Square, [pj], [ss[b], junk2[q]], accum=ss[b].ap[:, q:q + 1])
                    tt("dve", gq[b].ap, pj.ap[:, 0:256], spl.ap[:, 288 + g * 256:288 + (g + 1) * 256], ALU.mult, [pj, spl], [gq[b]])
                    cp("act", V[g].ap[:, ti, :], pj.ap[:, 256:384], [pj], [V[g]])

                def s2(i):
                    g, ti, b = geti(i)
                    rms_rstd(ss[b], rs[b], 128.0)
                    R = gq[b].ap.rearrange("p (a d) -> p a d", a=2)[:, :, 0:32].rearrange("p a (h f) -> p a h f", h=2)
                    Rsw = R[:, :, ::-1, :]
                    cosb = bc(COS.ap[:, g * 16 + ti, :].unsqueeze(1).unsqueeze(1), [128, 2, 2, 16])
                    s2b = bc(S2.ap[:, g * 16 + ti, :, :].unsqueeze(1), [128, 2, 2, 16])
                    B4 = rB[b].ap.rearrange("p (a h f) -> p a h f", a=2, h=2)
                    tt("pool", B4, Rsw, s2b, ALU.mult, [gq[b], S2], [rB[b]])
                    tt("pool", R, R, cosb, ALU.mult, [gq[b], COS], [gq[b]])
                    tt("pool", R, R, B4, ALU.add, [gq[b], rB[b]], [gq[b]])

                def s3(i):
                    g, ti, b = geti(i)
                    tt("dve", qkb[b].ap.rearrange("p (a d) -> p a d", a=2), gq[b].ap.rearrange("p (a d) -> p a d", a=2),
                       bc(rs[b].ap.unsqueeze(2), [128, 2, 128]), ALU.mult, [gq[b], rs[b]], [qkb[b]])

                def s4(i):
                    g, ti, b = geti(i)
                    trb = PB[2 + (i // 4) % 2]
                    tv = bf16v(trb).rearrange("p (k n) -> p k n", k=8)
                    for q in range(2):
                        tr(tv[:, (i % 4) * 2 + q, :], qkb[b].ap[:, q * 128:(q + 1) * 128], [qkb[b]], [trb])

                def s5(i):
                    g, ti, b = geti(i)
                    if i % 4 == 3:
                        trb = PB[2 + (i // 4) % 2]
                        t0 = ti - 3
                        tv4 = bf16v(trb).rearrange("p (t q n) -> p t q n", t=4, q=2)
                        cp("act", qT[g].ap[:, t0 * 128:(t0 + 4) * 128].rearrange("p (t n) -> p t n", t=4), tv4[:, :, 0, :], [trb], [qT[g]])
                        cp("dve", kT[g].ap[:, t0 * 128:(t0 + 4) * 128].rearrange("p (t n) -> p t n", t=4), tv4[:, :, 1, :], [trb], [kT[g]])

                pjs = {}
                run_pipeline(48, [s0, s1, s2, s3, s4, s5])

            def gate(c):
                slot, wv = w_get("gate_%d" % c)
                for tb in range(4):
                    pj = next_pj()
                    for kc in range(8):
                        mm(pj.ap, wv[:, kc, :], hT.ap[:, kc, tb * 512:(tb + 1) * 512], kc == 0, kc == 7, [hT, slot], [pj])
                    act(sg.ap[:, tb * 512:(tb + 1) * 512], pj.ap, AF.Silu, [pj], [sg])

            ST_BANKS = [4, 5, 2, 3]

            def attn(c):
                LAG = 2
                A = []
                Bq = []

                def add_st(jobs, mask, pv_list_fn):
                    box = {}

                    def fa():
                        st = PB[ST_BANKS[stc[0] % 4]]
                        stc[0] += 1
                        lo = min(j[3] for j in jobs) * 128
                        hi = (max(j[3] for j in jobs) + 1) * 128
                        for n_, (g, kb, qb, sl) in enumerate(jobs):
                            mm(st.ap[:, sl * 128:(sl + 1) * 128], kT[g].ap[:, kb * 128:(kb + 1) * 128], qT[g].ap[:, qb * 128:(qb + 1) * 128],
                               n_ == 0, False, [kT[g], qT[g]], [st])
                        mm(st.ap[:, lo:hi], ident, mask[:, lo:hi], False, True, [cbf], [st])
                        pt = PT[ptc[0] % 3]
                        ptc[0] += 1
                        act(pt.ap[:, lo:hi], st.ap[:, lo:hi], AF.Exp, [st], [pt], scale=ATT)
                        box["pt"] = pt

                    def fb():
                        pv_list_fn(box["pt"])
                    A.append(fa)
                    Bq.append(fb)

                for bb in range(4):
                    def fa2(bb=bb):
                        st = PB[ST_BANKS[stc[0] % 4]]
                        stc[0] += 1
                        for j in range(4):
                            r = bb * 4 + j
                            mm(st.ap[:, j * 128:(j + 1) * 128], kT[2].ap[:, r * 128:(r + 1) * 128], qT[2].ap[:, r * 128:(r + 1) * 128],
                               j == 0, False, [kT[2], qT[2]], [st])
                        mm(st.ap, ident, Mcc, False, True, [cbf], [st])
                        dst = PT2.ap[:, bb * 4:(bb + 1) * 4, :].rearrange("p a b -> p (a b)")
                        act(dst, st.ap, AF.Exp, [st], [PT2], scale=ATT)
                    A.append(fa2)
                    Bq.append(None)
                for w in range(4):
                    wc[0] += 1
                    num, den = (PB[6], PB[7]) if wc[0] % 2 == 0 else (PB[0], PB[1])
                    rden = rdens[wc[0] % 2]
                    first = [True]

                    def pv(vt, vap, pt_t, ptap, ocols, num=num, den=den, first=first):
                        mm(num.ap[:, ocols], vap, ptap, first[0], False, [vt, pt_t], [num])
                        mm(den.ap[:, ocols], ones.ap, ptap, first[0], False, [ones, pt_t], [den])
                        first[0] = False
                    for half in range(2):
                        jobs = []
                        for k in range(2):
                            qb = 4 * w + 2 * half + k
                            if qb > 0:
                                jobs.append((0, qb - 1, qb, 2 * k))
                            jobs.append((0, qb, qb, 2 * k + 1))

                        def pvs0(pt, jobs=jobs, pv=pv):
                            for (g, kb, qb, sl) in jobs:
                                pv(V[0], V[0].ap[:, kb, :], pt, pt.ap[:, sl * 128:(sl + 1) * 128], slice((qb % 4) * 128, (qb % 4 + 1) * 128))
                        add_st(jobs, Mpc, pvs0)
                    if w == 0:
                        jl = [([(1, r, r, r) for r in range(4)], Mcc)]
                    else:
                        jl = []
                        for half in range(2):
                            jobs = []
                            for k in range(2):
                                r = 2 * half + k
                                qb = 4 * w + r
                                jobs.append((1, qb - 4, qb, 2 * k))
                                jobs.append((1, qb, qb, 2 * k + 1))
                            jl.append((jobs, Mpc))
                    for (jobs, mask) in jl:
                        def pvs1(pt, jobs=jobs, pv=pv):
                            for (g, kb, qb, sl) in jobs:
                                pv(V[1], V[1].ap[:, kb, :], pt, pt.ap[:, sl * 128:(sl + 1) * 128], slice(qb % 4, 512, 4))
                        add_st(jobs, mask, pvs1)

                    def fin(w=w, pv=pv, num=num, den=den, rden=rden):
                        for r in range(16):
                            pv(V[2], V[2].ap[:, r, :], PT2, PT2.ap[:, r, 32 * w:32 * w + 32], slice(r, 512, 16))
                        sgw_t[0] = [sg]
                        final_norm(num, den, sg.ap[:, w * 512:(w + 1) * 512], yT[c % 4].ap[:, w * 512:(w + 1) * 512], yT[c % 4], rden, None)
                    A.append(None)
                    Bq.append(fin)
                nb = len(A)
                for step in range(nb + LAG):
                    if step < nb and A[step] is not None:
                        A[step]()
                    if step - LAG >= 0 and Bq[step - LAG] is not None:
                        Bq[step - LAG]()

            for G in range(2):
                for hh in range(4):
                    c = G * 4 + hh
                    pj_banks[0] = [0, 1, 4, 5]
                    proj_head(c)
                    chk(6)
                    gate(c)
                    pj_banks[0] = [0, 1]
                    chk(7)
                    attn(c)
                    chk(8)
                out_proj(G, s_, False)
                chk(9)
            K.barrier()
        with ExitStack() as es:
            memkv["kmT"] = sbt(es, "kmT", [128, 4, MEM], BF16)
            memkv["vm"] = sbt(es, "vm", [128, 2, 512], BF16)
            mem_prep(s_, l)
            mem_heads(es, l, is_last)
            out_proj(2, s_, is_last)
            K.barrier()

    def layer_B(s_, l, is_last):
        jb = l // 2
        with ExitStack() as es:
            build_hT(es)
            K.barrier()
        with ExitStack() as es:
            def mk(name, shape, dt, n=2):
                return [sbt(es, "%s%d" % (name, i), shape, dt) for i in range(n)]
            qfs = mk("qf", [128, 512], F32)
            Fts = mk("Ft", [128, 512], F32)
            Gt = sbt(es, "Gt", [128, 512], F32)
            kk = sbt(es, "kk", [128, 512], F32)
            Dts = mk("Dt", [128, 512], F32)
            Ets = mk("Et", [128, 512], BF16)
            dec = [0]
            Rt = mk("Rt", [128, 8, 4], F32)
            eg = mk("eg", [128, 8], F32, 3)
            qh = mk("qh", [128, 512], BF16)
            qi = mk("qi", [128, 512], BF16, 3)
            kh = [mk("kh%d_" % I, [128, 512], BF16) for I in range(4)]
            kd = mk("kd", [128, 512], BF16)
            kdt = mk("kdt", [128, 4, 128], BF16)
            Vb = mk("Vb", [128, 4, 128], BF16, 3)
            sgu = mk("sgu", [128, 512], BF16, 3)
            ATb = mk("ATb", [128, 512], BF16)
            Sall = mk("Sall", [128, 8, 128], BF16, 3)
            Sf = mk("Sf", [128, 128], F32)
            sq = sbt(es, "osq", [128, 512], BF16)
            Uf = mk("Uf", [128, 512], F32)
            rsn = sbt(es, "orsn", [128, 512], F32)
            sfc = [0]
            wcur = {}
            for I in range(3):
                for j_ in range(2):
                    memset("pool", kh[I][j_].ap, 0.0, [kh[I][j_]])

            def mA(u):
                c, tb = divmod(u, 4)
                u2 = u % 2
                qf, Ft = qfs[u2], Fts[u2]
                if tb == 0:
                    wcur[("g", c)] = w_get("gate_%d" % c)
                    wcur[("q", c)] = w_get("qfv_%d" % c, ahead=1)
                slot_q, wq = wcur[("q", c)]
                cols = slice(tb * 512, (tb + 1) * 512)
                pq = next_pj()
                for kc in range(8):
                    mm(pq.ap, wq[:, kc, 0:128], hT.ap[:, kc, cols], kc == 0, kc == 7, [hT, slot_q], [pq])
                cp("act", qf.ap, pq.ap, [pq], [qf])
                pf = next_pj()
                for kc in range(8):
                    mm(pf.ap, wq[:, kc, 128:256], hT.ap[:, kc, cols], kc == 0, kc == 7, [hT, slot_q], [pf])
                act(Ft.ap, pf.ap, AF.Sigmoid, [pf], [Ft])

            def m0(u):
                c, tb = divmod(u, 4)
                u2, u3 = u % 2, u % 3
                qf, Ft = qfs[u2], Fts[u2]
                slot_g, wg = wcur[("g", c)]
                slot_q, wq = wcur[("q", c)]
                lb_ap = lbv.ap[:, jb, c:c + 1]
                oml_ap = lbv.ap[:, 2 + jb, c:c + 1]
                cols = slice(tb * 512, (tb + 1) * 512)
                ts("dve", Ft.ap, Ft.ap, oml_ap, lb_ap, ALU.mult, ALU.add, [Ft, lbv], [Ft])
                ts("pool", kk.ap, Ft.ap, -1.0, 1.0, ALU.mult, ALU.add, [Ft], [kk])
                act(Ft.ap, Ft.ap, AF.Ln, [Ft], [Ft])
                K.op("dve", lambda e: e.tensor_tensor_scan(out=Gt.ap, data0=rst, data1=Ft.ap, initial=0.0, op0=ALU.mult, op1=ALU.add),
                     [Ft, cf], [Gt])
                G4 = Gt.ap.rearrange("p (c i t) -> p c i t", c=8, i=4)
                G3 = Gt.ap.rearrange("p (c t) -> p c t", c=8)
                R_ = Rt[u2]
                memset("pool", R_.ap[:, :, 0:1], 0.0, [R_])
                cp("pool", R_.ap[:, :, 1:4], G4[:, :, 0:3, 15], [Gt], [R_])
                yield
                trip = []

                def add_trip(sub_fn, exp_src, mult_fn):
                    trip.append((sub_fn, exp_src, mult_fn))

                add_trip(None, None, lambda Et: tt("dve", qi[u3].ap, qf.ap, Et.ap, ALU.mult, [qf, Et], [qi[u3]]))
                add_trip(lambda Dt: tt("pool", Dt.ap.rearrange("p (c i t) -> p c i t", c=8, i=4), G4, bc(R_.ap.unsqueeze(3), [128, 8, 4, 16]), ALU.subtract, [Gt, R_], [Dt]),
                         True, lambda Et: tt("dve", qh[u2].ap, qf.ap, Et.ap, ALU.mult, [qf, Et], [qh[u2]]))
                kk3 = kk.ap.rearrange("p (c t) -> p c t", c=8)
                for I in range(4):
                    se = "dve" if I % 2 == 0 else "pool"
                    wI = 16 * (I + 1)
                    add_trip(lambda Dt, I=I, se=se, wI=wI: tt(se, Dt.ap.rearrange("p (c t) -> p c t", c=8)[:, :, 0:wI], bc(R_.ap[:, :, I:I + 1], [128, 8, wI]), G3[:, :, 0:wI], ALU.subtract, [Gt, R_], [Dt]),
                             wI, lambda Et, I=I, wI=wI: stt(kh[I][u2].ap.rearrange("p (c t) -> p c t", c=8)[:, :, 0:wI], Et.ap.rearrange("p (c t) -> p c t", c=8)[:, :, 0:wI], 1e30, kk3[:, :, 0:wI], ALU.min, ALU.mult, [Et, kk], [kh[I][u2]]))
                add_trip(lambda Dt: tt("dve", Dt.ap.rearrange("p (c t) -> p c t", c=8), bc(G3[:, :, 63:64], [128, 8, 64]), G3, ALU.subtract, [Gt], [Dt]),
                         True, lambda Et: tt("dve", kd[u2].ap, Et.ap, kk.ap, ALU.mult, [Et, kk], [kd[u2]]))
                nt_ = len(trip)
                base = dec[0]
                dec[0] += nt_
                for k in range(nt_ + 2):
                    if k < nt_ and trip[k][0] is not None:
                        trip[k][0](Dts[(base + k) % 2])
                    if 0 <= k - 1 < nt_:
                        j = k - 1
                        src = Dts[(base + j) % 2] if trip[j][1] else Gt
                        ed = Ets[(base + j) % 2]
                        if trip[j][1] is True or trip[j][1] is None or trip[j][1] == 64:
                            act(ed.ap, src.ap, AF.Exp, [src], [ed])
                        else:
                            wj = trip[j][1]
                            act(ed.ap.rearrange("p (c t) -> p c t", c=8)[:, :, 0:wj], src.ap.rearrange("p (c t) -> p c t", c=8)[:, :, 0:wj], AF.Exp, [src], [ed])
                    if 0 <= k - 2 < nt_:
                        j = k - 2
                        trip[j][2](Ets[(base + j) % 2])
                    yield
                act(eg[u3].ap, G3[:, :, 63], AF.Exp, [Gt], [eg[u3]])
                yield
                pvv = next_pj()
                for t4 in range(4):
                    ti = tb * 4 + t4
                    for kc in range(8):
                        mm(pvv.ap[:, t4 * 128:(t4 + 1) * 128], hT.ap[:, kc, ti * 128:(ti + 1) * 128], wq[:, kc, 256:384], kc == 0, kc == 7, [hT, slot_q], [pvv])
                cp("act", Vb[u3].ap.rearrange("p a b -> p (a b)"), pvv.ap, [pvv], [Vb[u3]])
                pg = next_pj()
                for kc in range(8):
                    mm(pg.ap, wg[:, kc, :], hT.ap[:, kc, cols], kc == 0, kc == 7, [hT, slot_g], [pg])
                act(sgu[u3].ap, pg.ap, AF.Sigmoid, [pg], [sgu[u3]])
                tt("dve", sgu[u3].ap, pg.ap, sgu[u3].ap, ALU.mult, [pg, sgu[u3]], [sgu[u3]])

            def m1(u):
                c, tb = divmod(u, 4)
                u2, u3 = u % 2, u % 3
                trb = PB[2]
                tv = bf16v(trb)[:, 0:512].rearrange("p (k n) -> p k n", k=4)
                for t4 in range(4):
                    tr(tv[:, t4, :], kd[u2].ap[:, t4 * 128:(t4 + 1) * 128], [kd[u2]], [trb])
                cp("act", kdt[u2].ap, tv, [trb], [kdt[u2]])
                yield
                at = PB[3]
                fst = True
                for t4 in range(4):
                    for a in range(2):
                        for I in range(4):
                            c0 = t4 * 128 + a * 64 + 16 * I
                            mm(at.ap[:, c0:c0 + 16], kh[I][u2].ap[:, t4 * 128:(t4 + 1) * 128], qh[u2].ap[:, c0:c0 + 16], fst, False, [kh[I][u2], qh[u2]], [at])
                            fst = False
                tt("dve", ATb[u2].ap, at.ap, Mbd, ALU.mult, [at, cbf], [ATb[u2]])
                yield
                for ch in range(8):
                    t4, a = ch // 2, ch % 2
                    ub = PB[4 + a]
                    mm(ub.ap[:, t4 * 128:(t4 + 1) * 128], kdt[u2].ap[64 * a:64 * a + 64, t4, :], Vb[u3].ap[64 * a:64 * a + 64, t4, :],
                       t4 == 0, t4 == 3, [kdt[u2], Vb[u3]], [ub])
                if tb == 0:
                    memset("pool", Sall[u3].ap[:, 0, :], 0.0, [Sall[u3]])
                    memset("dve", Sf[sfc[0] % 2].ap, 0.0, [Sf[sfc[0] % 2]])
                yield
                for ch in range(8):
                    so = Sf[sfc[0] % 2]
                    sn = Sf[(sfc[0] + 1) % 2]
                    sfc[0] += 1
                    stt(sn.ap, so.ap, eg[u3].ap[:, ch:ch + 1], PB[4 + ch % 2].ap[:, (ch // 2) * 128:(ch // 2 + 1) * 128], ALU.mult, ALU.add, [so, eg[u3], PB[4 + ch % 2]], [sn])
                    if ch < 7:
                        cp("act", Sall[u3].ap[:, ch + 1, :], sn.ap, [sn], [Sall[u3]])
                    elif tb < 3:
                        cp("act", Sall[(u + 1) % 3].ap[:, 0, :], sn.ap, [sn], [Sall[(u + 1) % 3]])
                    yield

            def m2(u):
                c, tb = divmod(u, 4)
                u2, u3 = u % 2, u % 3
                cols = slice(tb * 512, (tb + 1) * 512)
                ob = PB[6]
                for ch in range(8):
                    mm(ob.ap[:, ch * 64:(ch + 1) * 64], Sall[u3].ap[:, ch, :], qi[u3].ap[:, ch * 64:(ch + 1) * 64], ch == 0, False, [Sall[u3], qi[u3]], [ob])
                for t4 in range(4):
                    mm(ob.ap[:, t4 * 128:(t4 + 1) * 128], Vb[u3].ap[:, t4, :], ATb[u2].ap[:, t4 * 128:(t4 + 1) * 128], False, t4 == 3, [Vb[u3], ATb[u2]], [ob])
                act(sq.ap, ob.ap, AF.Square, [ob], [sq])
                yield
                ssb = PB[7]
                mm(ssb.ap, ones.ap, sq.ap, True, True, [ones, sq], [ssb])
                act(rsn.ap, ssb.ap, AF.Ln, [ssb, epsT], [rsn], scale=1.0 / 128, bias=epsT.ap[:, 0:1])
                act(rsn.ap, rsn.ap, AF.Exp, [rsn], [rsn], scale=-0.5)
                yield
                tt("dve", rsn.ap, rsn.ap, sgu[u3].ap, ALU.mult, [rsn, sgu[u3]], [rsn])
                stt(yT[c % 4].ap[:, cols], ob.ap, spl.ap[:, 272:273], rsn.ap, ALU.mult, ALU.mult, [ob, spl, rsn], [yT[c % 4]])

            for G in range(2):
                n = 16
                pj_banks[0] = [0, 1]
                for step in range(n + 3):
                    def gen(fn, i):
                        return fn(16 * G + i) if 0 <= i < n else None

                    def merged():
                        gens = [g for g in (gen(m2, step - 3), gen(m1, step - 2), gen(m0, step - 1)) if g is not None]
                        while gens:
                            for g in list(gens):
                                try:
                                    next(g)
                                except StopIteration:
                                    gens.remove(g)
                    if step % 4 == 0:
                        merged()
                        if step < n:
                            mA(16 * G + step)
                    else:
                        if step < n:
                            mA(16 * G + step)
                        merged()
                out_proj(G, s_, False)
            K.barrier()
        with ExitStack() as es:
            memkv["kmT"] = sbt(es, "kmT", [128, 4, MEM], BF16)
            memkv["vm"] = sbt(es, "vm", [128, 2, 512], BF16)
            mem_prep(s_, l)
            mem_heads(es, l, is_last)
            out_proj(2, s_, is_last)
            K.barrier()

    def rotary_tables(s_):
        with ExitStack() as es:
            pi_ = sbt(es, "pos_i", [128, 48], I32)
            pf = sbt(es, "pos_f", [128, 48], F32)
            ang = sbt(es, "ang", [128, 48, 16], F32)
            u = sbt(es, "ru", [128, 48 * 16], F32)
            ni = sbt(es, "rni", [128, 48 * 16], I32)
            nf = sbt(es, "rnf", [128, 48 * 16], F32)
            m = sbt(es, "rm", [128, 48 * 16], F32)
            K.op("sp", lambda e: e.dma_start(out=pi_.ap, in_=pos_d[s_]), (), [pi_], sem=sem_pos)
            cp("dve", pf.ap, pi_.ap, [pi_], [pf])
            tt("dve", ang.ap, bc(pf.ap.unsqueeze(2), [128, 48, 16]), bc(invf.unsqueeze(1), [128, 48, 16]), ALU.mult, [pf, cf], [ang])
            angf = ang.ap.rearrange("p a b -> p (a b)")
            for which, dstT in ((0, S2), (1, COS)):
                dst = COS.ap if which == 1 else S2.ap[:, :, 1, :]
                if which == 1:
                    ts("dve", u.ap, angf, math.pi / 2, None, ALU.add, None, [ang], [u])
                    src_t, src = u, u.ap
                else:
                    src_t, src = ang, angf
                ts("dve", nf.ap, src, 1.0 / (2 * math.pi), None, ALU.mult, None, [src_t], [nf])
                cp("dve", ni.ap, nf.ap, [nf], [ni])
                cp("dve", nf.ap, ni.ap, [ni], [nf])
                stt(m.ap, nf.ap, -2 * math.pi, src, ALU.mult, ALU.add, [nf, src_t], [m])
                ts("dve", nf.ap, m.ap, math.pi, None, ALU.is_gt, None, [m], [nf])
                stt(m.ap, nf.ap, -2 * math.pi, m.ap, ALU.mult, ALU.add, [nf, m], [m])
                ts("dve", nf.ap, m.ap, -math.pi, None, ALU.is_lt, None, [m], [nf])
                stt(m.ap, nf.ap, 2 * math.pi, m.ap, ALU.mult, ALU.add, [nf, m], [m])
                ts("dve", m.ap, m.ap, PI_SAFE, -PI_SAFE, ALU.min, ALU.max, [m], [m])
                act(dst, m.ap.rearrange("p (a b) -> p a b", a=48), AF.Sin, [m], [dstT])
            ts("dve", S2.ap[:, :, 0, :], S2.ap[:, :, 1, :], -1.0, None, ALU.mult, None, [S2], [S2])
            K.barrier()

    try:
        for s_ in range(n_seq):
            rotary_tables(s_)
            for ti in range(NT):
                K.op("sp", lambda e, ti=ti, s_=s_: e.dma_start(out=x_t[ti].ap, in_=x_d[s_, ti * 128:(ti + 1) * 128, :]), (), [x_t[ti]], sem=sem_x[ti])
            chk(1)
            for l in range(n_layers):
                is_last = (l == n_layers - 1)
                load_spl(l)
                chk(2)
                if l % 2 == 0:
                    layer_A(s_, l, is_last)
                else:
                    layer_B(s_, l, is_last)
    except StopBuild:
        K.barrier()
        for ti in range(NT):
            K.op("sp", lambda e, ti=ti: e.dma_start(out=out_d[0, ti * 128:(ti + 1) * 128, :], in_=x_t[ti].ap), [x_t[ti]], (), sem=sem_x[ti])
    K.barrier()
    nc = K.finish()
    return nc, K


_CACHE = {}


def make_in_maps(inputs, n_layers=DEPTH, n_seq=SEQ_PER_CORE, n_cores=N_CORES):
    cbf, cf = const_arrays()
    lbl = np.ascontiguousarray(np.asarray(inputs["lb_logits"], dtype=np.float32).reshape(4, 8, 128).transpose(2, 0, 1)).reshape(128, 32)
    spl = np.stack([small_params(l, inputs) for l in range(DEPTH)], axis=0)
    wl = [layer_weights(l, inputs) for l in range(DEPTH)]
    x = np.asarray(inputs["x"], dtype=np.float32)
    mem = np.asarray(inputs["mem"], dtype=np.float32)
    pos = np.asarray(inputs["positions"]).astype(np.int32)
    maps = []
    for c in range(n_cores):
        sl = slice(c * n_seq, (c + 1) * n_seq)
        m = {
            "x": np.ascontiguousarray(x[sl]),
            "mem": np.ascontiguousarray(mem[sl]),
            "pos": np.stack([pos_layout(pos[b]) for b in range(c * n_seq, (c + 1) * n_seq)], axis=0),
            "cbf": cbf, "cf": cf, "lbl": lbl, "spl": spl,
        }
        for l in range(DEPTH):
            m["wl%d" % l] = wl[l]
        maps.append(m)
    return maps


def kernel(**inputs):
    if "nc" not in _CACHE:
        _CACHE["nc"] = build()[0]
    nc = _CACHE["nc"]
    maps = make_in_maps(inputs)
    res = run_bass_kernel_spmd(nc, maps, core_ids=list(range(N_CORES)))
    out = np.concatenate([np.asarray(r["out"]) for r in res.results], axis=0)
    return out.astype(np.float32)
```
